# Optimizing a Trainium2 kernel written in Bass

```python
import jax, jax.numpy as jnp
from jax import lax
import numpy as np

D_MODEL = 2048
BATCH = 8
SEQ = 2048
DEPTH = 1

GRID_W = 64
CTX_LEN = 256
HEAD_DIM = 128
N_Q_HEADS = 16
N_KV_HEADS = 4
Q_GROUP = N_Q_HEADS // N_KV_HEADS
ATTN_WIDTH = N_Q_HEADS * HEAD_DIM
KV_WIDTH = N_KV_HEADS * HEAD_DIM
D_RNN = D_MODEL
N_RNN_BLOCKS = 16
RNN_BLOCK = D_RNN // N_RNN_BLOCKS
CONV_WIDTH = 4
CONV_PAD_LO = 1
CONV_PAD_HI = 2
LRU_C = 8.0
D_FF = 4 * D_MODEL
N_BRANCH = 2
Q_BLOCK = 128
ROPE_THETA = 10000.0
NORM_EPS = 1e-6
N_MOD = 6
N_IN = ATTN_WIDTH + 2 * KV_WIDTH + 2 * D_RNN + N_BRANCH * D_MODEL
IN_SPLITS = (ATTN_WIDTH,
             ATTN_WIDTH + KV_WIDTH,
             ATTN_WIDTH + 2 * KV_WIDTH,
             ATTN_WIDTH + 2 * KV_WIDTH + D_RNN,
             ATTN_WIDTH + 2 * KV_WIDTH + 2 * D_RNN)

kernel_name = "hybrid_gqa_rglru_parallel_dit_block"


def rms_norm(x, g):
    xf = x.astype(jnp.float32)
    y = xf * lax.rsqrt(jnp.mean(xf * xf, axis=-1, keepdims=True) + NORM_EPS)
    return (y * g.astype(jnp.float32)).astype(x.dtype)


def modulate(h, shift, scale):
    return h * (1 + scale) + shift


def rope_tables(row_idx, col_idx):
    n_freq = HEAD_DIM // 4
    inv_freq = ROPE_THETA ** (-jnp.arange(n_freq, dtype=jnp.float32) / n_freq)
    ang = jnp.concatenate([row_idx.astype(jnp.float32)[:, None] * inv_freq,
                           col_idx.astype(jnp.float32)[:, None] * inv_freq], axis=-1)
    return jnp.cos(ang), jnp.sin(ang)


def apply_rope(x, cos, sin):
    xf = x.astype(jnp.float32).reshape(*x.shape[:-1], HEAD_DIM // 2, 2)
    x1, x2 = xf[..., 0], xf[..., 1]
    cs, sn = cos[None, :, None, :], sin[None, :, None, :]
    out = jnp.stack([x1 * cs - x2 * sn, x1 * sn + x2 * cs], axis=-1).reshape(x.shape)
    return out.astype(x.dtype)


def gqa_softmax(q_blk, k, v):
    s = jnp.einsum('bqhgd,bkhd->bhgqk', q_blk, k).astype(jnp.float32) * (HEAD_DIM ** -0.5)
    p = jax.nn.softmax(s, axis=-1)
    return jnp.einsum('bhgqk,bkhd->bqhgd', p.astype(v.dtype), v)


def latent_attention(q, k_lat, v_lat, k_ctx, v_ctx):
    b, s = q.shape[:2]
    k_all = jnp.concatenate([k_ctx, k_lat], axis=1)
    v_all = jnp.concatenate([v_ctx, v_lat], axis=1)
    nb = s // Q_BLOCK
    qb = q.reshape(b, nb, Q_BLOCK, N_KV_HEADS, Q_GROUP, HEAD_DIM).transpose(1, 0, 2, 3, 4, 5)
    o = lax.map(lambda q_blk: gqa_softmax(q_blk, k_all, v_all), qb)
    return o.transpose(1, 0, 2, 3, 4, 5).reshape(b, s, ATTN_WIDTH)


def context_attention(q_c, k_c, v_c):
    b, l = q_c.shape[:2]
    o = gqa_softmax(q_c.reshape(b, l, N_KV_HEADS, Q_GROUP, HEAD_DIM), k_c, v_c)
    return o.reshape(b, l, ATTN_WIDTH)


def depthwise_conv(x, w, bias):
    s = x.shape[1]
    xp = jnp.pad(x, ((0, 0), (CONV_PAD_LO, CONV_PAD_HI), (0, 0)))
    y = bias
    for k in range(CONV_WIDTH):
        y = y + xp[:, k:k + s] * w[k]
    return y


def rglru_coeffs(x, w_r, b_r, w_i, b_i, lam):
    b, s, _ = x.shape
    xb = x.reshape(b, s, N_RNN_BLOCKS, RNN_BLOCK)
    r = jax.nn.sigmoid(jnp.einsum('bshi,hij->bshj', xb, w_r).reshape(b, s, D_RNN) + b_r)
    i = jax.nn.sigmoid(jnp.einsum('bshi,hij->bshj', xb, w_i).reshape(b, s, D_RNN) + b_i)
    log_a = -LRU_C * r.astype(jnp.float32) * jax.nn.softplus(-lam.astype(jnp.float32))
    a = jnp.exp(log_a)
    mult = jnp.sqrt(-jnp.expm1(2.0 * log_a))
    return a, mult * (i * x).astype(jnp.float32)


def linear_scan(a, bx, reverse):
    def combine(e1, e2):
        a1, b1 = e1
        a2, b2 = e2
        return a1 * a2, a2 * b1 + b2
    _, h = lax.associative_scan(combine, (a, bx), axis=1, reverse=reverse)
    return h


def rglru_direction(x_ctx, x_lat, w_r, b_r, w_i, b_i, lam, reverse):
    a_c, bx_c = rglru_coeffs(x_ctx, w_r, b_r, w_i, b_i, lam)
    h_c = linear_scan(a_c, bx_c, reverse)
    idx = 0 if reverse else -1
    h0 = h_c[:, idx]
    a_l, bx_l = rglru_coeffs(x_lat, w_r, b_r, w_i, b_i, lam)
    first = -1 if reverse else 0
    bx_l = bx_l.at[:, first].add(a_l[:, first] * h0)
    h_l = linear_scan(a_l, bx_l, reverse)
    return h_c, h_l


def merge_branches(attn_o, rnn_h, xg, gl, w_o_attn, w_o_rnn, w_out):
    y_attn = attn_o @ w_o_attn
    y_rnn = (rnn_h * jax.nn.gelu(xg)) @ w_o_rnn
    g_attn, g_rnn = jnp.split(jax.nn.sigmoid(gl), N_BRANCH, axis=-1)
    return (g_attn * y_attn + g_rnn * y_rnn) @ w_out


def mixer(h_lat, h_ctx, cos, sin, w_in, q_gain, k_gain, conv_w, conv_b, w_rg, b_rg, w_ig, b_ig,
          lam, w_o_attn, w_o_rnn, w_out, update_ctx):
    b, s, _ = h_lat.shape
    l = h_ctx.shape[1]
    q, k, v, xr, xg, gl = jnp.split(h_lat @ w_in, IN_SPLITS, axis=-1)
    qc, kc, vc, xrc, xgc, glc = jnp.split(h_ctx @ w_in, IN_SPLITS, axis=-1)

    q = apply_rope(rms_norm(q.reshape(b, s, N_Q_HEADS, HEAD_DIM), q_gain), cos, sin)
    k = apply_rope(rms_norm(k.reshape(b, s, N_KV_HEADS, HEAD_DIM), k_gain), cos, sin)
    v = v.reshape(b, s, N_KV_HEADS, HEAD_DIM)
    kc = rms_norm(kc.reshape(b, l, N_KV_HEADS, HEAD_DIM), k_gain)
    vc = vc.reshape(b, l, N_KV_HEADS, HEAD_DIM)
    attn_lat = latent_attention(q, k, v, kc, vc)

    xr_lat = depthwise_conv(xr, conv_w, conv_b)
    xr_ctx = depthwise_conv(xrc, conv_w, conv_b)
    hc_f, hl_f = rglru_direction(xr_ctx, xr_lat, w_rg[0], b_rg[0], w_ig[0], b_ig[0], lam[0], False)
    hc_b, hl_b = rglru_direction(xr_ctx, xr_lat, w_rg[1], b_rg[1], w_ig[1], b_ig[1], lam[1], True)
    rnn_lat = (hl_f + hl_b).astype(h_lat.dtype)

    out_lat = merge_branches(attn_lat, rnn_lat, xg, gl, w_o_attn, w_o_rnn, w_out)
    out_ctx = None
    if update_ctx:
        qc = rms_norm(qc.reshape(b, l, N_Q_HEADS, HEAD_DIM), q_gain)
        attn_ctx = context_attention(qc, kc, vc)
        rnn_ctx = (hc_f + hc_b).astype(h_ctx.dtype)
        out_ctx = merge_branches(attn_ctx, rnn_ctx, xgc, glc, w_o_attn, w_o_rnn, w_out)
    return out_lat, out_ctx


def sq_relu_mlp(h, w_up, w_down):
    return jnp.square(jax.nn.relu(h @ w_up)) @ w_down


def setup_inputs(seed: int = 0) -> dict:
    key = jax.random.key(seed)
    ks = jax.random.split(key, 24)
    f32 = jnp.float32
    nrm = lambda k, shape, scale: jax.random.normal(k, shape, f32) * scale
    a0 = jax.random.uniform(ks[17], (DEPTH, 2, D_RNN), f32, 0.9, 0.999)
    sig = a0 ** (1.0 / LRU_C)
    lru_lambda = jnp.log(sig) - jnp.log1p(-sig)
    return {
        "x": nrm(ks[0], (BATCH, SEQ, D_MODEL), 1.0),
        "c": nrm(ks[1], (BATCH, D_MODEL), 1.0),
        "ctx": nrm(ks[2], (BATCH, CTX_LEN, D_MODEL), 1.0),
        "c_ctx": nrm(ks[3], (D_MODEL,), 1.0),
        "w_mod": nrm(ks[4], (DEPTH, D_MODEL, N_MOD * D_MODEL), 0.5 * D_MODEL ** -0.5),
        "b_mod": nrm(ks[5], (DEPTH, N_MOD * D_MODEL), 0.01),
        "g_mix": 1.0 + nrm(ks[6], (DEPTH, D_MODEL), 0.1),
        "g_mlp": 1.0 + nrm(ks[7], (DEPTH, D_MODEL), 0.1),
        "w_in": nrm(ks[8], (DEPTH, D_MODEL, N_IN), D_MODEL ** -0.5),
        "q_gain": 1.0 + nrm(ks[9], (DEPTH, HEAD_DIM), 0.1),
        "k_gain": 1.0 + nrm(ks[10], (DEPTH, HEAD_DIM), 0.1),
        "conv_w": nrm(ks[11], (DEPTH, CONV_WIDTH, D_RNN), CONV_WIDTH ** -0.5),
        "conv_b": nrm(ks[12], (DEPTH, D_RNN), 0.01),
        "w_rg": nrm(ks[13], (DEPTH, 2, N_RNN_BLOCKS, RNN_BLOCK, RNN_BLOCK), RNN_BLOCK ** -0.5),
        "b_rg": nrm(ks[14], (DEPTH, 2, D_RNN), 0.01),
        "w_ig": nrm(ks[15], (DEPTH, 2, N_RNN_BLOCKS, RNN_BLOCK, RNN_BLOCK), RNN_BLOCK ** -0.5),
        "b_ig": nrm(ks[16], (DEPTH, 2, D_RNN), 0.01),
        "lru_lambda": lru_lambda,
        "w_o_attn": nrm(ks[18], (DEPTH, ATTN_WIDTH, D_MODEL), ATTN_WIDTH ** -0.5),
        "w_o_rnn": nrm(ks[19], (DEPTH, D_RNN, D_MODEL), D_RNN ** -0.5),
        "w_out": nrm(ks[20], (DEPTH, D_MODEL, D_MODEL), D_MODEL ** -0.5),
        "w_up": nrm(ks[21], (DEPTH, D_MODEL, D_FF), D_MODEL ** -0.5),
        "w_down": nrm(ks[22], (DEPTH, D_FF, D_MODEL), D_FF ** -0.5),
        "g_final": 1.0 + nrm(ks[23], (D_MODEL,), 0.1),
    }


def reference(x, c, ctx, c_ctx, w_mod, b_mod, g_mix, g_mlp, w_in, q_gain, k_gain, conv_w, conv_b,
              w_rg, b_rg, w_ig, b_ig, lru_lambda, w_o_attn, w_o_rnn, w_out, w_up, w_down, g_final):
    n_lat = x.shape[1]
    rows = n_lat // GRID_W
    row_idx = jnp.repeat(jnp.arange(rows), GRID_W)
    col_idx = jnp.tile(jnp.arange(GRID_W), rows)
    cos, sin = rope_tables(row_idx, col_idx)

    for layer in range(DEPTH):
        update_ctx = layer < DEPTH - 1
        mod_lat = (jax.nn.silu(c) @ w_mod[layer] + b_mod[layer])[:, None, :]
        mod_ctx = jax.nn.silu(c_ctx) @ w_mod[layer] + b_mod[layer]
        sh_a, sc_a, ga_a, sh_f, sc_f, ga_f = jnp.split(mod_lat, N_MOD, axis=-1)
        csh_a, csc_a, cga_a, csh_f, csc_f, cga_f = jnp.split(mod_ctx, N_MOD, axis=-1)

        h_lat = modulate(rms_norm(x, g_mix[layer]), sh_a, sc_a)
        h_ctx = modulate(rms_norm(ctx, g_mix[layer]), csh_a, csc_a)
        mix_lat, mix_ctx = mixer(h_lat, h_ctx, cos, sin, w_in[layer], q_gain[layer], k_gain[layer],
                                 conv_w[layer], conv_b[layer], w_rg[layer], b_rg[layer],
                                 w_ig[layer], b_ig[layer], lru_lambda[layer], w_o_attn[layer],
                                 w_o_rnn[layer], w_out[layer], update_ctx)
        x = x + ga_a * mix_lat
        x = x + ga_f * sq_relu_mlp(modulate(rms_norm(x, g_mlp[layer]), sh_f, sc_f),
                                   w_up[layer], w_down[layer])
        if update_ctx:
            ctx = ctx + cga_a * mix_ctx
            ctx = ctx + cga_f * sq_relu_mlp(modulate(rms_norm(ctx, g_mlp[layer]), csh_f, csc_f),
                                            w_up[layer], w_down[layer])
    return rms_norm(x, g_final)
```

```python
import math
import numpy as np
import concourse.bass as bass
import concourse.mybir as mybir
from concourse.bass_utils import run_bass_kernel_spmd

F32 = mybir.dt.float32
BF16 = mybir.dt.bfloat16
I32 = mybir.dt.int32
AF = mybir.ActivationFunctionType
ALU = mybir.AluOpType
AX = mybir.AxisListType

D = 2048
S = 2048
L = 256
NT = S // 128
NCT = L // 128
KC = D // 128
NKEY = (S + L) // 128
N_IN = 11264
DFF = 8192
EPS = 1e-6
TB = 512
NTB = S // TB

PV_C, PV_CC, PV_GMIX, PV_GMLP, PV_CW, PV_CB, PV_BRG, PV_BIG, PV_LAM, PV_BMOD = (
    0, 16, 32, 48, 64, 128, 144, 176, 208, 240)
NPV = 240 + 96


class Tracker:
    def __init__(self, nc):
        self.nc = nc
        self.engs = {"pe": nc.tensor, "act": nc.scalar, "dve": nc.vector,
                     "pool": nc.gpsimd, "sp": nc.sync}
        self.esem = {k: nc.alloc_semaphore("es_" + k) for k in self.engs}
        self.ecnt = {k: 0 for k in self.engs}
        self.waited = {k: {} for k in self.engs}
        self.last_w = {}
        self.readers = {}
        self.dsem = {}
        self.dcnt = {}

    def _deps(self, eng, reads, writes, ww_ok=False):
        deps = []
        for r in reads:
            w = self.last_w.get(r)
            if w is not None:
                deps.append(w)
        for wkey in writes:
            w = self.last_w.get(wkey)
            if w is not None and not (w[2] == eng and (eng == "pe" or ww_ok)):
                deps.append(w)
            for rd in self.readers.get(wkey, ()):
                deps.append(rd)
        return deps

    def _wait(self, eng, deps):
        e = self.engs[eng]
        wd = self.waited[eng]
        need = {}
        for (sem, val, _) in deps:
            k = id(sem)
            if val > wd.get(k, 0) and val > need.get(k, (None, 0))[1]:
                need[k] = (sem, val)
        for k, (sem, val) in need.items():
            e.wait_ge(sem, val)
            wd[k] = val

    def _commit(self, token, reads, writes):
        for r in reads:
            self.readers.setdefault(r, []).append(token)
        for w in writes:
            self.last_w[w] = token
            self.readers[w] = []

    def op(self, eng, reads, writes, fn, ww_ok=False):
        self._wait(eng, self._deps(eng, reads, writes, ww_ok))
        inst = fn(self.engs[eng])
        self.ecnt[eng] += 1
        inst.then_inc(self.esem[eng], 1)
        self._commit((self.esem[eng], self.ecnt[eng], eng), reads, writes)

    def dma(self, q, slot, reads, writes, fn):
        if slot not in self.dsem:
            self.dsem[slot] = self.nc.alloc_semaphore("ds_%d" % len(self.dsem))
            self.dcnt[slot] = 0
        sem = self.dsem[slot]
        deps = self._deps("dma:" + slot, reads, writes)
        if self.dcnt[slot] > 0:
            deps.append((sem, self.dcnt[slot], None))
        self._wait(q, deps)
        inst = fn(self.engs[q])
        self.dcnt[slot] += 16
        inst.then_inc(sem, 16)
        self._commit((sem, self.dcnt[slot], "dma:" + slot), reads, writes)

    def barrier(self):
        toks = [(self.esem[k], self.ecnt[k], k) for k in self.engs if self.ecnt[k] > 0]
        toks += [(self.dsem[s], self.dcnt[s], None) for s in self.dsem]
        for eng in self.engs:
            self._wait(eng, toks)
        self.last_w = {}
        self.readers = {}

    def final_wait(self, eng):
        deps = [(self.esem[k], self.ecnt[k], k) for k in self.engs if self.ecnt[k] > 0 and k != eng]
        deps += [(self.dsem[s], self.dcnt[s], None) for s in self.dsem]
        self._wait(eng, deps)


def sb_view(ap, dims):
    return bass.AP(ap.tensor, ap.offset, [list(ap.ap[0])] + [list(d) for d in dims])


class Builder:
    def __init__(self, stage=99, debug=False):
        self.stage = stage
        self.debug = debug
        self.nc = bass.Bass("TRN2", target_bir_lowering=False)
        self.T = Tracker(self.nc)
        self.build()

    def dram_in(self, name, shape, dt=F32):
        return self.nc.dram_tensor(name, list(shape), dt, kind="ExternalInput").ap()

    def scratch(self, name, shape, dt):
        kind = "ExternalOutput" if self.debug else "Internal"
        return self.nc.dram_tensor(name, list(shape), dt, kind=kind).ap()

    def sb(self, name, shape, dt):
        return self.nc.alloc_sbuf_tensor("s_" + name, list(shape), dt)

    def sb_at(self, name, shape, dt, off):
        return self.nc.alloc_sbuf_tensor_at("s_" + name, list(shape), dt, offset=off)

    def plan(self, tag, src):
        self.wplan.append((tag, src))

    def _issue_panel(self, i):
        tag, src = self.wplan[i]
        slot = i % self.NSLOT
        kc, n = src.shape[1], src.shape[2]
        dst = self.wslots[slot][:, 0:kc * n].rearrange("p (k n) -> p k n", k=kc)
        self.T.dma("pool", "w%d" % slot, [], [("w", slot)],
                   lambda e: e.dma_start(out=dst, in_=src))

    def wnext(self, tag):
        i = self.wpos
        assert self.wplan[i][0] == tag, (self.wplan[i][0], tag)
        while self.wissued < min(len(self.wplan), i + self.NSLOT):
            self._issue_panel(self.wissued)
            self.wissued += 1
        self.wpos += 1
        slot = i % self.NSLOT
        src = self.wplan[i][1]
        kc, n = src.shape[1], src.shape[2]
        return (self.wslots[slot][:, 0:kc * n].rearrange("p (k n) -> p k n", k=kc), ("w", slot))

    def wsrc(self, w, r0, nrows, c0, ncols):
        return w[r0:r0 + nrows, c0:c0 + ncols].rearrange("(k p) n -> p k n", p=128)

    def psnext(self, pool=None):
        pool = pool or list(range(8))
        i = self.pscur.get(tuple(pool), 0)
        self.pscur[tuple(pool)] = i + 1
        b = pool[i % len(pool)]
        return self.ps[b], ("ps", b)


    def ar(self, name, shape, dt):
        nbytes = int(np.prod(shape[1:])) * (4 if dt in (F32, I32) else 2)
        nbytes = (nbytes + 63) // 64 * 64
        off = self.ar_ptr
        self.ar_ptr += nbytes
        assert self.ar_ptr <= self.ar_end, (name, self.ar_ptr, self.ar_end)
        self.ar_id += 1
        return self.nc.alloc_sbuf_tensor_at("a%d_%s" % (self.ar_id, name), list(shape), dt, offset=off)

    def ar_mark(self):
        return self.ar_ptr

    def ar_release(self, mark):
        self.T.barrier()
        self.ar_ptr = mark

    def build(self):
        nc, T = self.nc, self.T
        self.x_d = self.dram_in("x", [S, D])
        self.ctx_d = self.dram_in("ctx", [L, D])
        self.pvec_d = self.dram_in("pvec", [128, NPV])
        self.w_mod = self.dram_in("w_mod", [D, 6 * D])
        self.w_in = self.dram_in("w_in", [D, N_IN])
        self.qg_d = self.dram_in("q_gain", [1, 128])
        self.kg_d = self.dram_in("k_gain", [1, 128])
        self.w_rg = self.dram_in("w_rg", [2 * 16 * 128, 128])
        self.w_ig = self.dram_in("w_ig", [2 * 16 * 128, 128])
        self.w_oa = self.dram_in("w_o_attn", [D, D])
        self.w_or = self.dram_in("w_o_rnn", [D, D])
        self.w_out = self.dram_in("w_out", [D, D])
        self.w_up = self.dram_in("w_up", [D, DFF])
        self.w_down = self.dram_in("w_down", [DFF, D])
        self.gf_d = self.dram_in("g_final", [1, D])
        self.out_d = nc.dram_tensor("out", [S, D], F32, kind="ExternalOutput").ap()
        self.hT_d = self.scratch("hT_d", [D, S], BF16)
        self.uT_d = self.scratch("uT_d", [D, S], BF16)
        self.mT_d = self.scratch("mT_d", [D, S], BF16)
        self.ga_d = self.scratch("ga_d", [2, D], F32)
        if self.debug:
            self.dbg_d = nc.dram_tensor("dbg", [128, 4096], F32, kind="ExternalOutput").ap()
            self.dbg2_d = nc.dram_tensor("dbg2", [S, D], F32, kind="ExternalOutput").ap()

        self.NSLOT = 4
        self.wslots = [self.sb("wslot%d" % i, [128, 4096], BF16) for i in range(self.NSLOT)]
        self.wplan, self.wpos, self.wissued = [], 0, 0
        self.pvec = self.sb("pvec", [128, NPV], F32)
        self.identf = self.sb("identf", [128, 128], F32)
        self.identb = self.sb("identb", [128, 128], BF16)
        self.cosT = self.sb("cosT", [128, NT, 64], F32)
        self.sinT = self.sb("sinT", [128, NT, 64], F32)
        self.gq_b = self.sb("gq_b", [128, 128], F32)
        self.gk_b = self.sb("gk_b", [128, 128], F32)
        self.modL = self.sb("modL", [128, 96], F32)
        self.modC = self.sb("modC", [128, 96], F32)
        self.AB = self.sb("AB", [128, 6, 16], F32)
        self.cl = self.sb("cl", [128, 2, 2, 16], F32)
        self.epsb = self.sb("epsb", [128, 1], F32)
        self.silu = self.sb("silu", [128, 16, 2], BF16)
        self.nm_st = self.sb("nm_st", [128, 16], F32)
        self.nr_st = self.sb("nr_st", [128, 16], F32)
        self.ps = [nc.alloc_psum_tensor("ps%d" % i, [128, 512], F32) for i in range(8)]
        self.pscur = {}
        self.ar_ptr = (nc.sbuf_base + 63) // 64 * 64
        self.ar_end = nc.sbuf_top
        self.ar_id = 0
        self.nm_i = 0
        self.nr_i = 0

        self.plan_all()
        m0 = self.ar_mark()
        self.setup()
        self.ar_release(m0)
        self.KT = self.ar("KT", [128, 4, S + L], BF16)
        self.Vaug = self.ar("Vaug", [128, NKEY, 4, 130], BF16)
        m1 = self.ar_mark()
        self.hT = self.ar("hT", [128, KC, S], BF16)
        self.hcT = self.ar("hcT", [128, KC, L], BF16)
        m2 = self.ar_mark()
        self.nm_alloc(2)
        self.p0_mod(0, 2)
        self.p1_norm()
        self.kv_base, self.kv_end = m0, m1
        if self.stage >= 2:
            self.ar_release(m2)
            self.p3_rnn()
        if self.stage >= 3:
            self.ar_release(m2)
            self.p2_kv()
        if self.stage >= 4:
            self.ar_release(m1)
            self.p4_attn()
        if self.stage >= 5:
            self.ar_release(m0)
            self.p6_mlp()
        self.finish()

    def plan_all(self):
        W = self.wsrc
        for m in range(2):
            for pp in range(8):
                self.plan(("mod", m, pp), W(self.w_mod, 0, D, m * D + pp * 256, 256))
        if self.stage >= 2:
            for pp in range(8):
                for cl_ in range(2):
                    c = 2 * pp + cl_
                    self.plan(("xr", c), W(self.w_in, 0, D, 3072 + c * 128, 128))
                    self.plan(("xg", c), W(self.w_in, 0, D, 5120 + c * 128, 128))
                for q in range(4):
                    m = 2 + (pp * 4 + q) // 8
                    mp = (pp * 4 + q) % 8
                    self.plan(("mod", m, mp), W(self.w_mod, 0, D, m * D + mp * 256, 256))
        if self.stage >= 3:
            for i in range(4):
                self.plan(("kv", i), W(self.w_in, 0, D, 2048 + i * 256, 256))
        if self.stage >= 4:
            for tb in range(NTB):
                for hp in range(8):
                    self.plan(("q", tb, hp), W(self.w_in, 0, D, hp * 256, 256))
                for ccp in range(8):
                    self.plan(("gla", tb, ccp), W(self.w_in, 0, D, 7168 + ccp * 256, 256))
                    self.plan(("oa", tb, ccp), W(self.w_oa, 0, D, ccp * 256, 256))
                    self.plan(("glr", tb, ccp), W(self.w_in, 0, D, 9216 + ccp * 256, 256))
                    self.plan(("or", tb, ccp), W(self.w_or, 0, D, ccp * 256, 256))
        if self.stage >= 5:
            for tb in range(NTB):
                for np_ in range(8):
                    self.plan(("wout", tb, np_), W(self.w_out, 0, D, np_ * 256, 256))
                for fp_ in range(32):
                    self.plan(("wup", tb, fp_), W(self.w_up, 0, D, fp_ * 256, 256))
                for cb in range(4):
                    for fp_ in range(8):
                        self.plan(("wdn", tb, cb, fp_), W(self.w_down, fp_ * 1024, 1024, cb * 512, 512))

    def setup(self):
        nc, T = self.nc, self.T
        T.dma("sp", "pvec", [], ["pvec"], lambda e: e.dma_start(out=self.pvec[:, :], in_=self.pvec_d))
        T.dma("sp", "gq", [], ["gq_b"], lambda e: e.dma_start(
            out=self.gq_b[:, :], in_=bass.AP(self.qg_d.tensor, 0, [[0, 128], [1, 128]])))
        T.dma("sp", "gk", [], ["gk_b"], lambda e: e.dma_start(
            out=self.gk_b[:, :], in_=bass.AP(self.kg_d.tensor, 0, [[0, 128], [1, 128]])))
        T.op("dve", [], ["modL"], lambda e: e.memset(self.modL[:, :], 0.0))
        T.op("dve", [], ["modC"], lambda e: e.memset(self.modC[:, :], 0.0))
        T.op("dve", [], ["epsb"], lambda e: e.memset(self.epsb[:, :], EPS))
        it = self.ar("iota_i", [128, 128], I32)
        T.op("pool", [], ["iota_i"], lambda e: e.iota(it[:, :], [[1, 128]], base=0, channel_multiplier=-1))
        T.op("dve", ["iota_i"], ["identf"], lambda e: e.tensor_scalar(
            self.identf[:, :], it[:, :], 0, None, ALU.is_equal))
        T.op("dve", ["identf"], ["identb"], lambda e: e.tensor_copy(self.identb[:, :], self.identf[:, :]))
        rowi = self.ar("rowi", [128, NT], I32)
        coli = self.ar("coli", [128, NT], I32)
        fi = self.ar("fi", [128, 32], I32)
        T.op("pool", [], ["rowi"], lambda e: e.iota(rowi[0:64, :], [[2, NT]], base=0, channel_multiplier=0))
        T.op("pool", [], ["rowi"], lambda e: e.iota(rowi[64:128, :], [[2, NT]], base=1, channel_multiplier=0))
        T.op("pool", [], ["coli"], lambda e: e.iota(coli[0:64, :], [[0, NT]], base=0, channel_multiplier=1))
        T.op("pool", [], ["coli"], lambda e: e.iota(coli[64:128, :], [[0, NT]], base=0, channel_multiplier=1))
        T.op("pool", [], ["fi"], lambda e: e.iota(fi[:, :], [[1, 32]], base=0, channel_multiplier=0))
        rowf = self.ar("rowf", [128, NT], F32)
        colf = self.ar("colf", [128, NT], F32)
        ff = self.ar("ff", [128, 32], F32)
        T.op("dve", ["rowi"], ["rowf"], lambda e: e.tensor_copy(rowf[:, :], rowi[:, :]))
        T.op("dve", ["coli"], ["colf"], lambda e: e.tensor_copy(colf[:, :], coli[:, :]))
        T.op("dve", ["fi"], ["ff"], lambda e: e.tensor_copy(ff[:, :], fi[:, :]))
        T.op("act", ["ff"], ["ff"], lambda e: e.activation(ff[:, :], ff[:, :], AF.Exp,
                                                            scale=-math.log(10000.0) / 32.0))
        ang = self.ar("ang", [128, NT, 64], F32)
        kf = self.ar("kf", [128, NT, 64], F32)
        ki = self.ar("ki", [128, NT, 64], I32)
        msk = self.ar("msk", [128, NT, 64], F32)
        ffb = sb_view(ff[:, :], [[0, NT], [1, 32]])
        T.op("dve", ["rowf", "ff"], ["ang"], lambda e: e.tensor_tensor(
            ang[:, :, 0:32], sb_view(rowf[:, :], [[1, NT], [0, 32]]), ffb, ALU.mult))
        T.op("dve", ["colf", "ff"], ["ang"], lambda e: e.tensor_tensor(
            ang[:, :, 32:64], sb_view(colf[:, :], [[1, NT], [0, 32]]), ffb, ALU.mult))
        TWO_PI = 2.0 * math.pi
        for (dst, dn, shift) in ((self.sinT, "sinT", 0.0), (self.cosT, "cosT", math.pi / 2.0)):
            T.op("dve", ["ang"], ["kf"], lambda e: e.tensor_scalar(
                kf[:, :, :], ang[:, :, :], shift, 1.0 / TWO_PI, ALU.add, ALU.mult))
            T.op("dve", ["kf"], ["ki"], lambda e: e.tensor_copy(ki[:, :, :], kf[:, :, :]))
            T.op("dve", ["ki"], ["kf"], lambda e: e.tensor_copy(kf[:, :, :], ki[:, :, :]))
            T.op("dve", ["kf"], ["kf"], lambda e: e.tensor_scalar(
                kf[:, :, :], kf[:, :, :], -TWO_PI, shift, ALU.mult, ALU.add))
            T.op("dve", ["kf", "ang"], ["kf"], lambda e: e.tensor_tensor(
                kf[:, :, :], kf[:, :, :], ang[:, :, :], ALU.add))
            T.op("dve", ["kf"], ["msk"], lambda e: e.tensor_scalar(
                msk[:, :, :], kf[:, :, :], math.pi, -TWO_PI, ALU.is_gt, ALU.mult))
            T.op("dve", ["kf", "msk"], ["kf"], lambda e: e.tensor_tensor(
                kf[:, :, :], kf[:, :, :], msk[:, :, :], ALU.add))
            T.op("dve", ["kf"], ["msk"], lambda e: e.tensor_scalar(
                msk[:, :, :], kf[:, :, :], -math.pi, TWO_PI, ALU.is_lt, ALU.mult))
            T.op("dve", ["kf", "msk"], ["kf"], lambda e: e.tensor_tensor(
                kf[:, :, :], kf[:, :, :], msk[:, :, :], ALU.add))
            T.op("dve", ["kf"], ["kf"], lambda e: e.tensor_scalar(
                kf[:, :, :], kf[:, :, :], math.pi, -math.pi, ALU.min, ALU.max))
            T.op("act", ["kf"], [dn], lambda e: e.activation(dst[:, :, :], kf[:, :, :], AF.Sin))
        lam = self.pvec[:, PV_LAM:PV_LAM + 32]
        sp_ = self.ar("sp_", [128, 32], F32)
        T.op("act", ["pvec"], ["sp_"], lambda e: e.activation(sp_[:, :], lam, AF.Exp, scale=-1.0))
        T.op("act", ["sp_"], ["sp_"], lambda e: e.activation(sp_[:, :], sp_[:, :], AF.Ln, bias=1.0))
        for d in range(2):
            T.op("dve", ["sp_"], ["cl"], lambda e: e.tensor_scalar(
                self.cl[:, d, 0, :], sp_[:, d * 16:(d + 1) * 16], -8.0, None, ALU.mult))
            T.op("dve", ["sp_"], ["cl"], lambda e: e.tensor_scalar(
                self.cl[:, d, 1, :], sp_[:, d * 16:(d + 1) * 16], -16.0, None, ALU.mult))
        T.op("act", ["pvec"], ["silu"], lambda e: e.activation(
            self.silu[:, :, 0], self.pvec[:, PV_C:PV_C + 16], AF.Silu))
        T.op("act", ["pvec"], ["silu"], lambda e: e.activation(
            self.silu[:, :, 1], self.pvec[:, PV_CC:PV_CC + 16], AF.Silu))

    def mod_step(self, pst, psk, m, pp, mbase):
        T = self.T
        wp, wk = self.wnext(("mod", m, pp))

        def mm(e):
            last = None
            for cl_ in range(2):
                col = ((m - mbase) * 16 + pp * 2 + cl_) * 2
                for k in range(KC):
                    last = e.matmul(pst[:, col:col + 2], wp[:, k, cl_ * 128:(cl_ + 1) * 128],
                                    self.silu[:, k, :], start=(k == 0), stop=(k == KC - 1))
            return last
        T.op("pe", [wk, "silu"], [psk], mm)

    def mod_finish(self, pst, psk, m0, m1):
        T = self.T
        nm = m1 - m0
        pv = pst[:, 0:nm * 32].rearrange("p (c t) -> p c t", t=2)
        bm = self.pvec[:, PV_BMOD + m0 * 16:PV_BMOD + m1 * 16]
        T.op("dve", [psk, "pvec"], ["modL"], lambda e: e.tensor_tensor(
            self.modL[:, m0 * 16:m1 * 16], pv[:, :, 0], bm, ALU.add))
        T.op("dve", [psk, "pvec"], ["modC"], lambda e: e.tensor_tensor(
            self.modC[:, m0 * 16:m1 * 16], pv[:, :, 1], bm, ALU.add))
        if m0 == 0:
            gm = self.pvec[:, PV_GMIX:PV_GMIX + 16]
            for (src, sn, ia) in ((self.modL, "modL", 0), (self.modC, "modC", 2)):
                T.op("dve", [sn, "pvec"], ["AB"], lambda e: e.scalar_tensor_tensor(
                    self.AB[:, ia, :], src[:, 16:32], 1.0, gm, ALU.add, ALU.mult))
                T.op("dve", [sn], ["AB"], lambda e: e.tensor_copy(self.AB[:, ia + 1, :], src[:, 0:16]))
        else:
            gm = self.pvec[:, PV_GMLP:PV_GMLP + 16]
            T.op("dve", ["modL", "pvec"], ["AB"], lambda e: e.scalar_tensor_tensor(
                self.AB[:, 4, :], self.modL[:, 64:80], 1.0, gm, ALU.add, ALU.mult))
            T.op("dve", ["modL"], ["AB"], lambda e: e.tensor_copy(self.AB[:, 5, :], self.modL[:, 48:64]))
            gaT = self.gaT
            for (gi, c0) in ((0, 32), (1, 80)):
                pt, pk = self.psnext([0, 1, 2, 3, 4, 5, 6])
                T.op("pe", ["modL", "identf"], [pk], lambda e: e.transpose(
                    pt[0:16, 0:128], self.modL[:, c0:c0 + 16], self.identf[:, :]))
                T.op("dve", [pk], ["gaT"], lambda e: e.tensor_copy(gaT[:, gi * 128:(gi + 1) * 128], pt[0:16, 0:128]))
            for gi in range(2):
                T.dma("sp", "gast", ["gaT"], ["ga_d"], lambda e: e.dma_start(
                    out=self.ga_d[gi:gi + 1, :].rearrange("o (j p) -> (o j) p", p=128),
                    in_=gaT[:, gi * 128:(gi + 1) * 128]))

    def p0_mod(self, m0, m1):
        pst, psk = self.psnext()
        for m in range(m0, m1):
            for pp in range(8):
                self.mod_step(pst, psk, m, pp, m0)
        self.mod_finish(pst, psk, m0, m1)

    def nm_alloc(self, nxn=1):
        self.nm_junk = self.ar("nm_junk", [128, D], BF16)
        self.nm_xns = [self.ar("nm_xn%d" % i, [128, D], F32) for i in range(nxn)]
        self.nm_xn = self.nm_xns[0]

    def norm_mod_T(self, xt, xkey, ia, dst_fn, dkeys):
        T = self.T
        i = self.nm_i
        self.nm_i += 1
        xi = i % len(self.nm_xns)
        junk, xn, st = self.nm_junk, self.nm_xns[xi], self.nm_st
        kxn = "nm_xn" if xi == 0 else "nm_xn1"
        ss = st[:, (i % 8) * 2:(i % 8) * 2 + 1]
        rs = st[:, (i % 8) * 2 + 1:(i % 8) * 2 + 2]
        kss, krs = ("nm_ss", i % 8), ("nm_rs", i % 8)
        T.op("act", [xkey], ["nm_junk", kss], lambda e: e.activation(
            junk[:, :], xt, AF.Square, accum_out=ss))
        T.op("act", [kss, "epsb"], [kss], lambda e: e.activation(ss, ss, AF.Sqrt, scale=1.0 / D, bias=self.epsb[:, 0:1]))
        T.op("dve", [kss], [krs], lambda e: e.reciprocal(rs, ss))
        T.op("act", [xkey, krs], [kxn], lambda e: e.activation(xn[:, :], xt, AF.Identity, scale=rs))
        for q4 in range(4):
            pst, psk = self.psnext()

            def tr(e):
                last = None
                for jj in range(4):
                    j = q4 * 4 + jj
                    last = e.transpose(pst[:, jj * 128:(jj + 1) * 128], xn[:, j * 128:(j + 1) * 128],
                                       self.identf[:, :])
                return last
            T.op("pe", [kxn, "identf"], [psk], tr)
            for jj in range(4):
                j = q4 * 4 + jj
                T.op("dve", [psk, "AB"], dkeys, lambda e: e.tensor_scalar(
                    dst_fn(j), pst[:, jj * 128:(jj + 1) * 128],
                    self.AB[:, ia, j:j + 1], self.AB[:, ia + 1, j:j + 1], ALU.mult, ALU.add), ww_ok=(j > 0))
        return rs, krs

    def p1_norm(self):
        T = self.T
        xts = [self.ar("xt%d" % i, [128, D], F32) for i in range(2)]
        for t in range(NCT + NT):
            xt = xts[t % 2]
            xk = ("xt", t % 2)
            src = self.ctx_d[t * 128:(t + 1) * 128, :] if t < NCT else \
                self.x_d[(t - NCT) * 128:(t - NCT + 1) * 128, :]
            T.dma("sp", "xt%d" % (t % 2), [], [xk], lambda e: e.dma_start(out=xt[:, :], in_=src))
            if t < NCT:
                self.norm_mod_T(xt[:, :], xk, 2, lambda j: self.hcT[:, j, t * 128:(t + 1) * 128], [("hcT", t)])
            else:
                tt = t - NCT
                self.norm_mod_T(xt[:, :], xk, 0, lambda j: self.hT[:, j, tt * 128:(tt + 1) * 128], [("hT", tt)])
        for tb in range(NTB):
            dst = self.hT_d[:, tb * TB:(tb + 1) * TB].rearrange("(k p) n -> p k n", p=128)
            T.dma("sp", "hTst", [("hT", tb * 4 + i) for i in range(4)], [("hT_d", tb)],
                  lambda e: e.dma_start(out=dst, in_=self.hT[:, :, tb * TB:(tb + 1) * TB]))

    def nr_alloc(self, nh=2):
        self.nr_qn = [self.ar("nr_qn%d" % i, [128, nh * 128], F32) for i in range(2)]
        self.nr_t = [[self.ar("nr_t%d_%d" % (i, j), [128, nh * 64], F32) for j in range(4)] for i in range(2)]
        self.nr_junks = [self.ar("nr_junk%d" % i, [128, 128], F32) for i in range(4)]
        self.nr_jn = 0

    def normrope(self, ps, pskey, gain, gkey, tile, out, okey, nh=2):
        T = self.T
        i = self.nr_i
        self.nr_i += 1
        st = self.nr_st[:, (i % 4) * 4:(i % 4) * 4 + nh]
        rs = self.nr_st[:, (i % 4) * 4 + 2:(i % 4) * 4 + 2 + nh]
        kst, krs = ("nr_ss", i % 4), ("nr_rs", i % 4)
        for h in range(nh):
            jn = self.nr_jn
            self.nr_jn += 1
            jk = self.nr_junks[jn % 4]
            T.op("dve", [pskey], [("nr_junk", jn % 4), (kst, h)], lambda e: e.scalar_tensor_tensor(
                jk[:, :], ps[:, h * 128:(h + 1) * 128], 1.0, ps[:, h * 128:(h + 1) * 128], ALU.mult, ALU.mult,
                accum_out=st[:, h:h + 1]))
        ksts = [(kst, h) for h in range(nh)]
        T.op("act", ksts + ["epsb"], ksts, lambda e: e.activation(st, st, AF.Ln, scale=1.0 / 128, bias=self.epsb[:, 0:1]))
        T.op("act", ksts, [krs], lambda e: e.activation(rs, st, AF.Exp, scale=-0.5))
        if tile is None:
            for h in range(nh):
                T.op("dve", [pskey, krs, gkey], [okey], lambda e: e.scalar_tensor_tensor(
                    out[:, h * 128:(h + 1) * 128], ps[:, h * 128:(h + 1) * 128], rs[:, h:h + 1], gain[:, :],
                    ALU.mult, ALU.mult))
            return
        qn = self.nr_qn[i % 2]
        kq = ("nr_qn", i % 2)
        t1, t2, t3, t4 = self.nr_t[i % 2]
        kt = [("nr_t", i % 2, j) for j in range(4)]
        for h in range(nh):
            T.op("dve", [pskey, krs, gkey], [kq], lambda e: e.scalar_tensor_tensor(
                qn[:, h * 128:(h + 1) * 128], ps[:, h * 128:(h + 1) * 128], rs[:, h:h + 1], gain[:, :],
                ALU.mult, ALU.mult))
        ev = sb_view(qn[:, 0:], [[128, nh], [2, 64]])
        od = sb_view(qn[:, 1:], [[128, nh], [2, 64]])
        oev = sb_view(out[:, 0:], [[128, nh], [2, 64]])
        ood = sb_view(out[:, 1:], [[128, nh], [2, 64]])
        cs = sb_view(self.cosT[:, tile, :], [[0, nh], [1, 64]])
        sn = sb_view(self.sinT[:, tile, :], [[0, nh], [1, 64]])
        v3 = lambda t: t[:, :].rearrange("p (a b) -> p a b", a=nh)
        T.op("dve", [kq, "cosT"], [kt[0]], lambda e: e.tensor_tensor(v3(t1), ev, cs, ALU.mult))
        T.op("dve", [kq, "sinT"], [kt[1]], lambda e: e.tensor_tensor(v3(t2), od, sn, ALU.mult))
        T.op("dve", [kt[0], kt[1]], [okey], lambda e: e.tensor_tensor(oev, v3(t1), v3(t2), ALU.subtract))
        T.op("pool", [kq, "sinT"], [kt[2]], lambda e: e.tensor_tensor(v3(t3), ev, sn, ALU.mult))
        T.op("pool", [kq, "cosT"], [kt[3]], lambda e: e.tensor_tensor(v3(t4), od, cs, ALU.mult))
        T.op("pool", [kt[2], kt[3]], [okey], lambda e: e.tensor_tensor(ood, v3(t3), v3(t4), ALU.add))

    def p2_kv(self):
        T = self.T
        self.nr_alloc()
        krs_ = [self.ar("kr%d" % i, [128, 256], BF16) for i in range(2)]
        kraws = [self.ar("kraw%d" % i, [128, 256], F32) for i in range(2)]
        T.op("pool", [], ["Vones"], lambda e: e.memset(self.Vaug[:, :, :, 128:130], 1.0))
        n = 0
        for i in range(4):
            wp, wk = self.wnext(("kv", i))
            for t in range(NKEY):
                if t < NCT:
                    stat = lambda k: self.hcT[:, k, t * 128:(t + 1) * 128]
                    skey = ("hcT", t)
                else:
                    stat = lambda k: self.hT[:, k, (t - NCT) * 128:(t - NCT + 1) * 128]
                    skey = ("hT", t - NCT)
                pst, psk = self.psnext()

                def mm(e):
                    last = None
                    for k in range(KC):
                        last = e.matmul(pst[:, 0:256], stat(k), wp[:, k, :], start=(k == 0), stop=(k == KC - 1))
                    return last
                T.op("pe", [wk, skey], [psk], mm)
                if i < 2:
                    kr = krs_[n % 2]
                    kk = ("kr", n % 2)
                    kraw = kraws[n % 2]
                    kwk = ("kraw", n % 2)
                    n += 1
                    T.op("act", [psk], [kwk], lambda e: e.activation(kraw[:, :], pst[:, 0:256], AF.Identity))
                    self.normrope(kraw[:, :], kwk, self.gk_b, "gk_b", None if t < NCT else t - NCT, kr, kk)
                    ptt, ptk = self.psnext()
                    pb = ptt[:, 0:128].bitcast(BF16)

                    def tr(e):
                        e.transpose(pb[:, 0:128], kr[:, 0:128], self.identb[:, :])
                        return e.transpose(pb[:, 128:256], kr[:, 128:256], self.identb[:, :])
                    T.op("pe", [kk, "identb"], [ptk], tr)
                    T.op("act", [ptk], [("KT", i, t)], lambda e: e.activation(
                        self.KT[:, 2 * i:2 * i + 2, t * 128:(t + 1) * 128],
                        pb[:, 0:256].rearrange("p (a b) -> p a b", a=2), AF.Identity))
                else:
                    g0 = 2 * (i - 2)
                    T.op("act", [psk], [("V", i, t)], lambda e: e.activation(
                        self.Vaug[:, t, g0:g0 + 2, 0:128],
                        pst[:, 0:256].rearrange("p (a b) -> p a b", a=2), AF.Identity))

    def p3_rnn(self):
        T = self.T
        NC_ = S + 2 * L
        NW = S + L
        save = self.ar_ptr
        self.ar_ptr = self.kv_base
        xr = self.ar("xr", [128, NW], F32)
        self.gaT = self.ar("gaT", [16, 256], F32)
        xc0 = self.ar("xc0", [128, NC_], F32)
        xcb0 = self.ar("xcb0", [128, NC_], BF16)
        gx = self.ar("gx", [128, S], F32)
        Wg_ = [self.ar("Wg%d" % i, [128, 4, 128], BF16) for i in range(2)]
        assert self.ar_ptr <= self.kv_end, (self.ar_ptr, self.kv_end)
        self.ar_ptr = save
        xc_ = [xc0, self.ar("xc1", [128, NC_], F32)]
        xcb_ = [xcb0, self.ar("xcb1", [128, NC_], BF16)]
        A = self.ar("A", [128, NW], F32)
        I_ = self.ar("I", [128, NW], F32)
        M = [self.ar("M%d" % d, [128, NW], F32) for d in range(2)]
        PS7 = [0, 1, 2, 3, 4, 5, 6]
        modps, modk = self.ps[7], ("ps", 7)
        segs = [(0, L, None)] + [(L + b * 512, 512, b) for b in range(4)]
        cs = slice(0, 128)
        dblocks = [(o, min(512, NW - o)) for o in range(0, NW, 512)]
        Ak = [("A", o) for (o, _) in dblocks]
        Ik = [("I", o) for (o, _) in dblocks]

        def stageA(c):
            xc, xcb = xc_[c % 2], xcb_[c % 2]
            kxc, kxcb = ("xc", c % 2), ("xcb", c % 2)
            wxr, kxr = self.wnext(("xr", c))
            Wg, wgk = Wg_[c % 2], ("Wg", c % 2)
            for (gi, wsrc_) in ((0, self.w_rg), (1, self.w_ig)):
                T.dma("pool", "Wg%d_%d" % (c % 2, gi), [], [(wgk, gi)], lambda e: e.dma_start(
                    out=Wg[:, 2 * gi:2 * gi + 2, :],
                    in_=bass.AP(wsrc_.tensor, c * 128 * 128, [[128, 128], [16 * 128 * 128, 2], [1, 128]])))
            for (off, n, b) in segs:
                pst, psk = self.psnext(PS7)
                if b is None:
                    rhs = lambda k: self.hcT[:, k, :]
                    rk = [("hcT", 0), ("hcT", 1)]
                else:
                    rhs = lambda k: self.hT[:, k, b * 512:(b + 1) * 512]
                    rk = [("hT", b * 4 + i) for i in range(4)]

                def mm(e):
                    last = None
                    for k in range(KC):
                        last = e.matmul(pst[:, 0:n], wxr[:, k, cs], rhs(k), start=(k == 0), stop=(k == KC - 1))
                    return last
                T.op("pe", [kxr] + rk, [psk], mm)
                T.op("act", [psk], [("xr", off)], lambda e: e.activation(xr[:, off:off + n], pst[:, 0:n], AF.Identity))
            xrk = [("xr", o) for (o, _, _) in segs]
            w = lambda kk: self.pvec[:, PV_CW + kk * 16 + c:PV_CW + kk * 16 + c + 1]
            cbias = self.pvec[:, PV_CB + c:PV_CB + c + 1]
            for (off, n) in ((0, L), (L, S)):
                T.op("dve", xrk + ["pvec"], [kxc], lambda e: e.tensor_scalar(
                    xc[:, off:off + n], xr[:, off:off + n], w(1), cbias, ALU.mult, ALU.add))
                T.op("dve", xrk + [kxc, "pvec"], [kxc], lambda e: e.scalar_tensor_tensor(
                    xc[:, off + 1:off + n], xr[:, off:off + n - 1], w(0), xc[:, off + 1:off + n], ALU.mult, ALU.add))
                T.op("dve", xrk + [kxc, "pvec"], [kxc], lambda e: e.scalar_tensor_tensor(
                    xc[:, off:off + n - 1], xr[:, off + 1:off + n], w(2), xc[:, off:off + n - 1], ALU.mult, ALU.add))
                T.op("dve", xrk + [kxc, "pvec"], [kxc], lambda e: e.scalar_tensor_tensor(
                    xc[:, off:off + n - 2], xr[:, off + 2:off + n], w(3), xc[:, off:off + n - 2], ALU.mult, ALU.add))
            T.op("pool", [kxc], [kxc], lambda e: e.tensor_copy(xc[:, S + L:NC_], xc[:, 0:L]))
            T.op("act", [kxc], [kxcb], lambda e: e.activation(xcb[:, :], xc[:, :], AF.Identity))

        def stageG(c):
            wxg, kxg = self.wnext(("xg", c))
            for b in range(4):
                pst, psk = self.psnext(PS7)

                def mm(e):
                    last = None
                    for k in range(KC):
                        last = e.matmul(pst[:, :], wxg[:, k, cs], self.hT[:, k, b * 512:(b + 1) * 512],
                                        start=(k == 0), stop=(k == KC - 1))
                    return last
                T.op("pe", [kxg] + [("hT", b * 4 + i) for i in range(4)], [psk], mm)
                T.op("act", [psk], [("gx", b)], lambda e: e.activation(
                    gx[:, b * 512:(b + 1) * 512], pst[:, :], AF.Gelu_apprx_tanh))

        def stageB(c, d):
            xc, xcb = xc_[c % 2], xcb_[c % 2]
            kxc, kxcb = ("xc", c % 2), ("xcb", c % 2)
            Wg, wgk = Wg_[c % 2], ("Wg", c % 2)
            br = self.pvec[:, PV_BRG + d * 16 + c:PV_BRG + d * 16 + c + 1]
            bi = self.pvec[:, PV_BIG + d * 16 + c:PV_BIG + d * 16 + c + 1]
            Md, Mk = M[d], ("M", d)
            g0 = d * L
            for (o, n) in dblocks:
                for (gi, bias, dst, dk) in ((0, br, A, "A"), (1, bi, I_, "I")):
                    pst, psk = self.psnext(PS7)
                    T.op("pe", [(wgk, gi), kxcb], [psk], lambda e: e.matmul(
                        pst[:, 0:n], Wg[:, 2 * gi + d, :], xcb[:, g0 + o:g0 + o + n], start=True, stop=True))
                    T.op("act", [psk, "pvec"], [(dk, o)], lambda e: e.activation(
                        dst[:, o:o + n], pst[:, 0:n], AF.Sigmoid, bias=bias))
            T.op("act", Ak + ["cl"], [Mk], lambda e: e.activation(
                Md[:, :], A[:, :], AF.Exp, scale=self.cl[:, d, 1, c:c + 1]))
            T.op("act", Ak + ["cl"], Ak, lambda e: e.activation(
                A[:, :], A[:, :], AF.Exp, scale=self.cl[:, d, 0, c:c + 1]))
            T.op("dve", [Mk], [Mk], lambda e: e.tensor_scalar(Md[:, :], Md[:, :], 1.0, -1.0, ALU.min, ALU.mult))
            T.op("act", [Mk], [Mk], lambda e: e.activation(Md[:, :], Md[:, :], AF.Sqrt, scale=1.0, bias=1.0))
            T.op("dve", Ik + [kxc], Ik, lambda e: e.tensor_tensor(
                I_[:, :], I_[:, :], xc[:, g0:g0 + NW], ALU.mult))
            T.op("dve", Ik + [Mk], Ik, lambda e: e.tensor_tensor(I_[:, :], I_[:, :], Md[:, :], ALU.mult))
            if d == 0:
                T.op("dve", Ak + Ik, [Mk], lambda e: e.tensor_tensor_scan(
                    Md[:, :], A[:, :], I_[:, :], 0.0, ALU.mult, ALU.add))
            else:
                rv = lambda t: sb_view(t[:, NW - 1:NW], [[-1, NW]])
                T.op("dve", Ak + Ik, [Mk], lambda e: e.tensor_tensor_scan(
                    rv(Md), rv(A), rv(I_), 0.0, ALU.mult, ALU.add))

        def stageE(c):
            T.op("dve", [("M", 0), ("M", 1)], [("M", 0)], lambda e: e.tensor_tensor(
                M[0][:, L:L + S], M[0][:, L:L + S], M[1][:, 0:S], ALU.add))
            T.op("dve", [("M", 0)] + [("gx", b) for b in range(4)], [("M", 0)], lambda e: e.tensor_tensor(
                M[0][:, L:L + S], M[0][:, L:L + S], gx[:, :], ALU.mult))
            T.dma("pool", "ubst", [("M", 0)], [("uT_d", c)], lambda e: e.dma_start(
                out=self.uT_d[c * 128:(c + 1) * 128, :], in_=M[0][:, L:L + S]))

        stageA(0)
        for c in range(16):
            stageB(c, 0)
            stageG(c)
            if c % 2 == 1:
                pp = c // 2
                for q in range(4):
                    self.mod_step(modps, modk, 2 + (pp * 4 + q) // 8, (pp * 4 + q) % 8, 2)
            if c + 1 < 16:
                stageA(c + 1)
            stageB(c, 1)
            stageE(c)
        self.mod_finish(modps, modk, 2, 6)

    def p4_attn(self):
        T = self.T
        self.nr_alloc()
        hTblk = self.ar("hTblk", [128, KC, TB], BF16)
        uTblk = self.ar("uTblk", [128, KC, TB], BF16)
        attnT = self.ar("attnT", [128, 16, TB], BF16)
        mTblk = self.ar("mTblk", [128, KC, TB], BF16)
        QT = [self.ar("QT%d" % i, [128, 2, TB], BF16) for i in range(2)]
        NPT = 5
        PT = [self.ar("PT%d" % i, [128, TB], BF16) for i in range(NPT)]
        qr = [self.ar("qr%d" % i, [128, 256], BF16) for i in range(8)]
        at = [self.ar("at%d" % i, [128, 128], BF16) for i in range(8)]
        rden = self.ar("rden", [128, 8], F32)
        tA = [self.ar("tA%d" % i, [128, TB], F32) for i in range(2)]
        tB = [self.ar("tB%d" % i, [128, TB], F32) for i in range(2)]
        PS_S = [0, 1, 2]
        LA = 2
        isc = 1.0 / math.sqrt(128.0)
        pvap = lambda qt: self.ps[3 + qt // 2][:, (qt % 2) * 256:(qt % 2) * 256 + 129]
        pvk = lambda qt: ("ps", 3 + qt // 2)
        psq = lambda tt: self.ps[5][:, 0:256]
        psqk = lambda tt: ("ps", 5)
        pstb = [self.ps[6][:, 0:256].bitcast(BF16), self.ps[7][:, 0:256].bitcast(BF16)]
        qraw = [self.ar("qraw%d" % i, [128, 256], F32) for i in range(4)]
        st = {"npt": 0, "nat": 0, "ntr": 0}

        def trbuf():
            i = st["ntr"] % 2
            st["ntr"] += 1
            return pstb[i], ("ps", 6 + i)

        def qmm(tb, hp):
            wq, kwq = self.wnext(("q", tb, hp))
            for tt in range(4):
                def mm(e):
                    last = None
                    for k in range(KC):
                        last = e.matmul(psq(tt), hTblk[:, k, tt * 128:(tt + 1) * 128], wq[:, k, :],
                                        start=(k == 0), stop=(k == KC - 1))
                    return last
                T.op("pe", [kwq, "hTblk"], [psqk(tt)], mm)
                qi = (hp % 2) * 4 + tt
                T.op("dve", [psqk(tt)], [("qraw", tt)], lambda e: e.tensor_copy(qraw[tt][:, :], psq(tt)))
                self.normrope(qraw[tt][:, :], ("qraw", tt), self.gq_b, "gq_b", tb * 4 + tt, qr[qi], ("qr", qi))

        def qtr(hp):
            qt_ = QT[hp % 2]
            for tt in range(4):
                qi = (hp % 2) * 4 + tt
                q_ = qr[qi]
                pb, pbk = trbuf()

                def tr(e):
                    e.transpose(pb[:, 0:128], q_[:, 0:128], self.identb[:, :])
                    return e.transpose(pb[:, 128:256], q_[:, 128:256], self.identb[:, :])
                T.op("pe", [("qr", qi), "identb"], [pbk], tr)
                T.op("dve", [pbk], [(("QT", hp % 2), tt)], lambda e: e.tensor_copy(
                    qt_[:, :, tt * 128:(tt + 1) * 128], pb[:, 0:256].rearrange("p (a b) -> p a b", a=2)))

        def tail_norm(h):
            items = []
            for qt in range(4):
                ai = st["nat"] % 8
                st["nat"] += 1
                a_, ka_ = at[ai], ("at", ai)
                rd, krd = rden[:, ai:ai + 1], ("rden", ai)
                po = pvap(qt)
                T.op("dve", [pvk(qt)], [krd], lambda e: e.reciprocal(rd, po[:, 128:129]))
                T.op("dve", [pvk(qt), krd], [ka_], lambda e: e.tensor_scalar(a_[:, :], po[:, 0:128], rd, None, ALU.mult))
                items.append((h, qt, a_, ka_))
            return items

        def tail_tr(items):
            for (h, qt, a_, ka_) in items:
                pb, pbk = trbuf()
                T.op("pe", [ka_, "identb"], [pbk], lambda e: e.transpose(pb[:, 0:128], a_[:, :], self.identb[:, :]))
                T.op("dve", [pbk], [("attnT", h)], lambda e: e.tensor_copy(
                    attnT[:, h, qt * 128:(qt + 1) * 128], pb[:, 0:128]))

        for tb in range(NTB):
            T.dma("sp", "hTblk", [("hT_d", tb)], ["hTblk"], lambda e: e.dma_start(
                out=hTblk[:, :, :], in_=self.hT_d[:, tb * TB:(tb + 1) * TB].rearrange("(k p) n -> p k n", p=128)))
            T.dma("sp", "uTblk", [("uT_d", c) for c in range(16)], ["uTblk"], lambda e: e.dma_start(
                out=uTblk[:, :, :], in_=self.uT_d[:, tb * TB:(tb + 1) * TB].rearrange("(k p) n -> p k n", p=128)))
            pending = None
            qmm(tb, 0)
            for hp in range(8):
                g = hp // 2
                if hp < 7:
                    qmm(tb, hp + 1)
                qtr(hp)
                qt_ = QT[hp % 2]
                kqts = [(("QT", hp % 2), tt) for tt in range(4)]
                for hh in range(2):
                    h = 2 * hp + hh
                    sc = {}
                    for step in range(NKEY + LA):
                        if step < NKEY:
                            kc = step
                            pss, pssk = self.psnext(PS_S)
                            T.op("pe", kqts + ["KT"], [pssk], lambda e: e.matmul(
                                pss[:, :], self.KT[:, g, kc * 128:(kc + 1) * 128], qt_[:, hh, :], start=True, stop=True))
                            pi = st["npt"] % NPT
                            st["npt"] += 1
                            T.op("act", [pssk], [("PT", pi)], lambda e: e.activation(PT[pi][:, :], pss[:, :], AF.Exp, scale=isc))
                            sc[kc] = pi
                        if step == LA and pending is not None:
                            tail_tr(pending)
                            pending = None
                        j = step - LA
                        if j >= 0:
                            pj = sc.pop(j)

                            def pv(e):
                                last = None
                                for qt in range(4):
                                    last = e.matmul(pvap(qt), PT[pj][:, qt * 128:(qt + 1) * 128],
                                                    self.Vaug[:, j, g, 0:129],
                                                    start=(j == 0 and qt % 2 == 0), stop=(j == NKEY - 1),
                                                    skip_group_check=True)
                                return last
                            T.op("pe", [("PT", pj), "V"], [("ps", 3), ("ps", 4)], pv)
                    pending = tail_norm(h)
            tail_tr(pending)
            akeys = [("attnT", h) for h in range(16)]
            for ccp in range(8):
                def gemm2(tag, act, akeys_):
                    wp, wk = self.wnext((tag, tb, ccp))
                    res = []
                    for cl_ in range(2):
                        cs = slice(cl_ * 128, (cl_ + 1) * 128)
                        pst, psk = self.psnext()

                        def mm(e):
                            last = None
                            for k in range(KC):
                                last = e.matmul(pst[:, :], wp[:, k, cs], act[:, k, :], start=(k == 0), stop=(k == KC - 1))
                            return last
                        T.op("pe", [wk] + akeys_, [psk], mm)
                        res.append((pst, psk))
                    return res
                r1 = gemm2("gla", hTblk, ["hTblk"])
                for cl_ in range(2):
                    T.op("act", [r1[cl_][1]], [("tA", cl_)], lambda e: e.activation(tA[cl_][:, :], r1[cl_][0][:, :], AF.Sigmoid))
                r2 = gemm2("oa", attnT, akeys)
                for cl_ in range(2):
                    T.op("dve", [("tA", cl_), r2[cl_][1]], [("tA", cl_)], lambda e: e.tensor_tensor(
                        tA[cl_][:, :], tA[cl_][:, :], r2[cl_][0][:, :], ALU.mult))
                r3 = gemm2("glr", hTblk, ["hTblk"])
                for cl_ in range(2):
                    T.op("act", [r3[cl_][1]], [("tB", cl_)], lambda e: e.activation(tB[cl_][:, :], r3[cl_][0][:, :], AF.Sigmoid))
                r4 = gemm2("or", uTblk, ["uTblk"])
                for cl_ in range(2):
                    cc = 2 * ccp + cl_
                    T.op("dve", [("tB", cl_), r4[cl_][1]], [("tB", cl_)], lambda e: e.tensor_tensor(
                        tB[cl_][:, :], tB[cl_][:, :], r4[cl_][0][:, :], ALU.mult))
                    T.op("pool", [("tA", cl_), ("tB", cl_)], [("mTblk", cc)], lambda e: e.tensor_tensor(
                        mTblk[:, cc, :], tA[cl_][:, :], tB[cl_][:, :], ALU.add))
            T.dma("sp", "mTst", [("mTblk", cc) for cc in range(16)], [("mT_d", tb)], lambda e: e.dma_start(
                out=self.mT_d[:, tb * TB:(tb + 1) * TB].rearrange("(k p) n -> p k n", p=128), in_=mTblk[:, :, :]))

    def p6_mlp(self):
        T = self.T
        self.nm_alloc()
        mTblk = self.ar("mTblk2", [128, KC, TB], BF16)
        h2T = mTblk
        x1 = self.ar("x1buf", [128, 4, D], F32)
        actT = self.ar("actT", [128, 64, TB], BF16)
        gaa = self.ar("gaa", [128, D], F32)
        gaf = self.ar("gaf", [128, D], F32)
        gfb = self.ar("gfb", [128, D], F32)
        tmp = [self.ar("tmp%d" % i, [128, 512], F32) for i in range(3)]
        obuf = self.nm_xn
        T.dma("sp", "gaa", ["ga_d"], ["gaa"], lambda e: e.dma_start(
            out=gaa[:, :], in_=bass.AP(self.ga_d.tensor, 0, [[0, 128], [1, D]])))
        T.dma("sp", "gaf", ["ga_d"], ["gaf"], lambda e: e.dma_start(
            out=gaf[:, :], in_=bass.AP(self.ga_d.tensor, D, [[0, 128], [1, D]])))
        T.dma("sp", "gfb", [], ["gfb"], lambda e: e.dma_start(
            out=gfb[:, :], in_=bass.AP(self.gf_d.tensor, 0, [[0, 128], [1, D]])))
        nt = 0
        for tb in range(NTB):
            T.dma("sp", "mTld", [("mT_d", tb)], ["mTblk2"] + [("h2T", tt) for tt in range(4)], lambda e: e.dma_start(
                out=mTblk[:, :, :], in_=self.mT_d[:, tb * TB:(tb + 1) * TB].rearrange("(k p) n -> p k n", p=128)))
            x1k = [("x1", tt) for tt in range(4)]
            T.dma("sp", "x1ld", [], x1k, lambda e: e.dma_start(
                out=x1[:, :, :], in_=self.x_d[tb * TB:(tb + 1) * TB, :].rearrange("(t p) n -> p t n", p=128)))
            for np_ in range(8):
                wo, kwo = self.wnext(("wout", tb, np_))
                cols = slice(np_ * 256, (np_ + 1) * 256)
                for tt in range(4):
                    pst, psk = self.psnext()

                    def mm(e):
                        last = None
                        for k in range(KC):
                            last = e.matmul(pst[:, 0:256], mTblk[:, k, tt * 128:(tt + 1) * 128], wo[:, k, :],
                                            start=(k == 0), stop=(k == KC - 1))
                        return last
                    T.op("pe", [kwo, "mTblk2"], [psk], mm)
                    t_ = tmp[nt % 3]
                    kt_ = ("tmp", nt % 3)
                    nt += 1
                    T.op("dve", [psk, "gaa"], [kt_], lambda e: e.tensor_tensor(t_[:, 0:256], pst[:, 0:256], gaa[:, cols], ALU.mult))
                    T.op("pool", [kt_, ("x1", tt)], [("x1", tt)], lambda e: e.tensor_tensor(
                        x1[:, tt, cols], x1[:, tt, cols], t_[:, 0:256], ALU.add))
            if self.debug:
                T.dma("sp", "dbg2", x1k, [("dbg2", tb)], lambda e: e.dma_start(
                    out=self.dbg2_d[tb * TB:(tb + 1) * TB, :].rearrange("(t p) n -> p t n", p=128), in_=x1[:, :, :]))
            for tt in range(4):
                self.norm_mod_T(x1[:, tt, :], ("x1", tt), 4, lambda j: h2T[:, j, tt * 128:(tt + 1) * 128], [("h2T", tt), "mTblk2"])
            h2k = [("h2T", tt) for tt in range(4)]
            for fp_ in range(32):
                wu, kwu = self.wnext(("wup", tb, fp_))
                for cl_ in range(2):
                    fc = 2 * fp_ + cl_
                    cs = slice(cl_ * 128, (cl_ + 1) * 128)
                    pst, psk = self.psnext()

                    def mm(e):
                        last = None
                        for k in range(KC):
                            last = e.matmul(pst[:, :], wu[:, k, cs], h2T[:, k, :], start=(k == 0), stop=(k == KC - 1))
                        return last
                    T.op("pe", [kwu] + h2k, [psk], mm)
                    t_ = tmp[nt % 3]
                    kt_ = ("tmp", nt % 3)
                    nt += 1
                    T.op("act", [psk], [kt_], lambda e: e.activation(t_[:, :], pst[:, :], AF.Relu))
                    T.op("pool", [kt_], [("actT", fc)], lambda e: e.tensor_tensor(actT[:, fc, :], t_[:, :], t_[:, :], ALU.mult))
            for cb in range(4):
                banks = [0, 1, 2, 3] if cb % 2 == 0 else [4, 5, 6, 7]
                cols = slice(cb * 512, (cb + 1) * 512)
                for fp_ in range(8):
                    wd, kwd = self.wnext(("wdn", tb, cb, fp_))

                    def mm(e):
                        last = None
                        for tt in range(4):
                            for j in range(8):
                                last = e.matmul(self.ps[banks[tt]][:, :], actT[:, fp_ * 8 + j, tt * 128:(tt + 1) * 128],
                                                wd[:, j, :], start=(fp_ == 0 and j == 0), stop=(fp_ == 7 and j == 7))
                        return last
                    T.op("pe", [kwd] + [("actT", fp_ * 8 + j) for j in range(8)], [("ps", b) for b in banks], mm)
                for tt in range(4):
                    t_ = tmp[nt % 3]
                    kt_ = ("tmp", nt % 3)
                    nt += 1
                    T.op("dve", [("ps", banks[tt]), "gaf"], [kt_], lambda e: e.tensor_tensor(
                        t_[:, :], self.ps[banks[tt]][:, :], gaf[:, cols], ALU.mult))
                    T.op("pool", [kt_, ("x1", tt)], [("x1", tt)], lambda e: e.tensor_tensor(
                        x1[:, tt, cols], x1[:, tt, cols], t_[:, :], ALU.add))
            for tt in range(4):
                i = self.nm_i
                self.nm_i += 1
                ss = self.nm_st[:, (i % 8) * 2:(i % 8) * 2 + 1]
                rs = self.nm_st[:, (i % 8) * 2 + 1:(i % 8) * 2 + 2]
                kss, krs = ("nm_ss", i % 8), ("nm_rs", i % 8)
                T.op("act", [("x1", tt)], ["nm_junk", kss], lambda e: e.activation(
                    self.nm_junk[:, :], x1[:, tt, :], AF.Square, accum_out=ss))
                T.op("act", [kss, "epsb"], [kss], lambda e: e.activation(ss, ss, AF.Sqrt, scale=1.0 / D, bias=self.epsb[:, 0:1]))
                T.op("dve", [kss], [krs], lambda e: e.reciprocal(rs, ss))
                T.op("dve", [("x1", tt), krs, "gfb"], ["nm_xn"], lambda e: e.scalar_tensor_tensor(
                    obuf[:, :], x1[:, tt, :], rs, gfb[:, :], ALU.mult, ALU.mult))
                r0 = tb * TB + tt * 128
                T.dma("sp", "ost", ["nm_xn"], [("out", tb, tt)], lambda e: e.dma_start(
                    out=self.out_d[r0:r0 + 128, :], in_=obuf[:, :]))

    def finish(self):
        T = self.T
        if self.debug:
            T.barrier()
            self.ar_ptr = self.ar_end - 16384 - 64
            dbg = self.ar("dbgsb", [128, 4096], F32)
            T.op("dve", [], ["dbg"], lambda e: e.memset(dbg[:, :], 0.0))
            T.op("dve", ["cosT", "dbg"], ["dbg"], lambda e: e.tensor_copy(dbg[:, 0:1024], self.cosT[:, :, :].rearrange("p a b -> p (a b)")))
            T.op("dve", ["sinT", "dbg"], ["dbg"], lambda e: e.tensor_copy(dbg[:, 1024:2048], self.sinT[:, :, :].rearrange("p a b -> p (a b)")))
            T.op("dve", ["modL", "dbg"], ["dbg"], lambda e: e.tensor_copy(dbg[:, 2048:2144], self.modL[:, :]))
            T.op("dve", ["modC", "dbg"], ["dbg"], lambda e: e.tensor_copy(dbg[:, 2144:2240], self.modC[:, :]))
            T.op("dve", ["cl", "dbg"], ["dbg"], lambda e: e.tensor_copy(dbg[:, 2240:2304], self.cl[:, :, :, :].rearrange("p a b c -> p (a b c)")))
            if self.stage == 3:
                T.op("dve", ["dbg"], ["dbg"], lambda e: e.tensor_copy(dbg[:, 2304:2304 + 1152], self.KT[:, 0, 0:1152]))
                T.op("dve", ["dbg"], ["dbg"], lambda e: e.tensor_copy(dbg[:, 3456:3456 + 520], self.Vaug[:, 5, :, :].rearrange("p a b -> p (a b)")))
            T.dma("sp", "dbg", ["dbg"], ["dbg_d"], lambda e: e.dma_start(out=self.dbg_d, in_=dbg[:, :]))
        T.final_wait("sp")


def prep_inputs(inp, b):
    f = lambda a: np.ascontiguousarray(a, dtype=np.float32)
    fm = lambda v: f(np.asarray(v).reshape(-1, 16, 128).transpose(2, 0, 1).reshape(128, -1))
    pv = np.concatenate([
        fm(inp["c"][b][None]), fm(inp["c_ctx"][None]), fm(inp["g_mix"][0][None]), fm(inp["g_mlp"][0][None]),
        fm(inp["conv_w"][0]), fm(inp["conv_b"][0][None]), fm(inp["b_rg"][0]), fm(inp["b_ig"][0]),
        fm(inp["lru_lambda"][0]), fm(inp["b_mod"][0].reshape(6, D))], axis=1)
    assert pv.shape == (128, NPV), pv.shape
    return {
        "x": f(inp["x"][b]), "ctx": f(inp["ctx"][b]), "pvec": f(pv),
        "w_mod": f(inp["w_mod"][0]), "w_in": f(inp["w_in"][0]),
        "q_gain": f(inp["q_gain"][0][None]), "k_gain": f(inp["k_gain"][0][None]),
        "w_rg": f(inp["w_rg"][0].reshape(2 * 16 * 128, 128)), "w_ig": f(inp["w_ig"][0].reshape(2 * 16 * 128, 128)),
        "w_o_attn": f(inp["w_o_attn"][0]), "w_o_rnn": f(inp["w_o_rnn"][0]), "w_out": f(inp["w_out"][0]),
        "w_up": f(inp["w_up"][0]), "w_down": f(inp["w_down"][0]), "g_final": f(inp["g_final"][None]),
    }


def run(inputs, stage=99, debug=False, trace=False, ncores=8):
    bld = Builder(stage=stage, debug=debug)
    in_maps = [prep_inputs(inputs, b) for b in range(ncores)]
    res = run_bass_kernel_spmd(bld.nc, in_maps, core_ids=list(range(ncores)), trace=trace)
    return res


def kernel(**inputs):
    inputs = {k: np.asarray(v) for k, v in inputs.items()}
    res = run(inputs)
    return np.stack([res.results[b]["out"] for b in range(8)], axis=0).astype(np.float32)
```

```python
import math
import numpy as np
import concourse.bass as bass
import concourse.mybir as mybir
from concourse.bass_utils import run_bass_kernel_spmd

F32 = mybir.dt.float32
BF16 = mybir.dt.bfloat16
I32 = mybir.dt.int32
AF = mybir.ActivationFunctionType
ALU = mybir.AluOpType
AX = mybir.AxisListType

D = 2048
S = 2048
L = 256
NT = S // 128
NCT = L // 128
KC = D // 128
NKEY = (S + L) // 128
N_IN = 11264
DFF = 8192
EPS = 1e-6
TB = 512
NTB = S // TB

PV_C, PV_CC, PV_GMIX, PV_GMLP, PV_CW, PV_CB, PV_BRG, PV_BIG, PV_LAM, PV_BMOD = (
    0, 16, 32, 48, 64, 128, 144, 176, 208, 240)
NPV = 240 + 96


class Tracker:
    def __init__(self, nc):
        self.nc = nc
        self.engs = {"pe": nc.tensor, "act": nc.scalar, "dve": nc.vector,
                     "pool": nc.gpsimd, "sp": nc.sync}
        self.esem = {k: nc.alloc_semaphore("es_" + k) for k in self.engs}
        self.ecnt = {k: 0 for k in self.engs}
        self.waited = {k: {} for k in self.engs}
        self.last_w = {}
        self.readers = {}
        self.dsem = {}
        self.dcnt = {}

    def _deps(self, eng, reads, writes, ww_ok=False):
        deps = []
        for r in reads:
            w = self.last_w.get(r)
            if w is not None:
                deps.append(w)
        for wkey in writes:
            w = self.last_w.get(wkey)
            if w is not None and not (w[2] == eng and (eng == "pe" or ww_ok)):
                deps.append(w)
            for rd in self.readers.get(wkey, ()):
                deps.append(rd)
        return deps

    def _wait(self, eng, deps):
        e = self.engs[eng]
        wd = self.waited[eng]
        need = {}
        for (sem, val, _) in deps:
            k = id(sem)
            if val > wd.get(k, 0) and val > need.get(k, (None, 0))[1]:
                need[k] = (sem, val)
        for k, (sem, val) in need.items():
            e.wait_ge(sem, val)
            wd[k] = val

    def _commit(self, token, reads, writes):
        for r in reads:
            self.readers.setdefault(r, []).append(token)
        for w in writes:
            self.last_w[w] = token
            self.readers[w] = []

    def op(self, eng, reads, writes, fn, ww_ok=False):
        self._wait(eng, self._deps(eng, reads, writes, ww_ok))
        inst = fn(self.engs[eng])
        self.ecnt[eng] += 1
        inst.then_inc(self.esem[eng], 1)
        self._commit((self.esem[eng], self.ecnt[eng], eng), reads, writes)

    def dma(self, q, slot, reads, writes, fn):
        if slot not in self.dsem:
            self.dsem[slot] = self.nc.alloc_semaphore("ds_%d" % len(self.dsem))
            self.dcnt[slot] = 0
        sem = self.dsem[slot]
        deps = self._deps("dma:" + slot, reads, writes)
        if self.dcnt[slot] > 0:
            deps.append((sem, self.dcnt[slot], None))
        self._wait(q, deps)
        inst = fn(self.engs[q])
        self.dcnt[slot] += 16
        inst.then_inc(sem, 16)
        self._commit((sem, self.dcnt[slot], "dma:" + slot), reads, writes)

    def barrier(self):
        toks = [(self.esem[k], self.ecnt[k], k) for k in self.engs if self.ecnt[k] > 0]
        toks += [(self.dsem[s], self.dcnt[s], None) for s in self.dsem]
        for eng in self.engs:
            self._wait(eng, toks)
        self.last_w = {}
        self.readers = {}

    def final_wait(self, eng):
        deps = [(self.esem[k], self.ecnt[k], k) for k in self.engs if self.ecnt[k] > 0 and k != eng]
        deps += [(self.dsem[s], self.dcnt[s], None) for s in self.dsem]
        self._wait(eng, deps)


def sb_view(ap, dims):
    return bass.AP(ap.tensor, ap.offset, [list(ap.ap[0])] + [list(d) for d in dims])


class Builder:
    def __init__(self, stage=99, debug=False):
        self.stage = stage
        self.debug = debug
        self.nc = bass.Bass("TRN2", target_bir_lowering=False)
        self.T = Tracker(self.nc)
        self.build()

    def dram_in(self, name, shape, dt=F32):
        return self.nc.dram_tensor(name, list(shape), dt, kind="ExternalInput").ap()

    def scratch(self, name, shape, dt):
        kind = "ExternalOutput" if self.debug else "Internal"
        return self.nc.dram_tensor(name, list(shape), dt, kind=kind).ap()

    def sb(self, name, shape, dt):
        return self.nc.alloc_sbuf_tensor("s_" + name, list(shape), dt)

    def sb_at(self, name, shape, dt, off):
        return self.nc.alloc_sbuf_tensor_at("s_" + name, list(shape), dt, offset=off)

    def plan(self, tag, src):
        self.wplan.append((tag, src))

    def _issue_panel(self, i):
        tag, src = self.wplan[i]
        slot = i % self.NSLOT
        kc, n = src.shape[1], src.shape[2]
        dst = self.wslots[slot][:, 0:kc * n].rearrange("p (k n) -> p k n", k=kc)
        self.T.dma("pool", "w%d" % slot, [], [("w", slot)],
                   lambda e: e.dma_start(out=dst, in_=src))

    def wnext(self, tag):
        i = self.wpos
        assert self.wplan[i][0] == tag, (self.wplan[i][0], tag)
        while self.wissued < min(len(self.wplan), i + self.NSLOT):
            self._issue_panel(self.wissued)
            self.wissued += 1
        self.wpos += 1
        slot = i % self.NSLOT
        src = self.wplan[i][1]
        kc, n = src.shape[1], src.shape[2]
        return (self.wslots[slot][:, 0:kc * n].rearrange("p (k n) -> p k n", k=kc), ("w", slot))

    def wsrc(self, w, r0, nrows, c0, ncols):
        return w[r0:r0 + nrows, c0:c0 + ncols].rearrange("(k p) n -> p k n", p=128)

    def psnext(self, pool=None):
        pool = pool or list(range(8))
        i = self.pscur.get(tuple(pool), 0)
        self.pscur[tuple(pool)] = i + 1
        b = pool[i % len(pool)]
        return self.ps[b], ("ps", b)


    def ar(self, name, shape, dt):
        nbytes = int(np.prod(shape[1:])) * (4 if dt in (F32, I32) else 2)
        nbytes = (nbytes + 63) // 64 * 64
        off = self.ar_ptr
        self.ar_ptr += nbytes
        assert self.ar_ptr <= self.ar_end, (name, self.ar_ptr, self.ar_end)
        self.ar_id += 1
        return self.nc.alloc_sbuf_tensor_at("a%d_%s" % (self.ar_id, name), list(shape), dt, offset=off)

    def ar_mark(self):
        return self.ar_ptr

    def ar_release(self, mark):
        self.T.barrier()
        self.ar_ptr = mark

    def build(self):
        nc, T = self.nc, self.T
        self.x_d = self.dram_in("x", [S, D])
        self.ctx_d = self.dram_in("ctx", [L, D])
        self.pvec_d = self.dram_in("pvec", [128, NPV])
        self.w_mod = self.dram_in("w_mod", [D, 6 * D])
        self.w_in = self.dram_in("w_in", [D, N_IN])
        self.qg_d = self.dram_in("q_gain", [1, 128])
        self.kg_d = self.dram_in("k_gain", [1, 128])
        self.w_rg = self.dram_in("w_rg", [2 * 16 * 128, 128])
        self.w_ig = self.dram_in("w_ig", [2 * 16 * 128, 128])
        self.w_oa = self.dram_in("w_o_attn", [D, D])
        self.w_or = self.dram_in("w_o_rnn", [D, D])
        self.w_out = self.dram_in("w_out", [D, D])
        self.w_up = self.dram_in("w_up", [D, DFF])
        self.w_down = self.dram_in("w_down", [DFF, D])
        self.gf_d = self.dram_in("g_final", [1, D])
        self.out_d = nc.dram_tensor("out", [S, D], F32, kind="ExternalOutput").ap()
        self.hT_d = self.scratch("hT_d", [D, S], BF16)
        self.uT_d = self.scratch("uT_d", [D, S], BF16)
        self.mT_d = self.scratch("mT_d", [D, S], BF16)
        self.ga_d = self.scratch("ga_d", [2, D], F32)
        if self.debug:
            self.dbg_d = nc.dram_tensor("dbg", [128, 4096], F32, kind="ExternalOutput").ap()
            self.dbg2_d = nc.dram_tensor("dbg2", [S, D], F32, kind="ExternalOutput").ap()

        self.NSLOT = 4
        self.wslots = [self.sb("wslot%d" % i, [128, 4096], BF16) for i in range(self.NSLOT)]
        self.wplan, self.wpos, self.wissued = [], 0, 0
        self.pvec = self.sb("pvec", [128, NPV], F32)
        self.identf = self.sb("identf", [128, 128], F32)
        self.identb = self.sb("identb", [128, 128], BF16)
        self.cosT = self.sb("cosT", [128, NT, 64], F32)
        self.sinT = self.sb("sinT", [128, NT, 64], F32)
        self.gq_b = self.sb("gq_b", [128, 128], F32)
        self.gk_b = self.sb("gk_b", [128, 128], F32)
        self.modL = self.sb("modL", [128, 96], F32)
        self.modC = self.sb("modC", [128, 96], F32)
        self.AB = self.sb("AB", [128, 6, 16], F32)
        self.cl = self.sb("cl", [128, 2, 2, 16], F32)
        self.epsb = self.sb("epsb", [128, 1], F32)
        self.silu = self.sb("silu", [128, 16, 2], BF16)
        self.nm_st = self.sb("nm_st", [128, 16], F32)
        self.nr_st = self.sb("nr_st", [128, 16], F32)
        self.ps = [nc.alloc_psum_tensor("ps%d" % i, [128, 512], F32) for i in range(8)]
        self.pscur = {}
        self.ar_ptr = (nc.sbuf_base + 63) // 64 * 64
        self.ar_end = nc.sbuf_top
        self.ar_id = 0
        self.nm_i = 0
        self.nr_i = 0

        self.plan_all()
        m0 = self.ar_mark()
        self.setup()
        self.ar_release(m0)
        self.KT = self.ar("KT", [128, 4, S + L], BF16)
        self.Vaug = self.ar("Vaug", [128, NKEY, 4, 130], BF16)
        m1 = self.ar_mark()
        self.hT = self.ar("hT", [128, KC, S], BF16)
        self.hcT = self.ar("hcT", [128, KC, L], BF16)
        m2 = self.ar_mark()
        self.nm_alloc(2)
        self.p0_mod(0, 2)
        self.p1_norm()
        self.kv_base, self.kv_end = m0, m1
        if self.stage >= 2:
            self.ar_release(m2)
            self.p3_rnn()
        if self.stage >= 3:
            self.ar_release(m2)
            self.p2_kv()
        if self.stage >= 4:
            self.ar_release(m1)
            self.p4_attn()
        if self.stage >= 5:
            self.ar_release(m0)
            self.p6_mlp()
        self.finish()

    def plan_all(self):
        W = self.wsrc
        for m in range(2):
            for pp in range(8):
                self.plan(("mod", m, pp), W(self.w_mod, 0, D, m * D + pp * 256, 256))
        if self.stage >= 2:
            for pp in range(8):
                for cl_ in range(2):
                    c = 2 * pp + cl_
                    self.plan(("xr", c), W(self.w_in, 0, D, 3072 + c * 128, 128))
                    self.plan(("xg", c), W(self.w_in, 0, D, 5120 + c * 128, 128))
                for q in range(4):
                    m = 2 + (pp * 4 + q) // 8
                    mp = (pp * 4 + q) % 8
                    self.plan(("mod", m, mp), W(self.w_mod, 0, D, m * D + mp * 256, 256))
        if self.stage >= 3:
            for i in range(4):
                self.plan(("kv", i), W(self.w_in, 0, D, 2048 + i * 256, 256))
        if self.stage >= 4:
            for tb in range(NTB):
                for hp in range(8):
                    self.plan(("q", tb, hp), W(self.w_in, 0, D, hp * 256, 256))
                for ccp in range(8):
                    self.plan(("gla", tb, ccp), W(self.w_in, 0, D, 7168 + ccp * 256, 256))
                    self.plan(("oa", tb, ccp), W(self.w_oa, 0, D, ccp * 256, 256))
                    self.plan(("glr", tb, ccp), W(self.w_in, 0, D, 9216 + ccp * 256, 256))
                    self.plan(("or", tb, ccp), W(self.w_or, 0, D, ccp * 256, 256))
        if self.stage >= 5:
            for tb in range(NTB):
                for np_ in range(8):
                    self.plan(("wout", tb, np_), W(self.w_out, 0, D, np_ * 256, 256))
                for fp_ in range(32):
                    self.plan(("wup", tb, fp_), W(self.w_up, 0, D, fp_ * 256, 256))
                for cb in range(4):
                    for fp_ in range(8):
                        self.plan(("wdn", tb, cb, fp_), W(self.w_down, fp_ * 1024, 1024, cb * 512, 512))

    def setup(self):
        nc, T = self.nc, self.T
        T.dma("sp", "pvec", [], ["pvec"], lambda e: e.dma_start(out=self.pvec[:, :], in_=self.pvec_d))
        T.dma("sp", "gq", [], ["gq_b"], lambda e: e.dma_start(
            out=self.gq_b[:, :], in_=bass.AP(self.qg_d.tensor, 0, [[0, 128], [1, 128]])))
        T.dma("sp", "gk", [], ["gk_b"], lambda e: e.dma_start(
            out=self.gk_b[:, :], in_=bass.AP(self.kg_d.tensor, 0, [[0, 128], [1, 128]])))
        T.op("dve", [], ["modL"], lambda e: e.memset(self.modL[:, :], 0.0))
        T.op("dve", [], ["modC"], lambda e: e.memset(self.modC[:, :], 0.0))
        T.op("dve", [], ["epsb"], lambda e: e.memset(self.epsb[:, :], EPS))
        it = self.ar("iota_i", [128, 128], I32)
        T.op("pool", [], ["iota_i"], lambda e: e.iota(it[:, :], [[1, 128]], base=0, channel_multiplier=-1))
        T.op("dve", ["iota_i"], ["identf"], lambda e: e.tensor_scalar(
            self.identf[:, :], it[:, :], 0, None, ALU.is_equal))
        T.op("dve", ["identf"], ["identb"], lambda e: e.tensor_copy(self.identb[:, :], self.identf[:, :]))
        rowi = self.ar("rowi", [128, NT], I32)
        coli = self.ar("coli", [128, NT], I32)
        fi = self.ar("fi", [128, 32], I32)
        T.op("pool", [], ["rowi"], lambda e: e.iota(rowi[0:64, :], [[2, NT]], base=0, channel_multiplier=0))
        T.op("pool", [], ["rowi"], lambda e: e.iota(rowi[64:128, :], [[2, NT]], base=1, channel_multiplier=0))
        T.op("pool", [], ["coli"], lambda e: e.iota(coli[0:64, :], [[0, NT]], base=0, channel_multiplier=1))
        T.op("pool", [], ["coli"], lambda e: e.iota(coli[64:128, :], [[0, NT]], base=0, channel_multiplier=1))
        T.op("pool", [], ["fi"], lambda e: e.iota(fi[:, :], [[1, 32]], base=0, channel_multiplier=0))
        rowf = self.ar("rowf", [128, NT], F32)
        colf = self.ar("colf", [128, NT], F32)
        ff = self.ar("ff", [128, 32], F32)
        T.op("dve", ["rowi"], ["rowf"], lambda e: e.tensor_copy(rowf[:, :], rowi[:, :]))
        T.op("dve", ["coli"], ["colf"], lambda e: e.tensor_copy(colf[:, :], coli[:, :]))
        T.op("dve", ["fi"], ["ff"], lambda e: e.tensor_copy(ff[:, :], fi[:, :]))
        T.op("act", ["ff"], ["ff"], lambda e: e.activation(ff[:, :], ff[:, :], AF.Exp,
                                                            scale=-math.log(10000.0) / 32.0))
        ang = self.ar("ang", [128, NT, 64], F32)
        kf = self.ar("kf", [128, NT, 64], F32)
        ki = self.ar("ki", [128, NT, 64], I32)
        msk = self.ar("msk", [128, NT, 64], F32)
        ffb = sb_view(ff[:, :], [[0, NT], [1, 32]])
        T.op("dve", ["rowf", "ff"], ["ang"], lambda e: e.tensor_tensor(
            ang[:, :, 0:32], sb_view(rowf[:, :], [[1, NT], [0, 32]]), ffb, ALU.mult))
        T.op("dve", ["colf", "ff"], ["ang"], lambda e: e.tensor_tensor(
            ang[:, :, 32:64], sb_view(colf[:, :], [[1, NT], [0, 32]]), ffb, ALU.mult))
        TWO_PI = 2.0 * math.pi
        for (dst, dn, shift) in ((self.sinT, "sinT", 0.0), (self.cosT, "cosT", math.pi / 2.0)):
            T.op("dve", ["ang"], ["kf"], lambda e: e.tensor_scalar(
                kf[:, :, :], ang[:, :, :], shift, 1.0 / TWO_PI, ALU.add, ALU.mult))
            T.op("dve", ["kf"], ["ki"], lambda e: e.tensor_copy(ki[:, :, :], kf[:, :, :]))
            T.op("dve", ["ki"], ["kf"], lambda e: e.tensor_copy(kf[:, :, :], ki[:, :, :]))
            T.op("dve", ["kf"], ["kf"], lambda e: e.tensor_scalar(
                kf[:, :, :], kf[:, :, :], -TWO_PI, shift, ALU.mult, ALU.add))
            T.op("dve", ["kf", "ang"], ["kf"], lambda e: e.tensor_tensor(
                kf[:, :, :], kf[:, :, :], ang[:, :, :], ALU.add))
            T.op("dve", ["kf"], ["msk"], lambda e: e.tensor_scalar(
                msk[:, :, :], kf[:, :, :], math.pi, -TWO_PI, ALU.is_gt, ALU.mult))
            T.op("dve", ["kf", "msk"], ["kf"], lambda e: e.tensor_tensor(
                kf[:, :, :], kf[:, :, :], msk[:, :, :], ALU.add))
            T.op("dve", ["kf"], ["msk"], lambda e: e.tensor_scalar(
                msk[:, :, :], kf[:, :, :], -math.pi, TWO_PI, ALU.is_lt, ALU.mult))
            T.op("dve", ["kf", "msk"], ["kf"], lambda e: e.tensor_tensor(
                kf[:, :, :], kf[:, :, :], msk[:, :, :], ALU.add))
            T.op("dve", ["kf"], ["kf"], lambda e: e.tensor_scalar(
                kf[:, :, :], kf[:, :, :], math.pi, -math.pi, ALU.min, ALU.max))
            T.op("act", ["kf"], [dn], lambda e: e.activation(dst[:, :, :], kf[:, :, :], AF.Sin))
        lam = self.pvec[:, PV_LAM:PV_LAM + 32]
        sp_ = self.ar("sp_", [128, 32], F32)
        T.op("act", ["pvec"], ["sp_"], lambda e: e.activation(sp_[:, :], lam, AF.Exp, scale=-1.0))
        T.op("act", ["sp_"], ["sp_"], lambda e: e.activation(sp_[:, :], sp_[:, :], AF.Ln, bias=1.0))
        for d in range(2):
            T.op("dve", ["sp_"], ["cl"], lambda e: e.tensor_scalar(
                self.cl[:, d, 0, :], sp_[:, d * 16:(d + 1) * 16], -8.0, None, ALU.mult))
            T.op("dve", ["sp_"], ["cl"], lambda e: e.tensor_scalar(
                self.cl[:, d, 1, :], sp_[:, d * 16:(d + 1) * 16], -16.0, None, ALU.mult))
        T.op("act", ["pvec"], ["silu"], lambda e: e.activation(
            self.silu[:, :, 0], self.pvec[:, PV_C:PV_C + 16], AF.Silu))
        T.op("act", ["pvec"], ["silu"], lambda e: e.activation(
            self.silu[:, :, 1], self.pvec[:, PV_CC:PV_CC + 16], AF.Silu))

    def mod_step(self, pst, psk, m, pp, mbase):
        T = self.T
        wp, wk = self.wnext(("mod", m, pp))

        def mm(e):
            last = None
            for cl_ in range(2):
                col = ((m - mbase) * 16 + pp * 2 + cl_) * 2
                for k in range(KC):
                    last = e.matmul(pst[:, col:col + 2], wp[:, k, cl_ * 128:(cl_ + 1) * 128],
                                    self.silu[:, k, :], start=(k == 0), stop=(k == KC - 1))
            return last
        T.op("pe", [wk, "silu"], [psk], mm)

    def mod_finish(self, pst, psk, m0, m1):
        T = self.T
        nm = m1 - m0
        pv = pst[:, 0:nm * 32].rearrange("p (c t) -> p c t", t=2)
        bm = self.pvec[:, PV_BMOD + m0 * 16:PV_BMOD + m1 * 16]
        T.op("dve", [psk, "pvec"], ["modL"], lambda e: e.tensor_tensor(
            self.modL[:, m0 * 16:m1 * 16], pv[:, :, 0], bm, ALU.add))
        T.op("dve", [psk, "pvec"], ["modC"], lambda e: e.tensor_tensor(
            self.modC[:, m0 * 16:m1 * 16], pv[:, :, 1], bm, ALU.add))
        if m0 == 0:
            gm = self.pvec[:, PV_GMIX:PV_GMIX + 16]
            for (src, sn, ia) in ((self.modL, "modL", 0), (self.modC, "modC", 2)):
                T.op("dve", [sn, "pvec"], ["AB"], lambda e: e.scalar_tensor_tensor(
                    self.AB[:, ia, :], src[:, 16:32], 1.0, gm, ALU.add, ALU.mult))
                T.op("dve", [sn], ["AB"], lambda e: e.tensor_copy(self.AB[:, ia + 1, :], src[:, 0:16]))
        else:
            gm = self.pvec[:, PV_GMLP:PV_GMLP + 16]
            T.op("dve", ["modL", "pvec"], ["AB"], lambda e: e.scalar_tensor_tensor(
                self.AB[:, 4, :], self.modL[:, 64:80], 1.0, gm, ALU.add, ALU.mult))
            T.op("dve", ["modL"], ["AB"], lambda e: e.tensor_copy(self.AB[:, 5, :], self.modL[:, 48:64]))
            gaT = self.gaT
            for (gi, c0) in ((0, 32), (1, 80)):
                pt, pk = self.psnext([0, 1, 2, 3, 4, 5, 6])
                T.op("pe", ["modL", "identf"], [pk], lambda e: e.transpose(
                    pt[0:16, 0:128], self.modL[:, c0:c0 + 16], self.identf[:, :]))
                T.op("dve", [pk], ["gaT"], lambda e: e.tensor_copy(gaT[:, gi * 128:(gi + 1) * 128], pt[0:16, 0:128]))
            for gi in range(2):
                T.dma("sp", "gast", ["gaT"], ["ga_d"], lambda e: e.dma_start(
                    out=self.ga_d[gi:gi + 1, :].rearrange("o (j p) -> (o j) p", p=128),
                    in_=gaT[:, gi * 128:(gi + 1) * 128]))

    def p0_mod(self, m0, m1):
        pst, psk = self.psnext()
        for m in range(m0, m1):
            for pp in range(8):
                self.mod_step(pst, psk, m, pp, m0)
        self.mod_finish(pst, psk, m0, m1)

    def nm_alloc(self, nxn=1):
        self.nm_junk = self.ar("nm_junk", [128, D], BF16)
        self.nm_xns = [self.ar("nm_xn%d" % i, [128, D], F32) for i in range(nxn)]
        self.nm_xn = self.nm_xns[0]

    def norm_mod_T(self, xt, xkey, ia, dst_fn, dkeys):
        T = self.T
        i = self.nm_i
        self.nm_i += 1
        xi = i % len(self.nm_xns)
        junk, xn, st = self.nm_junk, self.nm_xns[xi], self.nm_st
        kxn = "nm_xn" if xi == 0 else "nm_xn1"
        ss = st[:, (i % 8) * 2:(i % 8) * 2 + 1]
        rs = st[:, (i % 8) * 2 + 1:(i % 8) * 2 + 2]
        kss, krs = ("nm_ss", i % 8), ("nm_rs", i % 8)
        T.op("act", [xkey], ["nm_junk", kss], lambda e: e.activation(
            junk[:, :], xt, AF.Square, accum_out=ss))
        T.op("act", [kss, "epsb"], [kss], lambda e: e.activation(ss, ss, AF.Sqrt, scale=1.0 / D, bias=self.epsb[:, 0:1]))
        T.op("dve", [kss], [krs], lambda e: e.reciprocal(rs, ss))
        T.op("act", [xkey, krs], [kxn], lambda e: e.activation(xn[:, :], xt, AF.Identity, scale=rs))
        for q4 in range(4):
            pst, psk = self.psnext()

            def tr(e):
                last = None
                for jj in range(4):
                    j = q4 * 4 + jj
                    last = e.transpose(pst[:, jj * 128:(jj + 1) * 128], xn[:, j * 128:(j + 1) * 128],
                                       self.identf[:, :])
                return last
            T.op("pe", [kxn, "identf"], [psk], tr)
            for jj in range(4):
                j = q4 * 4 + jj
                T.op("dve", [psk, "AB"], dkeys, lambda e: e.tensor_scalar(
                    dst_fn(j), pst[:, jj * 128:(jj + 1) * 128],
                    self.AB[:, ia, j:j + 1], self.AB[:, ia + 1, j:j + 1], ALU.mult, ALU.add), ww_ok=(j > 0))
        return rs, krs

    def p1_norm(self):
        T = self.T
        xts = [self.ar("xt%d" % i, [128, D], F32) for i in range(2)]
        for t in range(NCT + NT):
            xt = xts[t % 2]
            xk = ("xt", t % 2)
            src = self.ctx_d[t * 128:(t + 1) * 128, :] if t < NCT else \
                self.x_d[(t - NCT) * 128:(t - NCT + 1) * 128, :]
            T.dma("sp", "xt%d" % (t % 2), [], [xk], lambda e: e.dma_start(out=xt[:, :], in_=src))
            if t < NCT:
                self.norm_mod_T(xt[:, :], xk, 2, lambda j: self.hcT[:, j, t * 128:(t + 1) * 128], [("hcT", t)])
            else:
                tt = t - NCT
                self.norm_mod_T(xt[:, :], xk, 0, lambda j: self.hT[:, j, tt * 128:(tt + 1) * 128], [("hT", tt)])
        for tb in range(NTB):
            dst = self.hT_d[:, tb * TB:(tb + 1) * TB].rearrange("(k p) n -> p k n", p=128)
            T.dma("sp", "hTst", [("hT", tb * 4 + i) for i in range(4)], [("hT_d", tb)],
                  lambda e: e.dma_start(out=dst, in_=self.hT[:, :, tb * TB:(tb + 1) * TB]))

    def nr_alloc(self, nh=2):
        self.nr_qn = [self.ar("nr_qn%d" % i, [128, nh * 128], F32) for i in range(2)]
        self.nr_t = [[self.ar("nr_t%d_%d" % (i, j), [128, nh * 64], F32) for j in range(4)] for i in range(2)]
        self.nr_junks = [self.ar("nr_junk%d" % i, [128, 128], F32) for i in range(4)]
        self.nr_jn = 0

    def nr_stats(self, ps, pskey, nh=2):
        T = self.T
        i = self.nr_i
        self.nr_i += 1
        st = self.nr_st[:, (i % 4) * 4:(i % 4) * 4 + nh]
        rs = self.nr_st[:, (i % 4) * 4 + 2:(i % 4) * 4 + 2 + nh]
        kst, krs = ("nr_ss", i % 4), ("nr_rs", i % 4)
        for h in range(nh):
            jn = self.nr_jn
            self.nr_jn += 1
            jk = self.nr_junks[jn % 4]
            T.op("dve", [pskey], [("nr_junk", jn % 4), (kst, h)], lambda e: e.scalar_tensor_tensor(
                jk[:, :], ps[:, h * 128:(h + 1) * 128], 1.0, ps[:, h * 128:(h + 1) * 128], ALU.mult, ALU.mult,
                accum_out=st[:, h:h + 1]))
        ksts = [(kst, h) for h in range(nh)]
        T.op("act", ksts + ["epsb"], ksts, lambda e: e.activation(st, st, AF.Ln, scale=1.0 / 128, bias=self.epsb[:, 0:1]))
        T.op("act", ksts, [krs], lambda e: e.activation(rs, st, AF.Exp, scale=-0.5))
        return (i, rs, krs)

    def nr_apply(self, hd, ps, pskey, gain, gkey, tile, out, okey, nh=2):
        T = self.T
        i, rs, krs = hd
        if tile is None:
            for h in range(nh):
                T.op("dve", [pskey, krs, gkey], [okey], lambda e: e.scalar_tensor_tensor(
                    out[:, h * 128:(h + 1) * 128], ps[:, h * 128:(h + 1) * 128], rs[:, h:h + 1], gain[:, :],
                    ALU.mult, ALU.mult), ww_ok=(h > 0))
            return
        qn = self.nr_qn[i % 2]
        kq = ("nr_qn", i % 2)
        t1, t2, t3, t4 = self.nr_t[i % 2]
        kt = [("nr_t", i % 2, j) for j in range(4)]
        for h in range(nh):
            T.op("dve", [pskey, krs, gkey], [kq], lambda e: e.scalar_tensor_tensor(
                qn[:, h * 128:(h + 1) * 128], ps[:, h * 128:(h + 1) * 128], rs[:, h:h + 1], gain[:, :],
                ALU.mult, ALU.mult), ww_ok=(h > 0))
        ev = sb_view(qn[:, 0:], [[128, nh], [2, 64]])
        od = sb_view(qn[:, 1:], [[128, nh], [2, 64]])
        oev = sb_view(out[:, 0:], [[128, nh], [2, 64]])
        ood = sb_view(out[:, 1:], [[128, nh], [2, 64]])
        cs = sb_view(self.cosT[:, tile, :], [[0, nh], [1, 64]])
        sn = sb_view(self.sinT[:, tile, :], [[0, nh], [1, 64]])
        v3 = lambda t: t[:, :].rearrange("p (a b) -> p a b", a=nh)
        T.op("dve", [kq, "cosT"], [kt[0]], lambda e: e.tensor_tensor(v3(t1), ev, cs, ALU.mult))
        T.op("dve", [kq, "sinT"], [kt[1]], lambda e: e.tensor_tensor(v3(t2), od, sn, ALU.mult))
        T.op("dve", [kt[0], kt[1]], [okey], lambda e: e.tensor_tensor(oev, v3(t1), v3(t2), ALU.subtract))
        T.op("pool", [kq, "sinT"], [kt[2]], lambda e: e.tensor_tensor(v3(t3), ev, sn, ALU.mult))
        T.op("pool", [kq, "cosT"], [kt[3]], lambda e: e.tensor_tensor(v3(t4), od, cs, ALU.mult))
        T.op("pool", [kt[2], kt[3]], [okey], lambda e: e.tensor_tensor(ood, v3(t3), v3(t4), ALU.add))

    def normrope(self, ps, pskey, gain, gkey, tile, out, okey, nh=2):
        hd = self.nr_stats(ps, pskey, nh)
        self.nr_apply(hd, ps, pskey, gain, gkey, tile, out, okey, nh)

    def p2_kv(self):
        T = self.T
        self.nr_alloc()
        krs_ = [self.ar("kr%d" % i, [128, 256], BF16) for i in range(2)]
        kraws = [self.ar("kraw%d" % i, [128, 256], F32) for i in range(2)]
        T.op("pool", [], ["Vones"], lambda e: e.memset(self.Vaug[:, :, :, 128:130], 1.0))
        n = 0
        for i in range(4):
            wp, wk = self.wnext(("kv", i))
            for t in range(NKEY):
                if t < NCT:
                    stat = lambda k: self.hcT[:, k, t * 128:(t + 1) * 128]
                    skey = ("hcT", t)
                else:
                    stat = lambda k: self.hT[:, k, (t - NCT) * 128:(t - NCT + 1) * 128]
                    skey = ("hT", t - NCT)
                pst, psk = self.psnext()

                def mm(e):
                    last = None
                    for k in range(KC):
                        last = e.matmul(pst[:, 0:256], stat(k), wp[:, k, :], start=(k == 0), stop=(k == KC - 1))
                    return last
                T.op("pe", [wk, skey], [psk], mm)
                if i < 2:
                    kr = krs_[n % 2]
                    kk = ("kr", n % 2)
                    kraw = kraws[n % 2]
                    kwk = ("kraw", n % 2)
                    n += 1
                    T.op("act", [psk], [kwk], lambda e: e.activation(kraw[:, :], pst[:, 0:256], AF.Identity))
                    self.normrope(kraw[:, :], kwk, self.gk_b, "gk_b", None if t < NCT else t - NCT, kr, kk)
                    ptt, ptk = self.psnext()
                    pb = ptt[:, 0:128].bitcast(BF16)

                    def tr(e):
                        e.transpose(pb[:, 0:128], kr[:, 0:128], self.identb[:, :])
                        return e.transpose(pb[:, 128:256], kr[:, 128:256], self.identb[:, :])
                    T.op("pe", [kk, "identb"], [ptk], tr)
                    T.op("act", [ptk], [("KT", i, t)], lambda e: e.activation(
                        self.KT[:, 2 * i:2 * i + 2, t * 128:(t + 1) * 128],
                        pb[:, 0:256].rearrange("p (a b) -> p a b", a=2), AF.Identity))
                else:
                    g0 = 2 * (i - 2)
                    T.op("act", [psk], [("V", i, t)], lambda e: e.activation(
                        self.Vaug[:, t, g0:g0 + 2, 0:128],
                        pst[:, 0:256].rearrange("p (a b) -> p a b", a=2), AF.Identity))

    def p3_rnn(self):
        T = self.T
        NC_ = S + 2 * L
        NW = S + L
        save = self.ar_ptr
        self.ar_ptr = self.kv_base
        xr = self.ar("xr", [128, NW], F32)
        self.gaT = self.ar("gaT", [16, 256], F32)
        xc0 = self.ar("xc0", [128, NC_], F32)
        xcb0 = self.ar("xcb0", [128, NC_], BF16)
        gx = self.ar("gx", [128, S], F32)
        Wg_ = [self.ar("Wg%d" % i, [128, 4, 128], BF16) for i in range(2)]
        assert self.ar_ptr <= self.kv_end, (self.ar_ptr, self.kv_end)
        self.ar_ptr = save
        xc_ = [xc0, self.ar("xc1", [128, NC_], F32)]
        xcb_ = [xcb0, self.ar("xcb1", [128, NC_], BF16)]
        A = self.ar("A", [128, NW], F32)
        I_ = self.ar("I", [128, NW], F32)
        M = [self.ar("M%d" % d, [128, NW], F32) for d in range(2)]
        PS7 = [0, 1, 2, 3, 4, 5, 6]
        modps, modk = self.ps[7], ("ps", 7)
        segs = [(0, L, None)] + [(L + b * 512, 512, b) for b in range(4)]
        cs = slice(0, 128)
        dblocks = [(o, min(512, NW - o)) for o in range(0, NW, 512)]
        Ak = [("A", o) for (o, _) in dblocks]
        Ik = [("I", o) for (o, _) in dblocks]

        def stageA(c):
            xc, xcb = xc_[c % 2], xcb_[c % 2]
            kxc, kxcb = ("xc", c % 2), ("xcb", c % 2)
            wxr, kxr = self.wnext(("xr", c))
            Wg, wgk = Wg_[c % 2], ("Wg", c % 2)
            for (gi, wsrc_) in ((0, self.w_rg), (1, self.w_ig)):
                T.dma("pool", "Wg%d_%d" % (c % 2, gi), [], [(wgk, gi)], lambda e: e.dma_start(
                    out=Wg[:, 2 * gi:2 * gi + 2, :],
                    in_=bass.AP(wsrc_.tensor, c * 128 * 128, [[128, 128], [16 * 128 * 128, 2], [1, 128]])))
            for (off, n, b) in segs:
                pst, psk = self.psnext(PS7)
                if b is None:
                    rhs = lambda k: self.hcT[:, k, :]
                    rk = [("hcT", 0), ("hcT", 1)]
                else:
                    rhs = lambda k: self.hT[:, k, b * 512:(b + 1) * 512]
                    rk = [("hT", b * 4 + i) for i in range(4)]

                def mm(e):
                    last = None
                    for k in range(KC):
                        last = e.matmul(pst[:, 0:n], wxr[:, k, cs], rhs(k), start=(k == 0), stop=(k == KC - 1))
                    return last
                T.op("pe", [kxr] + rk, [psk], mm)
                T.op("act", [psk], [("xr", off)], lambda e: e.activation(xr[:, off:off + n], pst[:, 0:n], AF.Identity))
            xrk = [("xr", o) for (o, _, _) in segs]
            w = lambda kk: self.pvec[:, PV_CW + kk * 16 + c:PV_CW + kk * 16 + c + 1]
            cbias = self.pvec[:, PV_CB + c:PV_CB + c + 1]
            for (off, n) in ((0, L), (L, S)):
                T.op("dve", xrk + ["pvec"], [kxc], lambda e: e.tensor_scalar(
                    xc[:, off:off + n], xr[:, off:off + n], w(1), cbias, ALU.mult, ALU.add))
                T.op("dve", xrk + [kxc, "pvec"], [kxc], lambda e: e.scalar_tensor_tensor(
                    xc[:, off + 1:off + n], xr[:, off:off + n - 1], w(0), xc[:, off + 1:off + n], ALU.mult, ALU.add))
                T.op("dve", xrk + [kxc, "pvec"], [kxc], lambda e: e.scalar_tensor_tensor(
                    xc[:, off:off + n - 1], xr[:, off + 1:off + n], w(2), xc[:, off:off + n - 1], ALU.mult, ALU.add))
                T.op("dve", xrk + [kxc, "pvec"], [kxc], lambda e: e.scalar_tensor_tensor(
                    xc[:, off:off + n - 2], xr[:, off + 2:off + n], w(3), xc[:, off:off + n - 2], ALU.mult, ALU.add))
            T.op("pool", [kxc], [kxc], lambda e: e.tensor_copy(xc[:, S + L:NC_], xc[:, 0:L]))
            T.op("act", [kxc], [kxcb], lambda e: e.activation(xcb[:, :], xc[:, :], AF.Identity))

        def stageG(c):
            wxg, kxg = self.wnext(("xg", c))
            for b in range(4):
                pst, psk = self.psnext(PS7)

                def mm(e):
                    last = None
                    for k in range(KC):
                        last = e.matmul(pst[:, :], wxg[:, k, cs], self.hT[:, k, b * 512:(b + 1) * 512],
                                        start=(k == 0), stop=(k == KC - 1))
                    return last
                T.op("pe", [kxg] + [("hT", b * 4 + i) for i in range(4)], [psk], mm)
                T.op("act", [psk], [("gx", b)], lambda e: e.activation(
                    gx[:, b * 512:(b + 1) * 512], pst[:, :], AF.Gelu_apprx_tanh))

        def stageB(c, d):
            xc, xcb = xc_[c % 2], xcb_[c % 2]
            kxc, kxcb = ("xc", c % 2), ("xcb", c % 2)
            Wg, wgk = Wg_[c % 2], ("Wg", c % 2)
            br = self.pvec[:, PV_BRG + d * 16 + c:PV_BRG + d * 16 + c + 1]
            bi = self.pvec[:, PV_BIG + d * 16 + c:PV_BIG + d * 16 + c + 1]
            Md, Mk = M[d], ("M", d)
            g0 = d * L
            for (o, n) in dblocks:
                for (gi, bias, dst, dk) in ((0, br, A, "A"), (1, bi, I_, "I")):
                    pst, psk = self.psnext(PS7)
                    T.op("pe", [(wgk, gi), kxcb], [psk], lambda e: e.matmul(
                        pst[:, 0:n], Wg[:, 2 * gi + d, :], xcb[:, g0 + o:g0 + o + n], start=True, stop=True))
                    T.op("act", [psk, "pvec"], [(dk, o)], lambda e: e.activation(
                        dst[:, o:o + n], pst[:, 0:n], AF.Sigmoid, bias=bias))
            T.op("act", Ak + ["cl"], [Mk], lambda e: e.activation(
                Md[:, :], A[:, :], AF.Exp, scale=self.cl[:, d, 1, c:c + 1]))
            T.op("act", Ak + ["cl"], Ak, lambda e: e.activation(
                A[:, :], A[:, :], AF.Exp, scale=self.cl[:, d, 0, c:c + 1]))
            T.op("dve", [Mk], [Mk], lambda e: e.tensor_scalar(Md[:, :], Md[:, :], 1.0, -1.0, ALU.min, ALU.mult))
            T.op("act", [Mk], [Mk], lambda e: e.activation(Md[:, :], Md[:, :], AF.Sqrt, scale=1.0, bias=1.0))
            T.op("dve", Ik + [kxc], Ik, lambda e: e.tensor_tensor(
                I_[:, :], I_[:, :], xc[:, g0:g0 + NW], ALU.mult))
            T.op("dve", Ik + [Mk], Ik, lambda e: e.tensor_tensor(I_[:, :], I_[:, :], Md[:, :], ALU.mult))
            if d == 0:
                T.op("dve", Ak + Ik, [Mk], lambda e: e.tensor_tensor_scan(
                    Md[:, :], A[:, :], I_[:, :], 0.0, ALU.mult, ALU.add))
            else:
                rv = lambda t: sb_view(t[:, NW - 1:NW], [[-1, NW]])
                T.op("dve", Ak + Ik, [Mk], lambda e: e.tensor_tensor_scan(
                    rv(Md), rv(A), rv(I_), 0.0, ALU.mult, ALU.add))

        def stageE(c):
            T.op("dve", [("M", 0), ("M", 1)], [("M", 0)], lambda e: e.tensor_tensor(
                M[0][:, L:L + S], M[0][:, L:L + S], M[1][:, 0:S], ALU.add))
            T.op("dve", [("M", 0)] + [("gx", b) for b in range(4)], [("M", 0)], lambda e: e.tensor_tensor(
                M[0][:, L:L + S], M[0][:, L:L + S], gx[:, :], ALU.mult))
            T.dma("pool", "ubst", [("M", 0)], [("uT_d", c)], lambda e: e.dma_start(
                out=self.uT_d[c * 128:(c + 1) * 128, :], in_=M[0][:, L:L + S]))

        stageA(0)
        for c in range(16):
            stageB(c, 0)
            stageG(c)
            if c % 2 == 1:
                pp = c // 2
                for q in range(4):
                    self.mod_step(modps, modk, 2 + (pp * 4 + q) // 8, (pp * 4 + q) % 8, 2)
            if c + 1 < 16:
                stageA(c + 1)
            stageB(c, 1)
            stageE(c)
        self.mod_finish(modps, modk, 2, 6)

    def p4_attn(self):
        T = self.T
        self.nr_alloc()
        hTblk = self.ar("hTblk", [128, KC, TB], BF16)
        uTblk = self.ar("uTblk", [128, KC, TB], BF16)
        attnT = self.ar("attnT", [128, 16, TB], BF16)
        mTblk = self.ar("mTblk", [128, KC, TB], BF16)
        QT = [self.ar("QT%d" % i, [128, 2, TB], BF16) for i in range(2)]
        NPT = 5
        PT = [self.ar("PT%d" % i, [128, TB], BF16) for i in range(NPT)]
        qr = [self.ar("qr%d" % i, [128, 256], BF16) for i in range(8)]
        at = [self.ar("at%d" % i, [128, 128], BF16) for i in range(8)]
        rden = self.ar("rden", [128, 8], F32)
        tA = [self.ar("tA%d" % i, [128, TB], F32) for i in range(2)]
        tB = [self.ar("tB%d" % i, [128, TB], F32) for i in range(2)]
        PS_S = [0, 1, 2]
        LA = 2
        isc = 1.0 / math.sqrt(128.0)
        pvap = lambda qt: self.ps[3 + qt // 2][:, (qt % 2) * 256:(qt % 2) * 256 + 129]
        pvk = lambda qt: ("ps", 3 + qt // 2)
        psq = lambda tt: self.ps[5][:, 0:256]
        psqk = lambda tt: ("ps", 5)
        pstb = [self.ps[6][:, 0:256].bitcast(BF16), self.ps[7][:, 0:256].bitcast(BF16)]
        qraw = [self.ar("qraw%d" % i, [128, 256], F32) for i in range(4)]
        st = {"npt": 0, "nat": 0, "ntr": 0}

        def trbuf():
            i = st["ntr"] % 2
            st["ntr"] += 1
            return pstb[i], ("ps", 6 + i)

        def qmm(tb, hp):
            wq, kwq = self.wnext(("q", tb, hp))
            for tt in range(4):
                def mm(e):
                    last = None
                    for k in range(KC):
                        last = e.matmul(psq(tt), hTblk[:, k, tt * 128:(tt + 1) * 128], wq[:, k, :],
                                        start=(k == 0), stop=(k == KC - 1))
                    return last
                T.op("pe", [kwq, "hTblk"], [psqk(tt)], mm)
                T.op("act", [psqk(tt)], [("qraw", tt)], lambda e: e.activation(qraw[tt][:, :], psq(tt), AF.Identity))
            return [self.nr_stats(qraw[tt][:, :], ("qraw", tt)) for tt in range(4)]

        def qapply(tb, hp, hds):
            for tt in range(4):
                qi = (hp % 2) * 4 + tt
                self.nr_apply(hds[tt], qraw[tt][:, :], ("qraw", tt), self.gq_b, "gq_b", tb * 4 + tt, qr[qi], ("qr", qi))

        def qtr(hp):
            qt_ = QT[hp % 2]
            for tt in range(4):
                qi = (hp % 2) * 4 + tt
                q_ = qr[qi]
                pb, pbk = trbuf()

                def tr(e):
                    e.transpose(pb[:, 0:128], q_[:, 0:128], self.identb[:, :])
                    return e.transpose(pb[:, 128:256], q_[:, 128:256], self.identb[:, :])
                T.op("pe", [("qr", qi), "identb"], [pbk], tr)
                T.op("dve", [pbk], [(("QT", hp % 2), tt)], lambda e: e.tensor_copy(
                    qt_[:, :, tt * 128:(tt + 1) * 128], pb[:, 0:256].rearrange("p (a b) -> p a b", a=2)))

        def tail_norm(h):
            items = []
            for qt in range(4):
                ai = st["nat"] % 8
                st["nat"] += 1
                a_, ka_ = at[ai], ("at", ai)
                rd, krd = rden[:, ai:ai + 1], ("rden", ai)
                po = pvap(qt)
                T.op("dve", [pvk(qt)], [krd], lambda e: e.reciprocal(rd, po[:, 128:129]))
                T.op("dve", [pvk(qt), krd], [ka_], lambda e: e.tensor_scalar(a_[:, :], po[:, 0:128], rd, None, ALU.mult))
                items.append((h, qt, a_, ka_))
            return items

        def tail_tr(items):
            for (h, qt, a_, ka_) in items:
                pb, pbk = trbuf()
                T.op("pe", [ka_, "identb"], [pbk], lambda e: e.transpose(pb[:, 0:128], a_[:, :], self.identb[:, :]))
                T.op("dve", [pbk], [("attnT", h)], lambda e: e.tensor_copy(
                    attnT[:, h, qt * 128:(qt + 1) * 128], pb[:, 0:128]))

        for tb in range(NTB):
            T.dma("sp", "hTblk", [("hT_d", tb)], ["hTblk"], lambda e: e.dma_start(
                out=hTblk[:, :, :], in_=self.hT_d[:, tb * TB:(tb + 1) * TB].rearrange("(k p) n -> p k n", p=128)))
            T.dma("sp", "uTblk", [("uT_d", c) for c in range(16)], ["uTblk"], lambda e: e.dma_start(
                out=uTblk[:, :, :], in_=self.uT_d[:, tb * TB:(tb + 1) * TB].rearrange("(k p) n -> p k n", p=128)))
            pending = None
            qapply(tb, 0, qmm(tb, 0))
            for hp in range(8):
                g = hp // 2
                hds = qmm(tb, hp + 1) if hp < 7 else None
                qtr(hp)
                if hds is not None:
                    qapply(tb, hp + 1, hds)
                qt_ = QT[hp % 2]
                kqts = [(("QT", hp % 2), tt) for tt in range(4)]
                for hh in range(2):
                    h = 2 * hp + hh
                    sc = {}
                    for step in range(NKEY + LA):
                        if step < NKEY:
                            kc = step
                            pss, pssk = self.psnext(PS_S)
                            T.op("pe", kqts + ["KT"], [pssk], lambda e: e.matmul(
                                pss[:, :], self.KT[:, g, kc * 128:(kc + 1) * 128], qt_[:, hh, :], start=True, stop=True))
                            pi = st["npt"] % NPT
                            st["npt"] += 1
                            T.op("act", [pssk], [("PT", pi)], lambda e: e.activation(PT[pi][:, :], pss[:, :], AF.Exp, scale=isc))
                            sc[kc] = pi
                        if step == LA and pending is not None:
                            tail_tr(pending)
                            pending = None
                        j = step - LA
                        if j >= 0:
                            pj = sc.pop(j)

                            def pv(e):
                                last = None
                                for qt in range(4):
                                    last = e.matmul(pvap(qt), PT[pj][:, qt * 128:(qt + 1) * 128],
                                                    self.Vaug[:, j, g, 0:129],
                                                    start=(j == 0 and qt % 2 == 0), stop=(j == NKEY - 1),
                                                    skip_group_check=True)
                                return last
                            T.op("pe", [("PT", pj), "V"], [("ps", 3), ("ps", 4)], pv)
                    pending = tail_norm(h)
            tail_tr(pending)
            akeys = [("attnT", h) for h in range(16)]
            for ccp in range(8):
                def gemm2(tag, act, akeys_):
                    wp, wk = self.wnext((tag, tb, ccp))
                    res = []
                    for cl_ in range(2):
                        cs = slice(cl_ * 128, (cl_ + 1) * 128)
                        pst, psk = self.psnext()

                        def mm(e):
                            last = None
                            for k in range(KC):
                                last = e.matmul(pst[:, :], wp[:, k, cs], act[:, k, :], start=(k == 0), stop=(k == KC - 1))
                            return last
                        T.op("pe", [wk] + akeys_, [psk], mm)
                        res.append((pst, psk))
                    return res
                r1 = gemm2("gla", hTblk, ["hTblk"])
                for cl_ in range(2):
                    T.op("act", [r1[cl_][1]], [("tA", cl_)], lambda e: e.activation(tA[cl_][:, :], r1[cl_][0][:, :], AF.Sigmoid))
                r2 = gemm2("oa", attnT, akeys)
                for cl_ in range(2):
                    T.op("dve", [("tA", cl_), r2[cl_][1]], [("tA", cl_)], lambda e: e.tensor_tensor(
                        tA[cl_][:, :], tA[cl_][:, :], r2[cl_][0][:, :], ALU.mult))
                r3 = gemm2("glr", hTblk, ["hTblk"])
                for cl_ in range(2):
                    T.op("act", [r3[cl_][1]], [("tB", cl_)], lambda e: e.activation(tB[cl_][:, :], r3[cl_][0][:, :], AF.Sigmoid))
                r4 = gemm2("or", uTblk, ["uTblk"])
                for cl_ in range(2):
                    cc = 2 * ccp + cl_
                    T.op("dve", [("tB", cl_), r4[cl_][1]], [("tB", cl_)], lambda e: e.tensor_tensor(
                        tB[cl_][:, :], tB[cl_][:, :], r4[cl_][0][:, :], ALU.mult))
                    T.op("pool", [("tA", cl_), ("tB", cl_)], [("mTblk", cc)], lambda e: e.tensor_tensor(
                        mTblk[:, cc, :], tA[cl_][:, :], tB[cl_][:, :], ALU.add))
            T.dma("sp", "mTst", [("mTblk", cc) for cc in range(16)], [("mT_d", tb)], lambda e: e.dma_start(
                out=self.mT_d[:, tb * TB:(tb + 1) * TB].rearrange("(k p) n -> p k n", p=128), in_=mTblk[:, :, :]))

    def p6_mlp(self):
        T = self.T
        self.nm_alloc()
        mTblk = self.ar("mTblk2", [128, KC, TB], BF16)
        h2T = mTblk
        x1 = self.ar("x1buf", [128, 4, D], F32)
        actT = self.ar("actT", [128, 64, TB], BF16)
        gaa = self.ar("gaa", [128, D], F32)
        gaf = self.ar("gaf", [128, D], F32)
        gfb = self.ar("gfb", [128, D], F32)
        tmp = [self.ar("tmp%d" % i, [128, 512], F32) for i in range(3)]
        obuf = self.nm_xn
        T.dma("sp", "gaa", ["ga_d"], ["gaa"], lambda e: e.dma_start(
            out=gaa[:, :], in_=bass.AP(self.ga_d.tensor, 0, [[0, 128], [1, D]])))
        T.dma("sp", "gaf", ["ga_d"], ["gaf"], lambda e: e.dma_start(
            out=gaf[:, :], in_=bass.AP(self.ga_d.tensor, D, [[0, 128], [1, D]])))
        T.dma("sp", "gfb", [], ["gfb"], lambda e: e.dma_start(
            out=gfb[:, :], in_=bass.AP(self.gf_d.tensor, 0, [[0, 128], [1, D]])))
        nt = 0
        for tb in range(NTB):
            T.dma("sp", "mTld", [("mT_d", tb)], ["mTblk2"] + [("h2T", tt) for tt in range(4)], lambda e: e.dma_start(
                out=mTblk[:, :, :], in_=self.mT_d[:, tb * TB:(tb + 1) * TB].rearrange("(k p) n -> p k n", p=128)))
            x1k = [("x1", tt) for tt in range(4)]
            T.dma("sp", "x1ld", [], x1k, lambda e: e.dma_start(
                out=x1[:, :, :], in_=self.x_d[tb * TB:(tb + 1) * TB, :].rearrange("(t p) n -> p t n", p=128)))
            for np_ in range(8):
                wo, kwo = self.wnext(("wout", tb, np_))
                cols = slice(np_ * 256, (np_ + 1) * 256)
                for tt in range(4):
                    pst, psk = self.psnext()

                    def mm(e):
                        last = None
                        for k in range(KC):
                            last = e.matmul(pst[:, 0:256], mTblk[:, k, tt * 128:(tt + 1) * 128], wo[:, k, :],
                                            start=(k == 0), stop=(k == KC - 1))
                        return last
                    T.op("pe", [kwo, "mTblk2"], [psk], mm)
                    t_ = tmp[nt % 3]
                    kt_ = ("tmp", nt % 3)
                    nt += 1
                    T.op("dve", [psk, "gaa"], [kt_], lambda e: e.tensor_tensor(t_[:, 0:256], pst[:, 0:256], gaa[:, cols], ALU.mult))
                    T.op("pool", [kt_, ("x1", tt)], [("x1", tt)], lambda e: e.tensor_tensor(
                        x1[:, tt, cols], x1[:, tt, cols], t_[:, 0:256], ALU.add))
            if self.debug:
                T.dma("sp", "dbg2", x1k, [("dbg2", tb)], lambda e: e.dma_start(
                    out=self.dbg2_d[tb * TB:(tb + 1) * TB, :].rearrange("(t p) n -> p t n", p=128), in_=x1[:, :, :]))
            for tt in range(4):
                self.norm_mod_T(x1[:, tt, :], ("x1", tt), 4, lambda j: h2T[:, j, tt * 128:(tt + 1) * 128], [("h2T", tt), "mTblk2"])
            h2k = [("h2T", tt) for tt in range(4)]
            for fp_ in range(32):
                wu, kwu = self.wnext(("wup", tb, fp_))
                for cl_ in range(2):
                    fc = 2 * fp_ + cl_
                    cs = slice(cl_ * 128, (cl_ + 1) * 128)
                    pst, psk = self.psnext()

                    def mm(e):
                        last = None
                        for k in range(KC):
                            last = e.matmul(pst[:, :], wu[:, k, cs], h2T[:, k, :], start=(k == 0), stop=(k == KC - 1))
                        return last
                    T.op("pe", [kwu] + h2k, [psk], mm)
                    t_ = tmp[nt % 3]
                    kt_ = ("tmp", nt % 3)
                    nt += 1
                    T.op("act", [psk], [kt_], lambda e: e.activation(t_[:, :], pst[:, :], AF.Relu))
                    T.op("pool", [kt_], [("actT", fc)], lambda e: e.tensor_tensor(actT[:, fc, :], t_[:, :], t_[:, :], ALU.mult))
            for cb in range(4):
                banks = [0, 1, 2, 3] if cb % 2 == 0 else [4, 5, 6, 7]
                cols = slice(cb * 512, (cb + 1) * 512)
                for fp_ in range(8):
                    wd, kwd = self.wnext(("wdn", tb, cb, fp_))

                    def mm(e):
                        last = None
                        for tt in range(4):
                            for j in range(8):
                                last = e.matmul(self.ps[banks[tt]][:, :], actT[:, fp_ * 8 + j, tt * 128:(tt + 1) * 128],
                                                wd[:, j, :], start=(fp_ == 0 and j == 0), stop=(fp_ == 7 and j == 7))
                        return last
                    T.op("pe", [kwd] + [("actT", fp_ * 8 + j) for j in range(8)], [("ps", b) for b in banks], mm)
                for tt in range(4):
                    t_ = tmp[nt % 3]
                    kt_ = ("tmp", nt % 3)
                    nt += 1
                    T.op("dve", [("ps", banks[tt]), "gaf"], [kt_], lambda e: e.tensor_tensor(
                        t_[:, :], self.ps[banks[tt]][:, :], gaf[:, cols], ALU.mult))
                    T.op("pool", [kt_, ("x1", tt)], [("x1", tt)], lambda e: e.tensor_tensor(
                        x1[:, tt, cols], x1[:, tt, cols], t_[:, :], ALU.add))
            for tt in range(4):
                i = self.nm_i
                self.nm_i += 1
                ss = self.nm_st[:, (i % 8) * 2:(i % 8) * 2 + 1]
                rs = self.nm_st[:, (i % 8) * 2 + 1:(i % 8) * 2 + 2]
                kss, krs = ("nm_ss", i % 8), ("nm_rs", i % 8)
                T.op("act", [("x1", tt)], ["nm_junk", kss], lambda e: e.activation(
                    self.nm_junk[:, :], x1[:, tt, :], AF.Square, accum_out=ss))
                T.op("act", [kss, "epsb"], [kss], lambda e: e.activation(ss, ss, AF.Sqrt, scale=1.0 / D, bias=self.epsb[:, 0:1]))
                T.op("dve", [kss], [krs], lambda e: e.reciprocal(rs, ss))
                T.op("dve", [("x1", tt), krs, "gfb"], ["nm_xn"], lambda e: e.scalar_tensor_tensor(
                    obuf[:, :], x1[:, tt, :], rs, gfb[:, :], ALU.mult, ALU.mult))
                r0 = tb * TB + tt * 128
                T.dma("sp", "ost", ["nm_xn"], [("out", tb, tt)], lambda e: e.dma_start(
                    out=self.out_d[r0:r0 + 128, :], in_=obuf[:, :]))

    def finish(self):
        T = self.T
        if self.debug:
            T.barrier()
            self.ar_ptr = self.ar_end - 16384 - 64
            dbg = self.ar("dbgsb", [128, 4096], F32)
            T.op("dve", [], ["dbg"], lambda e: e.memset(dbg[:, :], 0.0))
            T.op("dve", ["cosT", "dbg"], ["dbg"], lambda e: e.tensor_copy(dbg[:, 0:1024], self.cosT[:, :, :].rearrange("p a b -> p (a b)")))
            T.op("dve", ["sinT", "dbg"], ["dbg"], lambda e: e.tensor_copy(dbg[:, 1024:2048], self.sinT[:, :, :].rearrange("p a b -> p (a b)")))
            T.op("dve", ["modL", "dbg"], ["dbg"], lambda e: e.tensor_copy(dbg[:, 2048:2144], self.modL[:, :]))
            T.op("dve", ["modC", "dbg"], ["dbg"], lambda e: e.tensor_copy(dbg[:, 2144:2240], self.modC[:, :]))
            T.op("dve", ["cl", "dbg"], ["dbg"], lambda e: e.tensor_copy(dbg[:, 2240:2304], self.cl[:, :, :, :].rearrange("p a b c -> p (a b c)")))
            if self.stage == 3:
                T.op("dve", ["dbg"], ["dbg"], lambda e: e.tensor_copy(dbg[:, 2304:2304 + 1152], self.KT[:, 0, 0:1152]))
                T.op("dve", ["dbg"], ["dbg"], lambda e: e.tensor_copy(dbg[:, 3456:3456 + 520], self.Vaug[:, 5, :, :].rearrange("p a b -> p (a b)")))
            T.dma("sp", "dbg", ["dbg"], ["dbg_d"], lambda e: e.dma_start(out=self.dbg_d, in_=dbg[:, :]))
        T.final_wait("sp")


def prep_inputs(inp, b):
    f = lambda a: np.ascontiguousarray(a, dtype=np.float32)
    fm = lambda v: f(np.asarray(v).reshape(-1, 16, 128).transpose(2, 0, 1).reshape(128, -1))
    pv = np.concatenate([
        fm(inp["c"][b][None]), fm(inp["c_ctx"][None]), fm(inp["g_mix"][0][None]), fm(inp["g_mlp"][0][None]),
        fm(inp["conv_w"][0]), fm(inp["conv_b"][0][None]), fm(inp["b_rg"][0]), fm(inp["b_ig"][0]),
        fm(inp["lru_lambda"][0]), fm(inp["b_mod"][0].reshape(6, D))], axis=1)
    assert pv.shape == (128, NPV), pv.shape
    return {
        "x": f(inp["x"][b]), "ctx": f(inp["ctx"][b]), "pvec": f(pv),
        "w_mod": f(inp["w_mod"][0]), "w_in": f(inp["w_in"][0]),
        "q_gain": f(inp["q_gain"][0][None]), "k_gain": f(inp["k_gain"][0][None]),
        "w_rg": f(inp["w_rg"][0].reshape(2 * 16 * 128, 128)), "w_ig": f(inp["w_ig"][0].reshape(2 * 16 * 128, 128)),
        "w_o_attn": f(inp["w_o_attn"][0]), "w_o_rnn": f(inp["w_o_rnn"][0]), "w_out": f(inp["w_out"][0]),
        "w_up": f(inp["w_up"][0]), "w_down": f(inp["w_down"][0]), "g_final": f(inp["g_final"][None]),
    }


def run(inputs, stage=99, debug=False, trace=False, ncores=8):
    bld = Builder(stage=stage, debug=debug)
    in_maps = [prep_inputs(inputs, b) for b in range(ncores)]
    res = run_bass_kernel_spmd(bld.nc, in_maps, core_ids=list(range(ncores)), trace=trace)
    return res


def kernel(**inputs):
    inputs = {k: np.asarray(v) for k, v in inputs.items()}
    res = run(inputs)
    return np.stack([res.results[b]["out"] for b in range(8)], axis=0).astype(np.float32)
```

```python
import math
import numpy as np
import concourse.bass as bass
import concourse.mybir as mybir
from concourse.bass_utils import run_bass_kernel_spmd

F32 = mybir.dt.float32
BF16 = mybir.dt.bfloat16
I32 = mybir.dt.int32
AF = mybir.ActivationFunctionType
ALU = mybir.AluOpType
AX = mybir.AxisListType

D = 2048
S = 2048
L = 256
NT = S // 128
NCT = L // 128
KC = D // 128
NKEY = (S + L) // 128
N_IN = 11264
DFF = 8192
EPS = 1e-6
TB = 512
NTB = S // TB

PV_C, PV_CC, PV_GMIX, PV_GMLP, PV_CW, PV_CB, PV_BRG, PV_BIG, PV_LAM, PV_BMOD = (
    0, 16, 32, 48, 64, 128, 144, 176, 208, 240)
NPV = 240 + 96


class Tracker:
    def __init__(self, nc):
        self.nc = nc
        self.engs = {"pe": nc.tensor, "act": nc.scalar, "dve": nc.vector,
                     "pool": nc.gpsimd, "sp": nc.sync}
        self.esem = {k: nc.alloc_semaphore("es_" + k) for k in self.engs}
        self.ecnt = {k: 0 for k in self.engs}
        self.waited = {k: {} for k in self.engs}
        self.last_w = {}
        self.readers = {}
        self.dsem = {}
        self.dcnt = {}

    def _deps(self, eng, reads, writes, ww_ok=False):
        deps = []
        for r in reads:
            w = self.last_w.get(r)
            if w is not None:
                deps.append(w)
        for wkey in writes:
            w = self.last_w.get(wkey)
            if w is not None and not (w[2] == eng and (eng == "pe" or ww_ok)):
                deps.append(w)
            for rd in self.readers.get(wkey, ()):
                deps.append(rd)
        return deps

    def _wait(self, eng, deps):
        e = self.engs[eng]
        wd = self.waited[eng]
        need = {}
        for (sem, val, _) in deps:
            k = id(sem)
            if val > wd.get(k, 0) and val > need.get(k, (None, 0))[1]:
                need[k] = (sem, val)
        for k, (sem, val) in need.items():
            e.wait_ge(sem, val)
            wd[k] = val

    def _commit(self, token, reads, writes):
        for r in reads:
            self.readers.setdefault(r, []).append(token)
        for w in writes:
            self.last_w[w] = token
            self.readers[w] = []

    def op(self, eng, reads, writes, fn, ww_ok=False):
        self._wait(eng, self._deps(eng, reads, writes, ww_ok))
        inst = fn(self.engs[eng])
        self.ecnt[eng] += 1
        inst.then_inc(self.esem[eng], 1)
        self._commit((self.esem[eng], self.ecnt[eng], eng), reads, writes)

    def dma(self, q, slot, reads, writes, fn):
        if slot not in self.dsem:
            self.dsem[slot] = self.nc.alloc_semaphore("ds_%d" % len(self.dsem))
            self.dcnt[slot] = 0
        sem = self.dsem[slot]
        deps = self._deps("dma:" + slot, reads, writes)
        if self.dcnt[slot] > 0:
            deps.append((sem, self.dcnt[slot], None))
        self._wait(q, deps)
        inst = fn(self.engs[q])
        self.dcnt[slot] += 16
        inst.then_inc(sem, 16)
        self._commit((sem, self.dcnt[slot], "dma:" + slot), reads, writes)

    def barrier(self):
        toks = [(self.esem[k], self.ecnt[k], k) for k in self.engs if self.ecnt[k] > 0]
        toks += [(self.dsem[s], self.dcnt[s], None) for s in self.dsem]
        for eng in self.engs:
            self._wait(eng, toks)
        self.last_w = {}
        self.readers = {}

    def final_wait(self, eng):
        deps = [(self.esem[k], self.ecnt[k], k) for k in self.engs if self.ecnt[k] > 0 and k != eng]
        deps += [(self.dsem[s], self.dcnt[s], None) for s in self.dsem]
        self._wait(eng, deps)


def sb_view(ap, dims):
    return bass.AP(ap.tensor, ap.offset, [list(ap.ap[0])] + [list(d) for d in dims])


class Builder:
    def __init__(self, stage=99, debug=False):
        self.stage = stage
        self.debug = debug
        self.nc = bass.Bass("TRN2", target_bir_lowering=False)
        self.T = Tracker(self.nc)
        self.build()

    def dram_in(self, name, shape, dt=F32):
        return self.nc.dram_tensor(name, list(shape), dt, kind="ExternalInput").ap()

    def scratch(self, name, shape, dt):
        kind = "ExternalOutput" if self.debug else "Internal"
        return self.nc.dram_tensor(name, list(shape), dt, kind=kind).ap()

    def sb(self, name, shape, dt):
        return self.nc.alloc_sbuf_tensor("s_" + name, list(shape), dt)

    def sb_at(self, name, shape, dt, off):
        return self.nc.alloc_sbuf_tensor_at("s_" + name, list(shape), dt, offset=off)

    def plan(self, tag, src):
        self.wplan.append((tag, src))

    def _issue_panel(self, i):
        tag, src = self.wplan[i]
        slot = i % self.NSLOT
        kc, n = src.shape[1], src.shape[2]
        dst = self.wslots[slot][:, 0:kc * n].rearrange("p (k n) -> p k n", k=kc)
        self.T.dma("pool", "w%d" % slot, [], [("w", slot)],
                   lambda e: e.dma_start(out=dst, in_=src))

    def wnext(self, tag):
        i = self.wpos
        assert self.wplan[i][0] == tag, (self.wplan[i][0], tag)
        while self.wissued < min(len(self.wplan), i + self.NSLOT):
            self._issue_panel(self.wissued)
            self.wissued += 1
        self.wpos += 1
        slot = i % self.NSLOT
        src = self.wplan[i][1]
        kc, n = src.shape[1], src.shape[2]
        return (self.wslots[slot][:, 0:kc * n].rearrange("p (k n) -> p k n", k=kc), ("w", slot))

    def wsrc(self, w, r0, nrows, c0, ncols):
        return w[r0:r0 + nrows, c0:c0 + ncols].rearrange("(k p) n -> p k n", p=128)

    def psnext(self, pool=None):
        pool = pool or list(range(8))
        i = self.pscur.get(tuple(pool), 0)
        self.pscur[tuple(pool)] = i + 1
        b = pool[i % len(pool)]
        return self.ps[b], ("ps", b)


    def ar(self, name, shape, dt):
        nbytes = int(np.prod(shape[1:])) * (4 if dt in (F32, I32) else 2)
        nbytes = (nbytes + 63) // 64 * 64
        off = self.ar_ptr
        self.ar_ptr += nbytes
        assert self.ar_ptr <= self.ar_end, (name, self.ar_ptr, self.ar_end)
        self.ar_id += 1
        return self.nc.alloc_sbuf_tensor_at("a%d_%s" % (self.ar_id, name), list(shape), dt, offset=off)

    def ar_mark(self):
        return self.ar_ptr

    def ar_release(self, mark):
        self.T.barrier()
        self.ar_ptr = mark

    def build(self):
        nc, T = self.nc, self.T
        self.x_d = self.dram_in("x", [S, D])
        self.ctx_d = self.dram_in("ctx", [L, D])
        self.pvec_d = self.dram_in("pvec", [128, NPV])
        self.w_mod = self.dram_in("w_mod", [D, 6 * D])
        self.w_in = self.dram_in("w_in", [D, N_IN])
        self.qg_d = self.dram_in("q_gain", [1, 128])
        self.kg_d = self.dram_in("k_gain", [1, 128])
        self.w_rg = self.dram_in("w_rg", [2 * 16 * 128, 128])
        self.w_ig = self.dram_in("w_ig", [2 * 16 * 128, 128])
        self.w_oa = self.dram_in("w_o_attn", [D, D])
        self.w_or = self.dram_in("w_o_rnn", [D, D])
        self.w_out = self.dram_in("w_out", [D, D])
        self.w_up = self.dram_in("w_up", [D, DFF])
        self.w_down = self.dram_in("w_down", [DFF, D])
        self.gf_d = self.dram_in("g_final", [1, D])
        self.out_d = nc.dram_tensor("out", [S, D], F32, kind="ExternalOutput").ap()
        self.hT_d = self.scratch("hT_d", [D, S], BF16)
        self.uT_d = self.scratch("uT_d", [D, S], BF16)
        self.mT_d = self.scratch("mT_d", [D, S], BF16)
        self.ga_d = self.scratch("ga_d", [2, D], F32)
        if self.debug:
            self.dbg_d = nc.dram_tensor("dbg", [128, 4096], F32, kind="ExternalOutput").ap()
            self.dbg2_d = nc.dram_tensor("dbg2", [S, D], F32, kind="ExternalOutput").ap()

        self.NSLOT = 4
        self.wslots = [self.sb("wslot%d" % i, [128, 4096], BF16) for i in range(self.NSLOT)]
        self.wplan, self.wpos, self.wissued = [], 0, 0
        self.pvec = self.sb("pvec", [128, NPV], F32)
        self.identf = self.sb("identf", [128, 128], F32)
        self.identb = self.sb("identb", [128, 128], BF16)
        self.cosT = self.sb("cosT", [128, NT, 64], F32)
        self.sinT = self.sb("sinT", [128, NT, 64], F32)
        self.gq_b = self.sb("gq_b", [128, 128], F32)
        self.gk_b = self.sb("gk_b", [128, 128], F32)
        self.modL = self.sb("modL", [128, 96], F32)
        self.modC = self.sb("modC", [128, 96], F32)
        self.AB = self.sb("AB", [128, 6, 16], F32)
        self.cl = self.sb("cl", [128, 2, 2, 16], F32)
        self.epsb = self.sb("epsb", [128, 1], F32)
        self.silu = self.sb("silu", [128, 16, 2], BF16)
        self.nm_st = self.sb("nm_st", [128, 16], F32)
        self.nr_st = self.sb("nr_st", [128, 16], F32)
        self.ps = [nc.alloc_psum_tensor("ps%d" % i, [128, 512], F32) for i in range(8)]
        self.pscur = {}
        self.ar_ptr = (nc.sbuf_base + 63) // 64 * 64
        self.ar_end = nc.sbuf_top
        self.ar_id = 0
        self.nm_i = 0
        self.nr_i = 0

        self.plan_all()
        m0 = self.ar_mark()
        self.setup()
        self.ar_release(m0)
        self.KT = self.ar("KT", [128, 4, S + L], BF16)
        self.Vaug = self.ar("Vaug", [128, NKEY, 4, 130], BF16)
        m1 = self.ar_mark()
        self.hT = self.ar("hT", [128, KC, S], BF16)
        self.hcT = self.ar("hcT", [128, KC, L], BF16)
        m2 = self.ar_mark()
        self.nm_alloc(2)
        self.p0_mod(0, 2)
        self.p1_norm()
        self.kv_base, self.kv_end = m0, m1
        if self.stage >= 2:
            self.ar_release(m2)
            self.p3_rnn()
        if self.stage >= 3:
            self.ar_release(m2)
            self.p2_kv()
        if self.stage >= 4:
            self.ar_release(m1)
            self.p4_attn()
        if self.stage >= 5:
            self.ar_release(m0)
            self.p6_mlp()
        self.finish()

    def plan_all(self):
        W = self.wsrc
        for m in range(2):
            for pp in range(8):
                self.plan(("mod", m, pp), W(self.w_mod, 0, D, m * D + pp * 256, 256))
        if self.stage >= 2:
            self.plan(("xr", 0), W(self.w_in, 0, D, 3072, 128))
            for c in range(16):
                if c + 1 < 16:
                    self.plan(("xr", c + 1), W(self.w_in, 0, D, 3072 + (c + 1) * 128, 128))
                self.plan(("xg", c), W(self.w_in, 0, D, 5120 + c * 128, 128))
                if c % 2 == 1:
                    pp = c // 2
                    for q in range(4):
                        m = 2 + (pp * 4 + q) // 8
                        mp = (pp * 4 + q) % 8
                        self.plan(("mod", m, mp), W(self.w_mod, 0, D, m * D + mp * 256, 256))
        if self.stage >= 3:
            for i in range(4):
                self.plan(("kv", i), W(self.w_in, 0, D, 2048 + i * 256, 256))
        if self.stage >= 4:
            for tb in range(NTB):
                for hp in range(8):
                    self.plan(("q", tb, hp), W(self.w_in, 0, D, hp * 256, 256))
                for ccp in range(8):
                    self.plan(("gla", tb, ccp), W(self.w_in, 0, D, 7168 + ccp * 256, 256))
                    self.plan(("oa", tb, ccp), W(self.w_oa, 0, D, ccp * 256, 256))
                    self.plan(("glr", tb, ccp), W(self.w_in, 0, D, 9216 + ccp * 256, 256))
                    self.plan(("or", tb, ccp), W(self.w_or, 0, D, ccp * 256, 256))
        if self.stage >= 5:
            for tb in range(NTB):
                for np_ in range(8):
                    self.plan(("wout", tb, np_), W(self.w_out, 0, D, np_ * 256, 256))
                for fp_ in range(32):
                    self.plan(("wup", tb, fp_), W(self.w_up, 0, D, fp_ * 256, 256))
                for cb in range(4):
                    for fp_ in range(8):
                        self.plan(("wdn", tb, cb, fp_), W(self.w_down, fp_ * 1024, 1024, cb * 512, 512))

    def setup(self):
        nc, T = self.nc, self.T
        T.dma("sp", "pvec", [], ["pvec"], lambda e: e.dma_start(out=self.pvec[:, :], in_=self.pvec_d))
        T.dma("sp", "gq", [], ["gq_b"], lambda e: e.dma_start(
            out=self.gq_b[:, :], in_=bass.AP(self.qg_d.tensor, 0, [[0, 128], [1, 128]])))
        T.dma("sp", "gk", [], ["gk_b"], lambda e: e.dma_start(
            out=self.gk_b[:, :], in_=bass.AP(self.kg_d.tensor, 0, [[0, 128], [1, 128]])))
        T.op("dve", [], ["modL"], lambda e: e.memset(self.modL[:, :], 0.0))
        T.op("dve", [], ["modC"], lambda e: e.memset(self.modC[:, :], 0.0))
        T.op("dve", [], ["epsb"], lambda e: e.memset(self.epsb[:, :], EPS))
        it = self.ar("iota_i", [128, 128], I32)
        T.op("pool", [], ["iota_i"], lambda e: e.iota(it[:, :], [[1, 128]], base=0, channel_multiplier=-1))
        T.op("dve", ["iota_i"], ["identf"], lambda e: e.tensor_scalar(
            self.identf[:, :], it[:, :], 0, None, ALU.is_equal))
        T.op("dve", ["identf"], ["identb"], lambda e: e.tensor_copy(self.identb[:, :], self.identf[:, :]))
        rowi = self.ar("rowi", [128, NT], I32)
        coli = self.ar("coli", [128, NT], I32)
        fi = self.ar("fi", [128, 32], I32)
        T.op("pool", [], ["rowi"], lambda e: e.iota(rowi[0:64, :], [[2, NT]], base=0, channel_multiplier=0))
        T.op("pool", [], ["rowi"], lambda e: e.iota(rowi[64:128, :], [[2, NT]], base=1, channel_multiplier=0))
        T.op("pool", [], ["coli"], lambda e: e.iota(coli[0:64, :], [[0, NT]], base=0, channel_multiplier=1))
        T.op("pool", [], ["coli"], lambda e: e.iota(coli[64:128, :], [[0, NT]], base=0, channel_multiplier=1))
        T.op("pool", [], ["fi"], lambda e: e.iota(fi[:, :], [[1, 32]], base=0, channel_multiplier=0))
        rowf = self.ar("rowf", [128, NT], F32)
        colf = self.ar("colf", [128, NT], F32)
        ff = self.ar("ff", [128, 32], F32)
        T.op("dve", ["rowi"], ["rowf"], lambda e: e.tensor_copy(rowf[:, :], rowi[:, :]))
        T.op("dve", ["coli"], ["colf"], lambda e: e.tensor_copy(colf[:, :], coli[:, :]))
        T.op("dve", ["fi"], ["ff"], lambda e: e.tensor_copy(ff[:, :], fi[:, :]))
        T.op("act", ["ff"], ["ff"], lambda e: e.activation(ff[:, :], ff[:, :], AF.Exp,
                                                            scale=-math.log(10000.0) / 32.0))
        ang = self.ar("ang", [128, NT, 64], F32)
        kf = self.ar("kf", [128, NT, 64], F32)
        ki = self.ar("ki", [128, NT, 64], I32)
        msk = self.ar("msk", [128, NT, 64], F32)
        ffb = sb_view(ff[:, :], [[0, NT], [1, 32]])
        T.op("dve", ["rowf", "ff"], ["ang"], lambda e: e.tensor_tensor(
            ang[:, :, 0:32], sb_view(rowf[:, :], [[1, NT], [0, 32]]), ffb, ALU.mult))
        T.op("dve", ["colf", "ff"], ["ang"], lambda e: e.tensor_tensor(
            ang[:, :, 32:64], sb_view(colf[:, :], [[1, NT], [0, 32]]), ffb, ALU.mult))
        TWO_PI = 2.0 * math.pi
        for (dst, dn, shift) in ((self.sinT, "sinT", 0.0), (self.cosT, "cosT", math.pi / 2.0)):
            T.op("dve", ["ang"], ["kf"], lambda e: e.tensor_scalar(
                kf[:, :, :], ang[:, :, :], shift, 1.0 / TWO_PI, ALU.add, ALU.mult))
            T.op("dve", ["kf"], ["ki"], lambda e: e.tensor_copy(ki[:, :, :], kf[:, :, :]))
            T.op("dve", ["ki"], ["kf"], lambda e: e.tensor_copy(kf[:, :, :], ki[:, :, :]))
            T.op("dve", ["kf"], ["kf"], lambda e: e.tensor_scalar(
                kf[:, :, :], kf[:, :, :], -TWO_PI, shift, ALU.mult, ALU.add))
            T.op("dve", ["kf", "ang"], ["kf"], lambda e: e.tensor_tensor(
                kf[:, :, :], kf[:, :, :], ang[:, :, :], ALU.add))
            T.op("dve", ["kf"], ["msk"], lambda e: e.tensor_scalar(
                msk[:, :, :], kf[:, :, :], math.pi, -TWO_PI, ALU.is_gt, ALU.mult))
            T.op("dve", ["kf", "msk"], ["kf"], lambda e: e.tensor_tensor(
                kf[:, :, :], kf[:, :, :], msk[:, :, :], ALU.add))
            T.op("dve", ["kf"], ["msk"], lambda e: e.tensor_scalar(
                msk[:, :, :], kf[:, :, :], -math.pi, TWO_PI, ALU.is_lt, ALU.mult))
            T.op("dve", ["kf", "msk"], ["kf"], lambda e: e.tensor_tensor(
                kf[:, :, :], kf[:, :, :], msk[:, :, :], ALU.add))
            T.op("dve", ["kf"], ["kf"], lambda e: e.tensor_scalar(
                kf[:, :, :], kf[:, :, :], math.pi, -math.pi, ALU.min, ALU.max))
            T.op("act", ["kf"], [dn], lambda e: e.activation(dst[:, :, :], kf[:, :, :], AF.Sin))
        lam = self.pvec[:, PV_LAM:PV_LAM + 32]
        sp_ = self.ar("sp_", [128, 32], F32)
        T.op("act", ["pvec"], ["sp_"], lambda e: e.activation(sp_[:, :], lam, AF.Exp, scale=-1.0))
        T.op("act", ["sp_"], ["sp_"], lambda e: e.activation(sp_[:, :], sp_[:, :], AF.Ln, bias=1.0))
        for d in range(2):
            T.op("dve", ["sp_"], ["cl"], lambda e: e.tensor_scalar(
                self.cl[:, d, 0, :], sp_[:, d * 16:(d + 1) * 16], -8.0, None, ALU.mult))
            T.op("dve", ["sp_"], ["cl"], lambda e: e.tensor_scalar(
                self.cl[:, d, 1, :], sp_[:, d * 16:(d + 1) * 16], -16.0, None, ALU.mult))
        T.op("act", ["pvec"], ["silu"], lambda e: e.activation(
            self.silu[:, :, 0], self.pvec[:, PV_C:PV_C + 16], AF.Silu))
        T.op("act", ["pvec"], ["silu"], lambda e: e.activation(
            self.silu[:, :, 1], self.pvec[:, PV_CC:PV_CC + 16], AF.Silu))

    def mod_step(self, pst, psk, m, pp, mbase):
        T = self.T
        wp, wk = self.wnext(("mod", m, pp))

        def mm(e):
            last = None
            for cl_ in range(2):
                col = ((m - mbase) * 16 + pp * 2 + cl_) * 2
                for k in range(KC):
                    last = e.matmul(pst[:, col:col + 2], wp[:, k, cl_ * 128:(cl_ + 1) * 128],
                                    self.silu[:, k, :], start=(k == 0), stop=(k == KC - 1))
            return last
        T.op("pe", [wk, "silu"], [psk], mm)

    def mod_finish(self, pst, psk, m0, m1):
        T = self.T
        nm = m1 - m0
        pv = pst[:, 0:nm * 32].rearrange("p (c t) -> p c t", t=2)
        bm = self.pvec[:, PV_BMOD + m0 * 16:PV_BMOD + m1 * 16]
        T.op("dve", [psk, "pvec"], ["modL"], lambda e: e.tensor_tensor(
            self.modL[:, m0 * 16:m1 * 16], pv[:, :, 0], bm, ALU.add))
        T.op("dve", [psk, "pvec"], ["modC"], lambda e: e.tensor_tensor(
            self.modC[:, m0 * 16:m1 * 16], pv[:, :, 1], bm, ALU.add))
        if m0 == 0:
            gm = self.pvec[:, PV_GMIX:PV_GMIX + 16]
            for (src, sn, ia) in ((self.modL, "modL", 0), (self.modC, "modC", 2)):
                T.op("dve", [sn, "pvec"], ["AB"], lambda e: e.scalar_tensor_tensor(
                    self.AB[:, ia, :], src[:, 16:32], 1.0, gm, ALU.add, ALU.mult))
                T.op("dve", [sn], ["AB"], lambda e: e.tensor_copy(self.AB[:, ia + 1, :], src[:, 0:16]))
        else:
            gm = self.pvec[:, PV_GMLP:PV_GMLP + 16]
            T.op("dve", ["modL", "pvec"], ["AB"], lambda e: e.scalar_tensor_tensor(
                self.AB[:, 4, :], self.modL[:, 64:80], 1.0, gm, ALU.add, ALU.mult))
            T.op("dve", ["modL"], ["AB"], lambda e: e.tensor_copy(self.AB[:, 5, :], self.modL[:, 48:64]))
            gaT = self.gaT
            for (gi, c0) in ((0, 32), (1, 80)):
                pt, pk = self.psnext([0, 1, 2, 3, 4, 5, 6])
                T.op("pe", ["modL", "identf"], [pk], lambda e: e.transpose(
                    pt[0:16, 0:128], self.modL[:, c0:c0 + 16], self.identf[:, :]))
                T.op("dve", [pk], ["gaT"], lambda e: e.tensor_copy(gaT[:, gi * 128:(gi + 1) * 128], pt[0:16, 0:128]))
            for gi in range(2):
                T.dma("sp", "gast", ["gaT"], ["ga_d"], lambda e: e.dma_start(
                    out=self.ga_d[gi:gi + 1, :].rearrange("o (j p) -> (o j) p", p=128),
                    in_=gaT[:, gi * 128:(gi + 1) * 128]))

    def p0_mod(self, m0, m1):
        pst, psk = self.psnext()
        for m in range(m0, m1):
            for pp in range(8):
                self.mod_step(pst, psk, m, pp, m0)
        self.mod_finish(pst, psk, m0, m1)

    def nm_alloc(self, nxn=1):
        self.nm_junk = self.ar("nm_junk", [128, D], BF16)
        self.nm_xns = [self.ar("nm_xn%d" % i, [128, D], F32) for i in range(nxn)]
        self.nm_xn = self.nm_xns[0]

    def norm_mod_T(self, xt, xkey, ia, dst_fn, dkeys):
        T = self.T
        i = self.nm_i
        self.nm_i += 1
        xi = i % len(self.nm_xns)
        junk, xn, st = self.nm_junk, self.nm_xns[xi], self.nm_st
        kxn = "nm_xn" if xi == 0 else "nm_xn1"
        ss = st[:, (i % 8) * 2:(i % 8) * 2 + 1]
        rs = st[:, (i % 8) * 2 + 1:(i % 8) * 2 + 2]
        kss, krs = ("nm_ss", i % 8), ("nm_rs", i % 8)
        T.op("act", [xkey], ["nm_junk", kss], lambda e: e.activation(
            junk[:, :], xt, AF.Square, accum_out=ss))
        T.op("act", [kss, "epsb"], [kss], lambda e: e.activation(ss, ss, AF.Sqrt, scale=1.0 / D, bias=self.epsb[:, 0:1]))
        T.op("dve", [kss], [krs], lambda e: e.reciprocal(rs, ss))
        T.op("act", [xkey, krs], [kxn], lambda e: e.activation(xn[:, :], xt, AF.Identity, scale=rs))
        for q4 in range(4):
            pst, psk = self.psnext()

            def tr(e):
                last = None
                for jj in range(4):
                    j = q4 * 4 + jj
                    last = e.transpose(pst[:, jj * 128:(jj + 1) * 128], xn[:, j * 128:(j + 1) * 128],
                                       self.identf[:, :])
                return last
            T.op("pe", [kxn, "identf"], [psk], tr)
            for jj in range(4):
                j = q4 * 4 + jj
                T.op("dve", [psk, "AB"], dkeys, lambda e: e.tensor_scalar(
                    dst_fn(j), pst[:, jj * 128:(jj + 1) * 128],
                    self.AB[:, ia, j:j + 1], self.AB[:, ia + 1, j:j + 1], ALU.mult, ALU.add), ww_ok=(j > 0))
        return rs, krs

    def p1_norm(self):
        T = self.T
        xts = [self.ar("xt%d" % i, [128, D], F32) for i in range(2)]
        for t in range(NCT + NT):
            xt = xts[t % 2]
            xk = ("xt", t % 2)
            src = self.ctx_d[t * 128:(t + 1) * 128, :] if t < NCT else \
                self.x_d[(t - NCT) * 128:(t - NCT + 1) * 128, :]
            T.dma("sp", "xt%d" % (t % 2), [], [xk], lambda e: e.dma_start(out=xt[:, :], in_=src))
            if t < NCT:
                self.norm_mod_T(xt[:, :], xk, 2, lambda j: self.hcT[:, j, t * 128:(t + 1) * 128], [("hcT", t)])
            else:
                tt = t - NCT
                self.norm_mod_T(xt[:, :], xk, 0, lambda j: self.hT[:, j, tt * 128:(tt + 1) * 128], [("hT", tt)])
        for tb in range(NTB):
            dst = self.hT_d[:, tb * TB:(tb + 1) * TB].rearrange("(k p) n -> p k n", p=128)
            T.dma("sp", "hTst", [("hT", tb * 4 + i) for i in range(4)], [("hT_d", tb)],
                  lambda e: e.dma_start(out=dst, in_=self.hT[:, :, tb * TB:(tb + 1) * TB]))

    def nr_alloc(self, nh=2):
        self.nr_qn = [self.ar("nr_qn%d" % i, [128, nh * 128], F32) for i in range(2)]
        self.nr_t = [[self.ar("nr_t%d_%d" % (i, j), [128, nh * 64], F32) for j in range(4)] for i in range(2)]
        self.nr_junk = self.ar("nr_junk", [128, 128], F32)

    def normrope(self, ps, pskey, gain, gkey, tile, out, okey, nh=2):
        T = self.T
        i = self.nr_i
        self.nr_i += 1
        st = self.nr_st[:, (i % 4) * 4:(i % 4) * 4 + nh]
        rs = self.nr_st[:, (i % 4) * 4 + 2:(i % 4) * 4 + 2 + nh]
        kst, krs = ("nr_ss", i % 4), ("nr_rs", i % 4)
        for h in range(nh):
            T.op("act", [pskey], ["nr_junk", kst], lambda e: e.activation(
                self.nr_junk[:, :], ps[:, h * 128:(h + 1) * 128], AF.Square, accum_out=st[:, h:h + 1]))
        T.op("act", [kst, "epsb"], [kst], lambda e: e.activation(st, st, AF.Sqrt, scale=1.0 / 128, bias=self.epsb[:, 0:1]))
        T.op("dve", [kst], [krs], lambda e: e.reciprocal(rs, st))
        if tile is None:
            for h in range(nh):
                T.op("dve", [pskey, krs, gkey], [okey], lambda e: e.scalar_tensor_tensor(
                    out[:, h * 128:(h + 1) * 128], ps[:, h * 128:(h + 1) * 128], rs[:, h:h + 1], gain[:, :],
                    ALU.mult, ALU.mult))
            return
        qn = self.nr_qn[i % 2]
        kq = ("nr_qn", i % 2)
        t1, t2, t3, t4 = self.nr_t[i % 2]
        kt = [("nr_t", i % 2, j) for j in range(4)]
        for h in range(nh):
            T.op("dve", [pskey, krs, gkey], [kq], lambda e: e.scalar_tensor_tensor(
                qn[:, h * 128:(h + 1) * 128], ps[:, h * 128:(h + 1) * 128], rs[:, h:h + 1], gain[:, :],
                ALU.mult, ALU.mult))
        ev = sb_view(qn[:, 0:], [[128, nh], [2, 64]])
        od = sb_view(qn[:, 1:], [[128, nh], [2, 64]])
        oev = sb_view(out[:, 0:], [[128, nh], [2, 64]])
        ood = sb_view(out[:, 1:], [[128, nh], [2, 64]])
        cs = sb_view(self.cosT[:, tile, :], [[0, nh], [1, 64]])
        sn = sb_view(self.sinT[:, tile, :], [[0, nh], [1, 64]])
        v3 = lambda t: t[:, :].rearrange("p (a b) -> p a b", a=nh)
        T.op("dve", [kq, "cosT"], [kt[0]], lambda e: e.tensor_tensor(v3(t1), ev, cs, ALU.mult))
        T.op("dve", [kq, "sinT"], [kt[1]], lambda e: e.tensor_tensor(v3(t2), od, sn, ALU.mult))
        T.op("dve", [kt[0], kt[1]], [okey], lambda e: e.tensor_tensor(oev, v3(t1), v3(t2), ALU.subtract))
        T.op("pool", [kq, "sinT"], [kt[2]], lambda e: e.tensor_tensor(v3(t3), ev, sn, ALU.mult))
        T.op("pool", [kq, "cosT"], [kt[3]], lambda e: e.tensor_tensor(v3(t4), od, cs, ALU.mult))
        T.op("pool", [kt[2], kt[3]], [okey], lambda e: e.tensor_tensor(ood, v3(t3), v3(t4), ALU.add))

    def p2_kv(self):
        T = self.T
        self.nr_alloc()
        krs_ = [self.ar("kr%d" % i, [128, 256], BF16) for i in range(2)]
        T.op("pool", [], ["Vones"], lambda e: e.memset(self.Vaug[:, :, :, 128:130], 1.0))
        n = 0
        for i in range(4):
            wp, wk = self.wnext(("kv", i))
            for t in range(NKEY):
                if t < NCT:
                    stat = lambda k: self.hcT[:, k, t * 128:(t + 1) * 128]
                    skey = ("hcT", t)
                else:
                    stat = lambda k: self.hT[:, k, (t - NCT) * 128:(t - NCT + 1) * 128]
                    skey = ("hT", t - NCT)
                pst, psk = self.psnext()

                def mm(e):
                    last = None
                    for k in range(KC):
                        last = e.matmul(pst[:, 0:256], stat(k), wp[:, k, :], start=(k == 0), stop=(k == KC - 1))
                    return last
                T.op("pe", [wk, skey], [psk], mm)
                if i < 2:
                    kr = krs_[n % 2]
                    kk = ("kr", n % 2)
                    n += 1
                    self.normrope(pst[:, 0:256], psk, self.gk_b, "gk_b", None if t < NCT else t - NCT, kr, kk)
                    ptt, ptk = self.psnext()
                    pb = ptt[:, 0:128].bitcast(BF16)

                    def tr(e):
                        e.transpose(pb[:, 0:128], kr[:, 0:128], self.identb[:, :])
                        return e.transpose(pb[:, 128:256], kr[:, 128:256], self.identb[:, :])
                    T.op("pe", [kk, "identb"], [ptk], tr)
                    T.op("act", [ptk], [("KT", i, t)], lambda e: e.activation(
                        self.KT[:, 2 * i:2 * i + 2, t * 128:(t + 1) * 128],
                        pb[:, 0:256].rearrange("p (a b) -> p a b", a=2), AF.Identity))
                else:
                    g0 = 2 * (i - 2)
                    T.op("act", [psk], [("V", i, t)], lambda e: e.activation(
                        self.Vaug[:, t, g0:g0 + 2, 0:128],
                        pst[:, 0:256].rearrange("p (a b) -> p a b", a=2), AF.Identity))

    def p3_rnn(self):
        T = self.T
        NC_ = S + 2 * L
        NW = S + L
        save = self.ar_ptr
        self.ar_ptr = self.kv_base
        xr = self.ar("xr", [128, NW], F32)
        self.gaT = self.ar("gaT", [16, 256], F32)
        xc0 = self.ar("xc0", [128, NC_], F32)
        xcb0 = self.ar("xcb0", [128, NC_], BF16)
        gx = self.ar("gx", [128, S], F32)
        Wg_ = [self.ar("Wg%d" % i, [128, 4, 128], BF16) for i in range(2)]
        assert self.ar_ptr <= self.kv_end, (self.ar_ptr, self.kv_end)
        self.ar_ptr = save
        xc_ = [xc0, self.ar("xc1", [128, NC_], F32)]
        xcb_ = [xcb0, self.ar("xcb1", [128, NC_], BF16)]
        A = self.ar("A", [128, NW], F32)
        I_ = self.ar("I", [128, NW], F32)
        M = [self.ar("M%d" % d, [128, NW], F32) for d in range(2)]
        PS7 = [0, 1, 2, 3, 4, 5, 6]
        modps, modk = self.ps[7], ("ps", 7)
        segs = [(0, L, None)] + [(L + b * 512, 512, b) for b in range(4)]
        cs = slice(0, 128)
        dblocks = [(o, min(512, NW - o)) for o in range(0, NW, 512)]
        Ak = [("A", o) for (o, _) in dblocks]
        Ik = [("I", o) for (o, _) in dblocks]

        def X1(c):
            wxr, kxr = self.wnext(("xr", c))
            Wg, wgk = Wg_[c % 2], ("Wg", c % 2)
            for (gi, wsrc_) in ((0, self.w_rg), (1, self.w_ig)):
                T.dma("pool", "Wg%d_%d" % (c % 2, gi), [], [(wgk, gi)], lambda e: e.dma_start(
                    out=Wg[:, 2 * gi:2 * gi + 2, :],
                    in_=bass.AP(wsrc_.tensor, c * 128 * 128, [[128, 128], [16 * 128 * 128, 2], [1, 128]])))
            outs = []
            for (off, n, b) in segs:
                pst, psk = self.psnext(PS7)
                if b is None:
                    rhs = lambda k: self.hcT[:, k, :]
                    rk = [("hcT", 0), ("hcT", 1)]
                else:
                    rhs = lambda k: self.hT[:, k, b * 512:(b + 1) * 512]
                    rk = [("hT", b * 4 + i) for i in range(4)]

                def mm(e):
                    last = None
                    for k in range(KC):
                        last = e.matmul(pst[:, 0:n], wxr[:, k, cs], rhs(k), start=(k == 0), stop=(k == KC - 1))
                    return last
                T.op("pe", [kxr] + rk, [psk], mm)
                outs.append((off, n, pst, psk))
            return outs

        def X2(c, outs):
            for (off, n, pst, psk) in outs:
                T.op("act", [psk], [("xr", off)], lambda e: e.activation(xr[:, off:off + n], pst[:, 0:n], AF.Identity))

        def CV(c):
            xc, xcb = xc_[c % 2], xcb_[c % 2]
            kxc, kxcb = ("xc", c % 2), ("xcb", c % 2)
            xrk = [("xr", o) for (o, _, _) in segs]
            w = lambda kk: self.pvec[:, PV_CW + kk * 16 + c:PV_CW + kk * 16 + c + 1]
            cbias = self.pvec[:, PV_CB + c:PV_CB + c + 1]
            for (off, n) in ((0, L), (L, S)):
                T.op("dve", xrk + ["pvec"], [kxc], lambda e: e.tensor_scalar(
                    xc[:, off:off + n], xr[:, off:off + n], w(1), cbias, ALU.mult, ALU.add))
                T.op("dve", xrk + [kxc, "pvec"], [kxc], lambda e: e.scalar_tensor_tensor(
                    xc[:, off + 1:off + n], xr[:, off:off + n - 1], w(0), xc[:, off + 1:off + n], ALU.mult, ALU.add))
                T.op("dve", xrk + [kxc, "pvec"], [kxc], lambda e: e.scalar_tensor_tensor(
                    xc[:, off:off + n - 1], xr[:, off + 1:off + n], w(2), xc[:, off:off + n - 1], ALU.mult, ALU.add))
                T.op("dve", xrk + [kxc, "pvec"], [kxc], lambda e: e.scalar_tensor_tensor(
                    xc[:, off:off + n - 2], xr[:, off + 2:off + n], w(3), xc[:, off:off + n - 2], ALU.mult, ALU.add))
            T.op("pool", [kxc], [kxc], lambda e: e.tensor_copy(xc[:, S + L:NC_], xc[:, 0:L]))
            T.op("act", [kxc], [kxcb], lambda e: e.activation(xcb[:, :], xc[:, :], AF.Identity))

        def GM(c):
            wxg, kxg = self.wnext(("xg", c))
            outs = []
            for b in range(4):
                pst, psk = self.psnext(PS7)

                def mm(e):
                    last = None
                    for k in range(KC):
                        last = e.matmul(pst[:, :], wxg[:, k, cs], self.hT[:, k, b * 512:(b + 1) * 512],
                                        start=(k == 0), stop=(k == KC - 1))
                    return last
                T.op("pe", [kxg] + [("hT", b * 4 + i) for i in range(4)], [psk], mm)
                outs.append((b, pst, psk))
            return outs

        def GE(outs):
            for (b, pst, psk) in outs:
                T.op("act", [psk], [("gx", b)], lambda e: e.activation(
                    gx[:, b * 512:(b + 1) * 512], pst[:, :], AF.Gelu_apprx_tanh))

        def Bgates(c, d):
            xcb, kxcb = xcb_[c % 2], ("xcb", c % 2)
            Wg, wgk = Wg_[c % 2], ("Wg", c % 2)
            br = self.pvec[:, PV_BRG + d * 16 + c:PV_BRG + d * 16 + c + 1]
            bi = self.pvec[:, PV_BIG + d * 16 + c:PV_BIG + d * 16 + c + 1]
            g0 = d * L
            for (o, n) in dblocks:
                for (gi, bias, dst, dk) in ((0, br, A, "A"), (1, bi, I_, "I")):
                    pst, psk = self.psnext(PS7)
                    T.op("pe", [(wgk, gi), kxcb], [psk], lambda e: e.matmul(
                        pst[:, 0:n], Wg[:, 2 * gi + d, :], xcb[:, g0 + o:g0 + o + n], start=True, stop=True))
                    T.op("act", [psk, "pvec"], [(dk, o)], lambda e: e.activation(
                        dst[:, o:o + n], pst[:, 0:n], AF.Sigmoid, bias=bias))

        def Bchain(c, d, mid=None):
            xc, kxc = xc_[c % 2], ("xc", c % 2)
            Md, Mk = M[d], ("M", d)
            g0 = d * L
            T.op("dve", Ik + [kxc], Ik, lambda e: e.tensor_tensor(I_[:, :], I_[:, :], xc[:, g0:g0 + NW], ALU.mult))
            T.op("act", Ak + ["cl"], [Mk], lambda e: e.activation(
                Md[:, :], A[:, :], AF.Exp, scale=self.cl[:, d, 1, c:c + 1]))
            T.op("act", Ak + ["cl"], Ak, lambda e: e.activation(
                A[:, :], A[:, :], AF.Exp, scale=self.cl[:, d, 0, c:c + 1]))
            T.op("dve", [Mk], [Mk], lambda e: e.tensor_scalar(Md[:, :], Md[:, :], 1.0, -1.0, ALU.min, ALU.mult))
            T.op("act", [Mk], [Mk], lambda e: e.activation(Md[:, :], Md[:, :], AF.Sqrt, scale=1.0, bias=1.0))
            if mid is not None:
                mid()
            T.op("dve", Ik + [Mk], Ik, lambda e: e.tensor_tensor(I_[:, :], I_[:, :], Md[:, :], ALU.mult))
            if d == 0:
                T.op("dve", Ak + Ik, [Mk], lambda e: e.tensor_tensor_scan(
                    Md[:, :], A[:, :], I_[:, :], 0.0, ALU.mult, ALU.add))
            else:
                rv = lambda t: sb_view(t[:, NW - 1:NW], [[-1, NW]])
                T.op("dve", Ak + Ik, [Mk], lambda e: e.tensor_tensor_scan(
                    rv(Md), rv(A), rv(I_), 0.0, ALU.mult, ALU.add))

        def stageE(c):
            if self.debug and c == 0:
                def dump(slot, ap, keys):
                    T.dma("sp", "dbg2_%d" % slot, keys, [("dbg2", slot)], lambda e: e.dma_start(
                        out=self.dbg2_d[slot * 128:(slot + 1) * 128, :], in_=ap))
                dump(0, xc_[0][:, L:L + S], [("xc", 0)])
                dump(1, gx[:, :], [("gx", b) for b in range(4)])
                dump(2, M[0][:, L:L + S], [("M", 0)])
                dump(3, M[1][:, 0:S], [("M", 1)])
                dump(4, A[:, 0:S], Ak)
                dump(5, I_[:, 0:S], Ik)
            T.op("dve", [("M", 0), ("M", 1)], [("M", 0)], lambda e: e.tensor_tensor(
                M[0][:, L:L + S], M[0][:, L:L + S], M[1][:, 0:S], ALU.add))
            T.op("dve", [("M", 0)] + [("gx", b) for b in range(4)], [("M", 0)], lambda e: e.tensor_tensor(
                M[0][:, L:L + S], M[0][:, L:L + S], gx[:, :], ALU.mult))
            T.dma("pool", "ubst", [("M", 0)], [("uT_d", c)], lambda e: e.dma_start(
                out=self.uT_d[c * 128:(c + 1) * 128, :], in_=M[0][:, L:L + S]))

        o0 = X1(0)
        X2(0, o0)
        CV(0)
        for c in range(16):
            Bgates(c, 0)
            nxt = X1(c + 1) if c + 1 < 16 else None
            Bchain(c, 0)
            if nxt is not None:
                X2(c + 1, nxt)
                CV(c + 1)
            GE(GM(c))
            if c % 2 == 1:
                pp = c // 2
                for q in range(4):
                    self.mod_step(modps, modk, 2 + (pp * 4 + q) // 8, (pp * 4 + q) % 8, 2)
            Bgates(c, 1)
            Bchain(c, 1)
            stageE(c)
        self.mod_finish(modps, modk, 2, 6)

    def p4_attn(self):
        T = self.T
        self.nr_alloc()
        hTblk = self.ar("hTblk", [128, KC, TB], BF16)
        uTblk = self.ar("uTblk", [128, KC, TB], BF16)
        attnT = self.ar("attnT", [128, 16, TB], BF16)
        mTblk = self.ar("mTblk", [128, KC, TB], BF16)
        QT = [self.ar("QT%d" % i, [128, 2, TB], BF16) for i in range(2)]
        NPT = 5
        PT = [self.ar("PT%d" % i, [128, TB], BF16) for i in range(NPT)]
        qr = [self.ar("qr%d" % i, [128, 256], BF16) for i in range(8)]
        at = [self.ar("at%d" % i, [128, 128], BF16) for i in range(8)]
        rden = self.ar("rden", [128, 8], F32)
        tA = [self.ar("tA%d" % i, [128, TB], F32) for i in range(2)]
        tB = [self.ar("tB%d" % i, [128, TB], F32) for i in range(2)]
        PS_S = [0, 1, 2]
        LA = 2
        isc = 1.0 / math.sqrt(128.0)
        pvap = lambda qt: self.ps[3 + qt // 2][:, (qt % 2) * 256:(qt % 2) * 256 + 129]
        pvk = lambda qt: ("ps", 3 + qt // 2)
        psq = lambda tt: self.ps[5][:, 0:256]
        psqk = lambda tt: ("ps", 5)
        pstb = [self.ps[6][:, 0:256].bitcast(BF16), self.ps[7][:, 0:256].bitcast(BF16)]
        qraw = [self.ar("qraw%d" % i, [128, 256], F32) for i in range(4)]
        st = {"npt": 0, "nat": 0, "ntr": 0}

        def trbuf():
            i = st["ntr"] % 2
            st["ntr"] += 1
            return pstb[i], ("ps", 6 + i)

        def qmm(tb, hp):
            wq, kwq = self.wnext(("q", tb, hp))
            for tt in range(4):
                def mm(e):
                    last = None
                    for k in range(KC):
                        last = e.matmul(psq(tt), hTblk[:, k, tt * 128:(tt + 1) * 128], wq[:, k, :],
                                        start=(k == 0), stop=(k == KC - 1))
                    return last
                T.op("pe", [kwq, "hTblk"], [psqk(tt)], mm)
                qi = (hp % 2) * 4 + tt
                T.op("act", [psqk(tt)], [("qraw", tt)], lambda e: e.activation(qraw[tt][:, :], psq(tt), AF.Identity))
                self.normrope(qraw[tt][:, :], ("qraw", tt), self.gq_b, "gq_b", tb * 4 + tt, qr[qi], ("qr", qi))

        def qtr(hp):
            qt_ = QT[hp % 2]
            for tt in range(4):
                qi = (hp % 2) * 4 + tt
                q_ = qr[qi]
                pb, pbk = trbuf()

                def tr(e):
                    e.transpose(pb[:, 0:128], q_[:, 0:128], self.identb[:, :])
                    return e.transpose(pb[:, 128:256], q_[:, 128:256], self.identb[:, :])
                T.op("pe", [("qr", qi), "identb"], [pbk], tr)
                T.op("dve", [pbk], [(("QT", hp % 2), tt)], lambda e: e.tensor_copy(
                    qt_[:, :, tt * 128:(tt + 1) * 128], pb[:, 0:256].rearrange("p (a b) -> p a b", a=2)))

        def tail_norm(h):
            items = []
            for qt in range(4):
                ai = st["nat"] % 8
                st["nat"] += 1
                a_, ka_ = at[ai], ("at", ai)
                rd, krd = rden[:, ai:ai + 1], ("rden", ai)
                po = pvap(qt)
                T.op("dve", [pvk(qt)], [krd], lambda e: e.reciprocal(rd, po[:, 128:129]))
                T.op("act", [pvk(qt), krd], [ka_], lambda e: e.activation(a_[:, :], po[:, 0:128], AF.Identity, scale=rd))
                items.append((h, qt, a_, ka_))
            return items

        def tail_tr(items):
            for (h, qt, a_, ka_) in items:
                pb, pbk = trbuf()
                T.op("pe", [ka_, "identb"], [pbk], lambda e: e.transpose(pb[:, 0:128], a_[:, :], self.identb[:, :]))
                T.op("dve", [pbk], [("attnT", h)], lambda e: e.tensor_copy(
                    attnT[:, h, qt * 128:(qt + 1) * 128], pb[:, 0:128]))

        for tb in range(NTB):
            T.dma("sp", "hTblk", [("hT_d", tb)], ["hTblk"], lambda e: e.dma_start(
                out=hTblk[:, :, :], in_=self.hT_d[:, tb * TB:(tb + 1) * TB].rearrange("(k p) n -> p k n", p=128)))
            T.dma("sp", "uTblk", [("uT_d", c) for c in range(16)], ["uTblk"], lambda e: e.dma_start(
                out=uTblk[:, :, :], in_=self.uT_d[:, tb * TB:(tb + 1) * TB].rearrange("(k p) n -> p k n", p=128)))
            pending = None
            qmm(tb, 0)
            for hp in range(8):
                g = hp // 2
                if hp < 7:
                    qmm(tb, hp + 1)
                qtr(hp)
                qt_ = QT[hp % 2]
                kqts = [(("QT", hp % 2), tt) for tt in range(4)]
                for hh in range(2):
                    h = 2 * hp + hh
                    sc = {}
                    for step in range(NKEY + LA):
                        if step < NKEY:
                            kc = step
                            pss, pssk = self.psnext(PS_S)
                            T.op("pe", kqts + ["KT"], [pssk], lambda e: e.matmul(
                                pss[:, :], self.KT[:, g, kc * 128:(kc + 1) * 128], qt_[:, hh, :], start=True, stop=True))
                            pi = st["npt"] % NPT
                            st["npt"] += 1
                            T.op("act", [pssk], [("PT", pi)], lambda e: e.activation(PT[pi][:, :], pss[:, :], AF.Exp, scale=isc))
                            sc[kc] = pi
                        if step == LA and pending is not None:
                            tail_tr(pending)
                            pending = None
                        j = step - LA
                        if j >= 0:
                            pj = sc.pop(j)

                            def pv(e):
                                last = None
                                for qt in range(4):
                                    last = e.matmul(pvap(qt), PT[pj][:, qt * 128:(qt + 1) * 128],
                                                    self.Vaug[:, j, g, 0:129],
                                                    start=(j == 0 and qt % 2 == 0), stop=(j == NKEY - 1),
                                                    skip_group_check=True)
                                return last
                            T.op("pe", [("PT", pj), "V"], [("ps", 3), ("ps", 4)], pv)
                    pending = tail_norm(h)
            tail_tr(pending)
            akeys = [("attnT", h) for h in range(16)]
            for ccp in range(8):
                def gemm2(tag, act, akeys_):
                    wp, wk = self.wnext((tag, tb, ccp))
                    res = []
                    for cl_ in range(2):
                        cs = slice(cl_ * 128, (cl_ + 1) * 128)
                        pst, psk = self.psnext()

                        def mm(e):
                            last = None
                            for k in range(KC):
                                last = e.matmul(pst[:, :], wp[:, k, cs], act[:, k, :], start=(k == 0), stop=(k == KC - 1))
                            return last
                        T.op("pe", [wk] + akeys_, [psk], mm)
                        res.append((pst, psk))
                    return res
                r1 = gemm2("gla", hTblk, ["hTblk"])
                for cl_ in range(2):
                    T.op("act", [r1[cl_][1]], [("tA", cl_)], lambda e: e.activation(tA[cl_][:, :], r1[cl_][0][:, :], AF.Sigmoid))
                r2 = gemm2("oa", attnT, akeys)
                for cl_ in range(2):
                    T.op("dve", [("tA", cl_), r2[cl_][1]], [("tA", cl_)], lambda e: e.tensor_tensor(
                        tA[cl_][:, :], tA[cl_][:, :], r2[cl_][0][:, :], ALU.mult))
                r3 = gemm2("glr", hTblk, ["hTblk"])
                for cl_ in range(2):
                    T.op("act", [r3[cl_][1]], [("tB", cl_)], lambda e: e.activation(tB[cl_][:, :], r3[cl_][0][:, :], AF.Sigmoid))
                r4 = gemm2("or", uTblk, ["uTblk"])
                for cl_ in range(2):
                    cc = 2 * ccp + cl_
                    T.op("dve", [("tB", cl_), r4[cl_][1]], [("tB", cl_)], lambda e: e.tensor_tensor(
                        tB[cl_][:, :], tB[cl_][:, :], r4[cl_][0][:, :], ALU.mult))
                    T.op("pool", [("tA", cl_), ("tB", cl_)], [("mTblk", cc)], lambda e: e.tensor_tensor(
                        mTblk[:, cc, :], tA[cl_][:, :], tB[cl_][:, :], ALU.add))
            T.dma("sp", "mTst", [("mTblk", cc) for cc in range(16)], [("mT_d", tb)], lambda e: e.dma_start(
                out=self.mT_d[:, tb * TB:(tb + 1) * TB].rearrange("(k p) n -> p k n", p=128), in_=mTblk[:, :, :]))

    def p6_mlp(self):
        T = self.T
        self.nm_alloc()
        mTblk = self.ar("mTblk2", [128, KC, TB], BF16)
        h2T = mTblk
        x1 = self.ar("x1buf", [128, 4, D], F32)
        actT = self.ar("actT", [128, 64, TB], BF16)
        gaa = self.ar("gaa", [128, D], F32)
        gaf = self.ar("gaf", [128, D], F32)
        gfb = self.ar("gfb", [128, D], F32)
        tmp = [self.ar("tmp%d" % i, [128, 512], F32) for i in range(3)]
        obuf = self.nm_xn
        T.dma("sp", "gaa", ["ga_d"], ["gaa"], lambda e: e.dma_start(
            out=gaa[:, :], in_=bass.AP(self.ga_d.tensor, 0, [[0, 128], [1, D]])))
        T.dma("sp", "gaf", ["ga_d"], ["gaf"], lambda e: e.dma_start(
            out=gaf[:, :], in_=bass.AP(self.ga_d.tensor, D, [[0, 128], [1, D]])))
        T.dma("sp", "gfb", [], ["gfb"], lambda e: e.dma_start(
            out=gfb[:, :], in_=bass.AP(self.gf_d.tensor, 0, [[0, 128], [1, D]])))
        nt = 0
        for tb in range(NTB):
            T.dma("sp", "mTld", [("mT_d", tb)], ["mTblk2"] + [("h2T", tt) for tt in range(4)], lambda e: e.dma_start(
                out=mTblk[:, :, :], in_=self.mT_d[:, tb * TB:(tb + 1) * TB].rearrange("(k p) n -> p k n", p=128)))
            x1k = [("x1", tt) for tt in range(4)]
            T.dma("sp", "x1ld", [], x1k, lambda e: e.dma_start(
                out=x1[:, :, :], in_=self.x_d[tb * TB:(tb + 1) * TB, :].rearrange("(t p) n -> p t n", p=128)))
            for np_ in range(8):
                wo, kwo = self.wnext(("wout", tb, np_))
                cols = slice(np_ * 256, (np_ + 1) * 256)
                for tt in range(4):
                    pst, psk = self.psnext()

                    def mm(e):
                        last = None
                        for k in range(KC):
                            last = e.matmul(pst[:, 0:256], mTblk[:, k, tt * 128:(tt + 1) * 128], wo[:, k, :],
                                            start=(k == 0), stop=(k == KC - 1))
                        return last
                    T.op("pe", [kwo, "mTblk2"], [psk], mm)
                    t_ = tmp[nt % 3]
                    kt_ = ("tmp", nt % 3)
                    nt += 1
                    T.op("dve", [psk, "gaa"], [kt_], lambda e: e.tensor_tensor(t_[:, 0:256], pst[:, 0:256], gaa[:, cols], ALU.mult))
                    T.op("pool", [kt_, ("x1", tt)], [("x1", tt)], lambda e: e.tensor_tensor(
                        x1[:, tt, cols], x1[:, tt, cols], t_[:, 0:256], ALU.add))
            if self.debug:
                T.dma("sp", "dbg2", x1k, [("dbg2", tb)], lambda e: e.dma_start(
                    out=self.dbg2_d[tb * TB:(tb + 1) * TB, :].rearrange("(t p) n -> p t n", p=128), in_=x1[:, :, :]))
            for tt in range(4):
                self.norm_mod_T(x1[:, tt, :], ("x1", tt), 4, lambda j: h2T[:, j, tt * 128:(tt + 1) * 128], [("h2T", tt), "mTblk2"])
            h2k = [("h2T", tt) for tt in range(4)]
            for fp_ in range(32):
                wu, kwu = self.wnext(("wup", tb, fp_))
                for cl_ in range(2):
                    fc = 2 * fp_ + cl_
                    cs = slice(cl_ * 128, (cl_ + 1) * 128)
                    pst, psk = self.psnext()

                    def mm(e):
                        last = None
                        for k in range(KC):
                            last = e.matmul(pst[:, :], wu[:, k, cs], h2T[:, k, :], start=(k == 0), stop=(k == KC - 1))
                        return last
                    T.op("pe", [kwu] + h2k, [psk], mm)
                    t_ = tmp[nt % 3]
                    kt_ = ("tmp", nt % 3)
                    nt += 1
                    T.op("act", [psk], [kt_], lambda e: e.activation(t_[:, :], pst[:, :], AF.Relu))
                    T.op("pool", [kt_], [("actT", fc)], lambda e: e.tensor_tensor(actT[:, fc, :], t_[:, :], t_[:, :], ALU.mult))
            for cb in range(4):
                banks = [0, 1, 2, 3] if cb % 2 == 0 else [4, 5, 6, 7]
                cols = slice(cb * 512, (cb + 1) * 512)
                for fp_ in range(8):
                    wd, kwd = self.wnext(("wdn", tb, cb, fp_))

                    def mm(e):
                        last = None
                        for tt in range(4):
                            for j in range(8):
                                last = e.matmul(self.ps[banks[tt]][:, :], actT[:, fp_ * 8 + j, tt * 128:(tt + 1) * 128],
                                                wd[:, j, :], start=(fp_ == 0 and j == 0), stop=(fp_ == 7 and j == 7))
                        return last
                    T.op("pe", [kwd] + [("actT", fp_ * 8 + j) for j in range(8)], [("ps", b) for b in banks], mm)
                for tt in range(4):
                    t_ = tmp[nt % 3]
                    kt_ = ("tmp", nt % 3)
                    nt += 1
                    T.op("dve", [("ps", banks[tt]), "gaf"], [kt_], lambda e: e.tensor_tensor(
                        t_[:, :], self.ps[banks[tt]][:, :], gaf[:, cols], ALU.mult))
                    T.op("pool", [kt_, ("x1", tt)], [("x1", tt)], lambda e: e.tensor_tensor(
                        x1[:, tt, cols], x1[:, tt, cols], t_[:, :], ALU.add))
            for tt in range(4):
                i = self.nm_i
                self.nm_i += 1
                ss = self.nm_st[:, (i % 8) * 2:(i % 8) * 2 + 1]
                rs = self.nm_st[:, (i % 8) * 2 + 1:(i % 8) * 2 + 2]
                kss, krs = ("nm_ss", i % 8), ("nm_rs", i % 8)
                T.op("act", [("x1", tt)], ["nm_junk", kss], lambda e: e.activation(
                    self.nm_junk[:, :], x1[:, tt, :], AF.Square, accum_out=ss))
                T.op("act", [kss, "epsb"], [kss], lambda e: e.activation(ss, ss, AF.Sqrt, scale=1.0 / D, bias=self.epsb[:, 0:1]))
                T.op("dve", [kss], [krs], lambda e: e.reciprocal(rs, ss))
                T.op("dve", [("x1", tt), krs, "gfb"], ["nm_xn"], lambda e: e.scalar_tensor_tensor(
                    obuf[:, :], x1[:, tt, :], rs, gfb[:, :], ALU.mult, ALU.mult))
                r0 = tb * TB + tt * 128
                T.dma("sp", "ost", ["nm_xn"], [("out", tb, tt)], lambda e: e.dma_start(
                    out=self.out_d[r0:r0 + 128, :], in_=obuf[:, :]))

    def finish(self):
        T = self.T
        if self.debug:
            T.barrier()
            self.ar_ptr = self.ar_end - 16384 - 64
            dbg = self.ar("dbgsb", [128, 4096], F32)
            T.op("dve", [], ["dbg"], lambda e: e.memset(dbg[:, :], 0.0))
            T.op("dve", ["cosT", "dbg"], ["dbg"], lambda e: e.tensor_copy(dbg[:, 0:1024], self.cosT[:, :, :].rearrange("p a b -> p (a b)")))
            T.op("dve", ["sinT", "dbg"], ["dbg"], lambda e: e.tensor_copy(dbg[:, 1024:2048], self.sinT[:, :, :].rearrange("p a b -> p (a b)")))
            T.op("dve", ["modL", "dbg"], ["dbg"], lambda e: e.tensor_copy(dbg[:, 2048:2144], self.modL[:, :]))
            T.op("dve", ["modC", "dbg"], ["dbg"], lambda e: e.tensor_copy(dbg[:, 2144:2240], self.modC[:, :]))
            T.op("dve", ["cl", "dbg"], ["dbg"], lambda e: e.tensor_copy(dbg[:, 2240:2304], self.cl[:, :, :, :].rearrange("p a b c -> p (a b c)")))
            if self.stage == 3:
                T.op("dve", ["dbg"], ["dbg"], lambda e: e.tensor_copy(dbg[:, 2304:2304 + 1152], self.KT[:, 0, 0:1152]))
                T.op("dve", ["dbg"], ["dbg"], lambda e: e.tensor_copy(dbg[:, 3456:3456 + 520], self.Vaug[:, 5, :, :].rearrange("p a b -> p (a b)")))
            T.dma("sp", "dbg", ["dbg"], ["dbg_d"], lambda e: e.dma_start(out=self.dbg_d, in_=dbg[:, :]))
        T.final_wait("sp")


def prep_inputs(inp, b):
    f = lambda a: np.ascontiguousarray(a, dtype=np.float32)
    fm = lambda v: f(np.asarray(v).reshape(-1, 16, 128).transpose(2, 0, 1).reshape(128, -1))
    pv = np.concatenate([
        fm(inp["c"][b][None]), fm(inp["c_ctx"][None]), fm(inp["g_mix"][0][None]), fm(inp["g_mlp"][0][None]),
        fm(inp["conv_w"][0]), fm(inp["conv_b"][0][None]), fm(inp["b_rg"][0]), fm(inp["b_ig"][0]),
        fm(inp["lru_lambda"][0]), fm(inp["b_mod"][0].reshape(6, D))], axis=1)
    assert pv.shape == (128, NPV), pv.shape
    return {
        "x": f(inp["x"][b]), "ctx": f(inp["ctx"][b]), "pvec": f(pv),
        "w_mod": f(inp["w_mod"][0]), "w_in": f(inp["w_in"][0]),
        "q_gain": f(inp["q_gain"][0][None]), "k_gain": f(inp["k_gain"][0][None]),
        "w_rg": f(inp["w_rg"][0].reshape(2 * 16 * 128, 128)), "w_ig": f(inp["w_ig"][0].reshape(2 * 16 * 128, 128)),
        "w_o_attn": f(inp["w_o_attn"][0]), "w_o_rnn": f(inp["w_o_rnn"][0]), "w_out": f(inp["w_out"][0]),
        "w_up": f(inp["w_up"][0]), "w_down": f(inp["w_down"][0]), "g_final": f(inp["g_final"][None]),
    }


def run(inputs, stage=99, debug=False, trace=False, ncores=8):
    bld = Builder(stage=stage, debug=debug)
    in_maps = [prep_inputs(inputs, b) for b in range(ncores)]
    res = run_bass_kernel_spmd(bld.nc, in_maps, core_ids=list(range(ncores)), trace=trace)
    return res


def kernel(**inputs):
    inputs = {k: np.asarray(v) for k, v in inputs.items()}
    res = run(inputs)
    return np.stack([res.results[b]["out"] for b in range(8)], axis=0).astype(np.float32)
```

```python
import math
import numpy as np
import concourse.bass as bass
import concourse.mybir as mybir
from concourse.bass_utils import run_bass_kernel_spmd

F32 = mybir.dt.float32
BF16 = mybir.dt.bfloat16
I32 = mybir.dt.int32
AF = mybir.ActivationFunctionType
ALU = mybir.AluOpType
AX = mybir.AxisListType

D = 2048
S = 2048
L = 256
NT = S // 128
NCT = L // 128
KC = D // 128
NKEY = (S + L) // 128
N_IN = 11264
DFF = 8192
EPS = 1e-6
TB = 512
NTB = S // TB

PV_C, PV_CC, PV_GMIX, PV_GMLP, PV_CW, PV_CB, PV_BRG, PV_BIG, PV_LAM, PV_BMOD = (
    0, 16, 32, 48, 64, 128, 144, 176, 208, 240)
NPV = 240 + 96


class Tracker:
    def __init__(self, nc):
        self.nc = nc
        self.engs = {"pe": nc.tensor, "act": nc.scalar, "dve": nc.vector,
                     "pool": nc.gpsimd, "sp": nc.sync}
        self.esem = {k: nc.alloc_semaphore("es_" + k) for k in self.engs}
        self.ecnt = {k: 0 for k in self.engs}
        self.waited = {k: {} for k in self.engs}
        self.last_w = {}
        self.readers = {}
        self.dsem = {}
        self.dcnt = {}

    def _deps(self, eng, reads, writes, ww_ok=False):
        deps = []
        for r in reads:
            w = self.last_w.get(r)
            if w is not None:
                deps.append(w)
        for wkey in writes:
            w = self.last_w.get(wkey)
            if w is not None and not (w[2] == eng and (eng == "pe" or ww_ok)):
                deps.append(w)
            for rd in self.readers.get(wkey, ()):
                deps.append(rd)
        return deps

    def _wait(self, eng, deps):
        e = self.engs[eng]
        wd = self.waited[eng]
        need = {}
        for (sem, val, _) in deps:
            k = id(sem)
            if val > wd.get(k, 0) and val > need.get(k, (None, 0))[1]:
                need[k] = (sem, val)
        for k, (sem, val) in need.items():
            e.wait_ge(sem, val)
            wd[k] = val

    def _commit(self, token, reads, writes):
        for r in reads:
            self.readers.setdefault(r, []).append(token)
        for w in writes:
            self.last_w[w] = token
            self.readers[w] = []

    def op(self, eng, reads, writes, fn, ww_ok=False):
        self._wait(eng, self._deps(eng, reads, writes, ww_ok))
        inst = fn(self.engs[eng])
        self.ecnt[eng] += 1
        inst.then_inc(self.esem[eng], 1)
        self._commit((self.esem[eng], self.ecnt[eng], eng), reads, writes)

    def dma(self, q, slot, reads, writes, fn):
        if slot not in self.dsem:
            self.dsem[slot] = self.nc.alloc_semaphore("ds_%d" % len(self.dsem))
            self.dcnt[slot] = 0
        sem = self.dsem[slot]
        deps = self._deps("dma:" + slot, reads, writes)
        if self.dcnt[slot] > 0:
            deps.append((sem, self.dcnt[slot], None))
        self._wait(q, deps)
        inst = fn(self.engs[q])
        self.dcnt[slot] += 16
        inst.then_inc(sem, 16)
        self._commit((sem, self.dcnt[slot], "dma:" + slot), reads, writes)

    def barrier(self):
        toks = [(self.esem[k], self.ecnt[k], k) for k in self.engs if self.ecnt[k] > 0]
        toks += [(self.dsem[s], self.dcnt[s], None) for s in self.dsem]
        for eng in self.engs:
            self._wait(eng, toks)
        self.last_w = {}
        self.readers = {}

    def final_wait(self, eng):
        deps = [(self.esem[k], self.ecnt[k], k) for k in self.engs if self.ecnt[k] > 0 and k != eng]
        deps += [(self.dsem[s], self.dcnt[s], None) for s in self.dsem]
        self._wait(eng, deps)


def sb_view(ap, dims):
    return bass.AP(ap.tensor, ap.offset, [list(ap.ap[0])] + [list(d) for d in dims])


class Builder:
    def __init__(self, stage=99, debug=False):
        self.stage = stage
        self.debug = debug
        self.nc = bass.Bass("TRN2", target_bir_lowering=False)
        self.T = Tracker(self.nc)
        self.build()

    def dram_in(self, name, shape, dt=F32):
        return self.nc.dram_tensor(name, list(shape), dt, kind="ExternalInput").ap()

    def scratch(self, name, shape, dt):
        kind = "ExternalOutput" if self.debug else "Internal"
        return self.nc.dram_tensor(name, list(shape), dt, kind=kind).ap()

    def sb(self, name, shape, dt):
        return self.nc.alloc_sbuf_tensor("s_" + name, list(shape), dt)

    def sb_at(self, name, shape, dt, off):
        return self.nc.alloc_sbuf_tensor_at("s_" + name, list(shape), dt, offset=off)

    def plan(self, tag, src):
        self.wplan.append((tag, src))

    def _issue_panel(self, i):
        tag, src = self.wplan[i]
        slot = i % self.NSLOT
        kc, n = src.shape[1], src.shape[2]
        dst = self.wslots[slot][:, 0:kc * n].rearrange("p (k n) -> p k n", k=kc)
        self.T.dma("pool", "w%d" % slot, [], [("w", slot)],
                   lambda e: e.dma_start(out=dst, in_=src))

    def wnext(self, tag):
        i = self.wpos
        assert self.wplan[i][0] == tag, (self.wplan[i][0], tag)
        while self.wissued < min(len(self.wplan), i + self.NSLOT):
            self._issue_panel(self.wissued)
            self.wissued += 1
        self.wpos += 1
        slot = i % self.NSLOT
        src = self.wplan[i][1]
        kc, n = src.shape[1], src.shape[2]
        return (self.wslots[slot][:, 0:kc * n].rearrange("p (k n) -> p k n", k=kc), ("w", slot))

    def wsrc(self, w, r0, nrows, c0, ncols):
        return w[r0:r0 + nrows, c0:c0 + ncols].rearrange("(k p) n -> p k n", p=128)

    def psnext(self, pool=None):
        pool = pool or list(range(8))
        i = self.pscur.get(tuple(pool), 0)
        self.pscur[tuple(pool)] = i + 1
        b = pool[i % len(pool)]
        return self.ps[b], ("ps", b)


    def ar(self, name, shape, dt):
        nbytes = int(np.prod(shape[1:])) * (4 if dt in (F32, I32) else 2)
        nbytes = (nbytes + 63) // 64 * 64
        off = self.ar_ptr
        self.ar_ptr += nbytes
        assert self.ar_ptr <= self.ar_end, (name, self.ar_ptr, self.ar_end)
        self.ar_id += 1
        return self.nc.alloc_sbuf_tensor_at("a%d_%s" % (self.ar_id, name), list(shape), dt, offset=off)

    def ar_mark(self):
        return self.ar_ptr

    def ar_release(self, mark):
        self.T.barrier()
        self.ar_ptr = mark

    def build(self):
        nc, T = self.nc, self.T
        self.x_d = self.dram_in("x", [S, D])
        self.ctx_d = self.dram_in("ctx", [L, D])
        self.pvec_d = self.dram_in("pvec", [128, NPV])
        self.w_mod = self.dram_in("w_mod", [D, 6 * D])
        self.w_in = self.dram_in("w_in", [D, N_IN])
        self.qg_d = self.dram_in("q_gain", [1, 128])
        self.kg_d = self.dram_in("k_gain", [1, 128])
        self.w_rg = self.dram_in("w_rg", [2 * 16 * 128, 128])
        self.w_ig = self.dram_in("w_ig", [2 * 16 * 128, 128])
        self.w_oa = self.dram_in("w_o_attn", [D, D])
        self.w_or = self.dram_in("w_o_rnn", [D, D])
        self.w_out = self.dram_in("w_out", [D, D])
        self.w_up = self.dram_in("w_up", [D, DFF])
        self.w_down = self.dram_in("w_down", [DFF, D])
        self.gf_d = self.dram_in("g_final", [1, D])
        self.out_d = nc.dram_tensor("out", [S, D], F32, kind="ExternalOutput").ap()
        self.hT_d = self.scratch("hT_d", [D, S], BF16)
        self.uT_d = self.scratch("uT_d", [D, S], BF16)
        self.mT_d = self.scratch("mT_d", [D, S], BF16)
        self.ga_d = self.scratch("ga_d", [2, D], F32)
        if self.debug:
            self.dbg_d = nc.dram_tensor("dbg", [128, 4096], F32, kind="ExternalOutput").ap()
            self.dbg2_d = nc.dram_tensor("dbg2", [S, D], F32, kind="ExternalOutput").ap()

        self.NSLOT = 4
        self.wslots = [self.sb("wslot%d" % i, [128, 4096], BF16) for i in range(self.NSLOT)]
        self.wplan, self.wpos, self.wissued = [], 0, 0
        self.pvec = self.sb("pvec", [128, NPV], F32)
        self.identf = self.sb("identf", [128, 128], F32)
        self.identb = self.sb("identb", [128, 128], BF16)
        self.cosT = self.sb("cosT", [128, NT, 64], F32)
        self.sinT = self.sb("sinT", [128, NT, 64], F32)
        self.gq_b = self.sb("gq_b", [128, 128], F32)
        self.gk_b = self.sb("gk_b", [128, 128], F32)
        self.modL = self.sb("modL", [128, 96], F32)
        self.modC = self.sb("modC", [128, 96], F32)
        self.AB = self.sb("AB", [128, 6, 16], F32)
        self.cl = self.sb("cl", [128, 2, 2, 16], F32)
        self.epsb = self.sb("epsb", [128, 1], F32)
        self.silu = self.sb("silu", [128, 16, 2], BF16)
        self.nm_st = self.sb("nm_st", [128, 16], F32)
        self.nr_st = self.sb("nr_st", [128, 16], F32)
        self.ps = [nc.alloc_psum_tensor("ps%d" % i, [128, 512], F32) for i in range(8)]
        self.pscur = {}
        self.ar_ptr = (nc.sbuf_base + 63) // 64 * 64
        self.ar_end = nc.sbuf_top
        self.ar_id = 0
        self.nm_i = 0
        self.nr_i = 0

        self.plan_all()
        m0 = self.ar_mark()
        self.setup()
        self.ar_release(m0)
        self.KT = self.ar("KT", [128, 4, S + L], BF16)
        self.Vaug = self.ar("Vaug", [128, NKEY, 4, 130], BF16)
        m1 = self.ar_mark()
        self.hT = self.ar("hT", [128, KC, S], BF16)
        self.hcT = self.ar("hcT", [128, KC, L], BF16)
        m2 = self.ar_mark()
        self.nm_alloc(2)
        self.p0_mod(0, 2)
        self.p1_norm()
        self.kv_base, self.kv_end = m0, m1
        if self.stage >= 2:
            self.ar_release(m2)
            self.p3_rnn()
        if self.stage >= 3:
            self.ar_release(m2)
            self.p2_kv()
        if self.stage >= 4:
            self.ar_release(m1)
            self.p4_attn()
        if self.stage >= 5:
            self.ar_release(m0)
            self.p6_mlp()
        self.finish()

    def plan_all(self):
        W = self.wsrc
        for m in range(2):
            for pp in range(8):
                self.plan(("mod", m, pp), W(self.w_mod, 0, D, m * D + pp * 256, 256))
        if self.stage >= 2:
            self.plan(("xr", 0), W(self.w_in, 0, D, 3072, 128))
            for c in range(16):
                if c + 1 < 16:
                    self.plan(("xr", c + 1), W(self.w_in, 0, D, 3072 + (c + 1) * 128, 128))
                self.plan(("xg", c), W(self.w_in, 0, D, 5120 + c * 128, 128))
                if c % 2 == 1:
                    pp = c // 2
                    for q in range(4):
                        m = 2 + (pp * 4 + q) // 8
                        mp = (pp * 4 + q) % 8
                        self.plan(("mod", m, mp), W(self.w_mod, 0, D, m * D + mp * 256, 256))
        if self.stage >= 3:
            for i in range(4):
                self.plan(("kv", i), W(self.w_in, 0, D, 2048 + i * 256, 256))
        if self.stage >= 4:
            for tb in range(NTB):
                for hp in range(8):
                    self.plan(("q", tb, hp), W(self.w_in, 0, D, hp * 256, 256))
                for ccp in range(8):
                    self.plan(("gla", tb, ccp), W(self.w_in, 0, D, 7168 + ccp * 256, 256))
                    self.plan(("oa", tb, ccp), W(self.w_oa, 0, D, ccp * 256, 256))
                    self.plan(("glr", tb, ccp), W(self.w_in, 0, D, 9216 + ccp * 256, 256))
                    self.plan(("or", tb, ccp), W(self.w_or, 0, D, ccp * 256, 256))
        if self.stage >= 5:
            for tb in range(NTB):
                for np_ in range(8):
                    self.plan(("wout", tb, np_), W(self.w_out, 0, D, np_ * 256, 256))
                for fp_ in range(32):
                    self.plan(("wup", tb, fp_), W(self.w_up, 0, D, fp_ * 256, 256))
                for cb in range(4):
                    for fp_ in range(8):
                        self.plan(("wdn", tb, cb, fp_), W(self.w_down, fp_ * 1024, 1024, cb * 512, 512))

    def setup(self):
        nc, T = self.nc, self.T
        T.dma("sp", "pvec", [], ["pvec"], lambda e: e.dma_start(out=self.pvec[:, :], in_=self.pvec_d))
        T.dma("sp", "gq", [], ["gq_b"], lambda e: e.dma_start(
            out=self.gq_b[:, :], in_=bass.AP(self.qg_d.tensor, 0, [[0, 128], [1, 128]])))
        T.dma("sp", "gk", [], ["gk_b"], lambda e: e.dma_start(
            out=self.gk_b[:, :], in_=bass.AP(self.kg_d.tensor, 0, [[0, 128], [1, 128]])))
        T.op("dve", [], ["modL"], lambda e: e.memset(self.modL[:, :], 0.0))
        T.op("dve", [], ["modC"], lambda e: e.memset(self.modC[:, :], 0.0))
        T.op("dve", [], ["epsb"], lambda e: e.memset(self.epsb[:, :], EPS))
        it = self.ar("iota_i", [128, 128], I32)
        T.op("pool", [], ["iota_i"], lambda e: e.iota(it[:, :], [[1, 128]], base=0, channel_multiplier=-1))
        T.op("dve", ["iota_i"], ["identf"], lambda e: e.tensor_scalar(
            self.identf[:, :], it[:, :], 0, None, ALU.is_equal))
        T.op("dve", ["identf"], ["identb"], lambda e: e.tensor_copy(self.identb[:, :], self.identf[:, :]))
        rowi = self.ar("rowi", [128, NT], I32)
        coli = self.ar("coli", [128, NT], I32)
        fi = self.ar("fi", [128, 32], I32)
        T.op("pool", [], ["rowi"], lambda e: e.iota(rowi[0:64, :], [[2, NT]], base=0, channel_multiplier=0))
        T.op("pool", [], ["rowi"], lambda e: e.iota(rowi[64:128, :], [[2, NT]], base=1, channel_multiplier=0))
        T.op("pool", [], ["coli"], lambda e: e.iota(coli[0:64, :], [[0, NT]], base=0, channel_multiplier=1))
        T.op("pool", [], ["coli"], lambda e: e.iota(coli[64:128, :], [[0, NT]], base=0, channel_multiplier=1))
        T.op("pool", [], ["fi"], lambda e: e.iota(fi[:, :], [[1, 32]], base=0, channel_multiplier=0))
        rowf = self.ar("rowf", [128, NT], F32)
        colf = self.ar("colf", [128, NT], F32)
        ff = self.ar("ff", [128, 32], F32)
        T.op("dve", ["rowi"], ["rowf"], lambda e: e.tensor_copy(rowf[:, :], rowi[:, :]))
        T.op("dve", ["coli"], ["colf"], lambda e: e.tensor_copy(colf[:, :], coli[:, :]))
        T.op("dve", ["fi"], ["ff"], lambda e: e.tensor_copy(ff[:, :], fi[:, :]))
        T.op("act", ["ff"], ["ff"], lambda e: e.activation(ff[:, :], ff[:, :], AF.Exp,
                                                            scale=-math.log(10000.0) / 32.0))
        ang = self.ar("ang", [128, NT, 64], F32)
        kf = self.ar("kf", [128, NT, 64], F32)
        ki = self.ar("ki", [128, NT, 64], I32)
        msk = self.ar("msk", [128, NT, 64], F32)
        ffb = sb_view(ff[:, :], [[0, NT], [1, 32]])
        T.op("dve", ["rowf", "ff"], ["ang"], lambda e: e.tensor_tensor(
            ang[:, :, 0:32], sb_view(rowf[:, :], [[1, NT], [0, 32]]), ffb, ALU.mult))
        T.op("dve", ["colf", "ff"], ["ang"], lambda e: e.tensor_tensor(
            ang[:, :, 32:64], sb_view(colf[:, :], [[1, NT], [0, 32]]), ffb, ALU.mult))
        TWO_PI = 2.0 * math.pi
        for (dst, dn, shift) in ((self.sinT, "sinT", 0.0), (self.cosT, "cosT", math.pi / 2.0)):
            T.op("dve", ["ang"], ["kf"], lambda e: e.tensor_scalar(
                kf[:, :, :], ang[:, :, :], shift, 1.0 / TWO_PI, ALU.add, ALU.mult))
            T.op("dve", ["kf"], ["ki"], lambda e: e.tensor_copy(ki[:, :, :], kf[:, :, :]))
            T.op("dve", ["ki"], ["kf"], lambda e: e.tensor_copy(kf[:, :, :], ki[:, :, :]))
            T.op("dve", ["kf"], ["kf"], lambda e: e.tensor_scalar(
                kf[:, :, :], kf[:, :, :], -TWO_PI, shift, ALU.mult, ALU.add))
            T.op("dve", ["kf", "ang"], ["kf"], lambda e: e.tensor_tensor(
                kf[:, :, :], kf[:, :, :], ang[:, :, :], ALU.add))
            T.op("dve", ["kf"], ["msk"], lambda e: e.tensor_scalar(
                msk[:, :, :], kf[:, :, :], math.pi, -TWO_PI, ALU.is_gt, ALU.mult))
            T.op("dve", ["kf", "msk"], ["kf"], lambda e: e.tensor_tensor(
                kf[:, :, :], kf[:, :, :], msk[:, :, :], ALU.add))
            T.op("dve", ["kf"], ["msk"], lambda e: e.tensor_scalar(
                msk[:, :, :], kf[:, :, :], -math.pi, TWO_PI, ALU.is_lt, ALU.mult))
            T.op("dve", ["kf", "msk"], ["kf"], lambda e: e.tensor_tensor(
                kf[:, :, :], kf[:, :, :], msk[:, :, :], ALU.add))
            T.op("dve", ["kf"], ["kf"], lambda e: e.tensor_scalar(
                kf[:, :, :], kf[:, :, :], math.pi, -math.pi, ALU.min, ALU.max))
            T.op("act", ["kf"], [dn], lambda e: e.activation(dst[:, :, :], kf[:, :, :], AF.Sin))
        lam = self.pvec[:, PV_LAM:PV_LAM + 32]
        sp_ = self.ar("sp_", [128, 32], F32)
        T.op("act", ["pvec"], ["sp_"], lambda e: e.activation(sp_[:, :], lam, AF.Exp, scale=-1.0))
        T.op("act", ["sp_"], ["sp_"], lambda e: e.activation(sp_[:, :], sp_[:, :], AF.Ln, bias=1.0))
        for d in range(2):
            T.op("dve", ["sp_"], ["cl"], lambda e: e.tensor_scalar(
                self.cl[:, d, 0, :], sp_[:, d * 16:(d + 1) * 16], -8.0, None, ALU.mult))
            T.op("dve", ["sp_"], ["cl"], lambda e: e.tensor_scalar(
                self.cl[:, d, 1, :], sp_[:, d * 16:(d + 1) * 16], -16.0, None, ALU.mult))
        T.op("act", ["pvec"], ["silu"], lambda e: e.activation(
            self.silu[:, :, 0], self.pvec[:, PV_C:PV_C + 16], AF.Silu))
        T.op("act", ["pvec"], ["silu"], lambda e: e.activation(
            self.silu[:, :, 1], self.pvec[:, PV_CC:PV_CC + 16], AF.Silu))

    def mod_step(self, pst, psk, m, pp, mbase):
        T = self.T
        wp, wk = self.wnext(("mod", m, pp))

        def mm(e):
            last = None
            for cl_ in range(2):
                col = ((m - mbase) * 16 + pp * 2 + cl_) * 2
                for k in range(KC):
                    last = e.matmul(pst[:, col:col + 2], wp[:, k, cl_ * 128:(cl_ + 1) * 128],
                                    self.silu[:, k, :], start=(k == 0), stop=(k == KC - 1))
            return last
        T.op("pe", [wk, "silu"], [psk], mm)

    def mod_finish(self, pst, psk, m0, m1):
        T = self.T
        nm = m1 - m0
        pv = pst[:, 0:nm * 32].rearrange("p (c t) -> p c t", t=2)
        bm = self.pvec[:, PV_BMOD + m0 * 16:PV_BMOD + m1 * 16]
        T.op("dve", [psk, "pvec"], ["modL"], lambda e: e.tensor_tensor(
            self.modL[:, m0 * 16:m1 * 16], pv[:, :, 0], bm, ALU.add))
        T.op("dve", [psk, "pvec"], ["modC"], lambda e: e.tensor_tensor(
            self.modC[:, m0 * 16:m1 * 16], pv[:, :, 1], bm, ALU.add))
        if m0 == 0:
            gm = self.pvec[:, PV_GMIX:PV_GMIX + 16]
            for (src, sn, ia) in ((self.modL, "modL", 0), (self.modC, "modC", 2)):
                T.op("dve", [sn, "pvec"], ["AB"], lambda e: e.scalar_tensor_tensor(
                    self.AB[:, ia, :], src[:, 16:32], 1.0, gm, ALU.add, ALU.mult))
                T.op("dve", [sn], ["AB"], lambda e: e.tensor_copy(self.AB[:, ia + 1, :], src[:, 0:16]))
        else:
            gm = self.pvec[:, PV_GMLP:PV_GMLP + 16]
            T.op("dve", ["modL", "pvec"], ["AB"], lambda e: e.scalar_tensor_tensor(
                self.AB[:, 4, :], self.modL[:, 64:80], 1.0, gm, ALU.add, ALU.mult))
            T.op("dve", ["modL"], ["AB"], lambda e: e.tensor_copy(self.AB[:, 5, :], self.modL[:, 48:64]))
            gaT = self.gaT
            for (gi, c0) in ((0, 32), (1, 80)):
                pt, pk = self.psnext([0, 1, 2, 3, 4, 5, 6])
                T.op("pe", ["modL", "identf"], [pk], lambda e: e.transpose(
                    pt[0:16, 0:128], self.modL[:, c0:c0 + 16], self.identf[:, :]))
                T.op("dve", [pk], ["gaT"], lambda e: e.tensor_copy(gaT[:, gi * 128:(gi + 1) * 128], pt[0:16, 0:128]))
            for gi in range(2):
                T.dma("sp", "gast", ["gaT"], ["ga_d"], lambda e: e.dma_start(
                    out=self.ga_d[gi:gi + 1, :].rearrange("o (j p) -> (o j) p", p=128),
                    in_=gaT[:, gi * 128:(gi + 1) * 128]))

    def p0_mod(self, m0, m1):
        pst, psk = self.psnext()
        for m in range(m0, m1):
            for pp in range(8):
                self.mod_step(pst, psk, m, pp, m0)
        self.mod_finish(pst, psk, m0, m1)

    def nm_alloc(self, nxn=1):
        self.nm_junk = self.ar("nm_junk", [128, D], BF16)
        self.nm_xns = [self.ar("nm_xn%d" % i, [128, D], F32) for i in range(nxn)]
        self.nm_xn = self.nm_xns[0]

    def nm_front(self, xt, xkey):
        T = self.T
        i = self.nm_i
        self.nm_i += 1
        xi = i % len(self.nm_xns)
        junk, xn, st = self.nm_junk, self.nm_xns[xi], self.nm_st
        kxn = "nm_xn" if xi == 0 else "nm_xn1"
        ss = st[:, (i % 8) * 2:(i % 8) * 2 + 1]
        rs = st[:, (i % 8) * 2 + 1:(i % 8) * 2 + 2]
        kss, krs = ("nm_ss", i % 8), ("nm_rs", i % 8)
        T.op("act", [xkey], ["nm_junk", kss], lambda e: e.activation(
            junk[:, :], xt, AF.Square, accum_out=ss))
        T.op("act", [kss, "epsb"], [kss], lambda e: e.activation(ss, ss, AF.Sqrt, scale=1.0 / D, bias=self.epsb[:, 0:1]))
        T.op("dve", [kss], [krs], lambda e: e.reciprocal(rs, ss))
        T.op("act", [xkey, krs], [kxn], lambda e: e.activation(xn[:, :], xt, AF.Identity, scale=rs))
        return (xn, kxn, rs, krs)

    def nm_back(self, fr, ia, dst_fn, dkeys):
        T = self.T
        xn, kxn, rs, krs = fr
        for q4 in range(4):
            pst, psk = self.psnext()

            def tr(e):
                last = None
                for jj in range(4):
                    j = q4 * 4 + jj
                    last = e.transpose(pst[:, jj * 128:(jj + 1) * 128], xn[:, j * 128:(j + 1) * 128],
                                       self.identf[:, :])
                return last
            T.op("pe", [kxn, "identf"], [psk], tr)
            for jj in range(4):
                j = q4 * 4 + jj
                T.op("dve", [psk, "AB"], dkeys, lambda e: e.tensor_scalar(
                    dst_fn(j), pst[:, jj * 128:(jj + 1) * 128],
                    self.AB[:, ia, j:j + 1], self.AB[:, ia + 1, j:j + 1], ALU.mult, ALU.add), ww_ok=(j > 0))

    def norm_mod_T(self, xt, xkey, ia, dst_fn, dkeys):
        fr = self.nm_front(xt, xkey)
        self.nm_back(fr, ia, dst_fn, dkeys)
        return fr[2], fr[3]

    def p1_norm(self):
        T = self.T
        xts = [self.ar("xt%d" % i, [128, D], F32) for i in range(2)]

        def front(t):
            xt = xts[t % 2]
            xk = ("xt", t % 2)
            src = self.ctx_d[t * 128:(t + 1) * 128, :] if t < NCT else \
                self.x_d[(t - NCT) * 128:(t - NCT + 1) * 128, :]
            T.dma("sp", "xt%d" % (t % 2), [], [xk], lambda e: e.dma_start(out=xt[:, :], in_=src))
            return self.nm_front(xt[:, :], xk)

        def back(t, fr):
            if t < NCT:
                self.nm_back(fr, 2, lambda j: self.hcT[:, j, t * 128:(t + 1) * 128], [("hcT", t)])
            else:
                tt = t - NCT
                self.nm_back(fr, 0, lambda j: self.hT[:, j, tt * 128:(tt + 1) * 128], [("hT", tt)])

        fr = front(0)
        for t in range(NCT + NT):
            nfr = front(t + 1) if t + 1 < NCT + NT else None
            back(t, fr)
            fr = nfr
        for tb in range(NTB):
            dst = self.hT_d[:, tb * TB:(tb + 1) * TB].rearrange("(k p) n -> p k n", p=128)
            T.dma("sp", "hTst", [("hT", tb * 4 + i) for i in range(4)], [("hT_d", tb)],
                  lambda e: e.dma_start(out=dst, in_=self.hT[:, :, tb * TB:(tb + 1) * TB]))

    def nr_alloc(self, nh=2):
        self.nr_qn = [self.ar("nr_qn%d" % i, [128, nh * 128], F32) for i in range(2)]
        self.nr_t = [[self.ar("nr_t%d_%d" % (i, j), [128, nh * 64], F32) for j in range(4)] for i in range(2)]
        self.nr_junk = self.ar("nr_junk", [128, 128], F32)

    def normrope(self, ps, pskey, gain, gkey, tile, out, okey, nh=2):
        T = self.T
        i = self.nr_i
        self.nr_i += 1
        st = self.nr_st[:, (i % 4) * 4:(i % 4) * 4 + nh]
        rs = self.nr_st[:, (i % 4) * 4 + 2:(i % 4) * 4 + 2 + nh]
        kst, krs = ("nr_ss", i % 4), ("nr_rs", i % 4)
        for h in range(nh):
            T.op("act", [pskey], ["nr_junk", kst], lambda e: e.activation(
                self.nr_junk[:, :], ps[:, h * 128:(h + 1) * 128], AF.Square, accum_out=st[:, h:h + 1]))
        T.op("act", [kst, "epsb"], [kst], lambda e: e.activation(st, st, AF.Sqrt, scale=1.0 / 128, bias=self.epsb[:, 0:1]))
        T.op("dve", [kst], [krs], lambda e: e.reciprocal(rs, st))
        if tile is None:
            for h in range(nh):
                T.op("dve", [pskey, krs, gkey], [okey], lambda e: e.scalar_tensor_tensor(
                    out[:, h * 128:(h + 1) * 128], ps[:, h * 128:(h + 1) * 128], rs[:, h:h + 1], gain[:, :],
                    ALU.mult, ALU.mult))
            return
        qn = self.nr_qn[i % 2]
        kq = ("nr_qn", i % 2)
        t1, t2, t3, t4 = self.nr_t[i % 2]
        kt = [("nr_t", i % 2, j) for j in range(4)]
        for h in range(nh):
            T.op("dve", [pskey, krs, gkey], [kq], lambda e: e.scalar_tensor_tensor(
                qn[:, h * 128:(h + 1) * 128], ps[:, h * 128:(h + 1) * 128], rs[:, h:h + 1], gain[:, :],
                ALU.mult, ALU.mult))
        ev = sb_view(qn[:, 0:], [[128, nh], [2, 64]])
        od = sb_view(qn[:, 1:], [[128, nh], [2, 64]])
        oev = sb_view(out[:, 0:], [[128, nh], [2, 64]])
        ood = sb_view(out[:, 1:], [[128, nh], [2, 64]])
        cs = sb_view(self.cosT[:, tile, :], [[0, nh], [1, 64]])
        sn = sb_view(self.sinT[:, tile, :], [[0, nh], [1, 64]])
        v3 = lambda t: t[:, :].rearrange("p (a b) -> p a b", a=nh)
        T.op("dve", [kq, "cosT"], [kt[0]], lambda e: e.tensor_tensor(v3(t1), ev, cs, ALU.mult))
        T.op("dve", [kq, "sinT"], [kt[1]], lambda e: e.tensor_tensor(v3(t2), od, sn, ALU.mult))
        T.op("dve", [kt[0], kt[1]], [okey], lambda e: e.tensor_tensor(oev, v3(t1), v3(t2), ALU.subtract))
        T.op("pool", [kq, "sinT"], [kt[2]], lambda e: e.tensor_tensor(v3(t3), ev, sn, ALU.mult))
        T.op("pool", [kq, "cosT"], [kt[3]], lambda e: e.tensor_tensor(v3(t4), od, cs, ALU.mult))
        T.op("pool", [kt[2], kt[3]], [okey], lambda e: e.tensor_tensor(ood, v3(t3), v3(t4), ALU.add))

    def p2_kv(self):
        T = self.T
        self.nr_alloc()
        krs_ = [self.ar("kr%d" % i, [128, 256], BF16) for i in range(2)]
        T.op("pool", [], ["Vones"], lambda e: e.memset(self.Vaug[:, :, :, 128:130], 1.0))
        n = 0
        for i in range(4):
            wp, wk = self.wnext(("kv", i))
            for t in range(NKEY):
                if t < NCT:
                    stat = lambda k: self.hcT[:, k, t * 128:(t + 1) * 128]
                    skey = ("hcT", t)
                else:
                    stat = lambda k: self.hT[:, k, (t - NCT) * 128:(t - NCT + 1) * 128]
                    skey = ("hT", t - NCT)
                pst, psk = self.psnext()

                def mm(e):
                    last = None
                    for k in range(KC):
                        last = e.matmul(pst[:, 0:256], stat(k), wp[:, k, :], start=(k == 0), stop=(k == KC - 1))
                    return last
                T.op("pe", [wk, skey], [psk], mm)
                if i < 2:
                    kr = krs_[n % 2]
                    kk = ("kr", n % 2)
                    n += 1
                    self.normrope(pst[:, 0:256], psk, self.gk_b, "gk_b", None if t < NCT else t - NCT, kr, kk)
                    ptt, ptk = self.psnext()
                    pb = ptt[:, 0:128].bitcast(BF16)

                    def tr(e):
                        e.transpose(pb[:, 0:128], kr[:, 0:128], self.identb[:, :])
                        return e.transpose(pb[:, 128:256], kr[:, 128:256], self.identb[:, :])
                    T.op("pe", [kk, "identb"], [ptk], tr)
                    T.op("act", [ptk], [("KT", i, t)], lambda e: e.activation(
                        self.KT[:, 2 * i:2 * i + 2, t * 128:(t + 1) * 128],
                        pb[:, 0:256].rearrange("p (a b) -> p a b", a=2), AF.Identity))
                else:
                    g0 = 2 * (i - 2)
                    T.op("act", [psk], [("V", i, t)], lambda e: e.activation(
                        self.Vaug[:, t, g0:g0 + 2, 0:128],
                        pst[:, 0:256].rearrange("p (a b) -> p a b", a=2), AF.Identity))

    def p3_rnn(self):
        T = self.T
        NC_ = S + 2 * L
        NW = S + L
        save = self.ar_ptr
        self.ar_ptr = self.kv_base
        xr = self.ar("xr", [128, NW], F32)
        self.gaT = self.ar("gaT", [16, 256], F32)
        xc0 = self.ar("xc0", [128, NC_], F32)
        xcb0 = self.ar("xcb0", [128, NC_], BF16)
        gx = self.ar("gx", [128, S], F32)
        Wg_ = [self.ar("Wg%d" % i, [128, 4, 128], BF16) for i in range(2)]
        assert self.ar_ptr <= self.kv_end, (self.ar_ptr, self.kv_end)
        self.ar_ptr = save
        xc_ = [xc0, self.ar("xc1", [128, NC_], F32)]
        xcb_ = [xcb0, self.ar("xcb1", [128, NC_], BF16)]
        A = self.ar("A", [128, NW], F32)
        I_ = self.ar("I", [128, NW], F32)
        M = [self.ar("M%d" % d, [128, NW], F32) for d in range(2)]
        PS7 = [0, 1, 2, 3, 4, 5, 6]
        modps, modk = self.ps[7], ("ps", 7)
        segs = [(0, L, None)] + [(L + b * 512, 512, b) for b in range(4)]
        cs = slice(0, 128)
        dblocks = [(o, min(512, NW - o)) for o in range(0, NW, 512)]
        Ak = [("A", o) for (o, _) in dblocks]
        Ik = [("I", o) for (o, _) in dblocks]

        def X1(c):
            wxr, kxr = self.wnext(("xr", c))
            Wg, wgk = Wg_[c % 2], ("Wg", c % 2)
            for (gi, wsrc_) in ((0, self.w_rg), (1, self.w_ig)):
                T.dma("pool", "Wg%d_%d" % (c % 2, gi), [], [(wgk, gi)], lambda e: e.dma_start(
                    out=Wg[:, 2 * gi:2 * gi + 2, :],
                    in_=bass.AP(wsrc_.tensor, c * 128 * 128, [[128, 128], [16 * 128 * 128, 2], [1, 128]])))
            outs = []
            for (off, n, b) in segs:
                pst, psk = self.psnext(PS7)
                if b is None:
                    rhs = lambda k: self.hcT[:, k, :]
                    rk = [("hcT", 0), ("hcT", 1)]
                else:
                    rhs = lambda k: self.hT[:, k, b * 512:(b + 1) * 512]
                    rk = [("hT", b * 4 + i) for i in range(4)]

                def mm(e):
                    last = None
                    for k in range(KC):
                        last = e.matmul(pst[:, 0:n], wxr[:, k, cs], rhs(k), start=(k == 0), stop=(k == KC - 1))
                    return last
                T.op("pe", [kxr] + rk, [psk], mm)
                outs.append((off, n, pst, psk))
            return outs

        def X2(c, outs):
            for (off, n, pst, psk) in outs:
                T.op("act", [psk], [("xr", off)], lambda e: e.activation(xr[:, off:off + n], pst[:, 0:n], AF.Identity))

        def CV(c):
            xc, xcb = xc_[c % 2], xcb_[c % 2]
            kxc, kxcb = ("xc", c % 2), ("xcb", c % 2)
            xrk = [("xr", o) for (o, _, _) in segs]
            w = lambda kk: self.pvec[:, PV_CW + kk * 16 + c:PV_CW + kk * 16 + c + 1]
            cbias = self.pvec[:, PV_CB + c:PV_CB + c + 1]
            for (off, n) in ((0, L), (L, S)):
                T.op("dve", xrk + ["pvec"], [kxc], lambda e: e.tensor_scalar(
                    xc[:, off:off + n], xr[:, off:off + n], w(1), cbias, ALU.mult, ALU.add))
                T.op("dve", xrk + [kxc, "pvec"], [kxc], lambda e: e.scalar_tensor_tensor(
                    xc[:, off + 1:off + n], xr[:, off:off + n - 1], w(0), xc[:, off + 1:off + n], ALU.mult, ALU.add))
                T.op("dve", xrk + [kxc, "pvec"], [kxc], lambda e: e.scalar_tensor_tensor(
                    xc[:, off:off + n - 1], xr[:, off + 1:off + n], w(2), xc[:, off:off + n - 1], ALU.mult, ALU.add))
                T.op("dve", xrk + [kxc, "pvec"], [kxc], lambda e: e.scalar_tensor_tensor(
                    xc[:, off:off + n - 2], xr[:, off + 2:off + n], w(3), xc[:, off:off + n - 2], ALU.mult, ALU.add))
            T.op("pool", [kxc], [kxc], lambda e: e.tensor_copy(xc[:, S + L:NC_], xc[:, 0:L]))
            T.op("act", [kxc], [kxcb], lambda e: e.activation(xcb[:, :], xc[:, :], AF.Identity))

        def GM(c):
            wxg, kxg = self.wnext(("xg", c))
            outs = []
            for b in range(4):
                pst, psk = self.psnext(PS7)

                def mm(e):
                    last = None
                    for k in range(KC):
                        last = e.matmul(pst[:, :], wxg[:, k, cs], self.hT[:, k, b * 512:(b + 1) * 512],
                                        start=(k == 0), stop=(k == KC - 1))
                    return last
                T.op("pe", [kxg] + [("hT", b * 4 + i) for i in range(4)], [psk], mm)
                outs.append((b, pst, psk))
            return outs

        def GE(outs):
            for (b, pst, psk) in outs:
                T.op("act", [psk], [("gx", b)], lambda e: e.activation(
                    gx[:, b * 512:(b + 1) * 512], pst[:, :], AF.Gelu_apprx_tanh))

        def Bgates(c, d):
            xcb, kxcb = xcb_[c % 2], ("xcb", c % 2)
            Wg, wgk = Wg_[c % 2], ("Wg", c % 2)
            br = self.pvec[:, PV_BRG + d * 16 + c:PV_BRG + d * 16 + c + 1]
            bi = self.pvec[:, PV_BIG + d * 16 + c:PV_BIG + d * 16 + c + 1]
            g0 = d * L
            for (o, n) in dblocks:
                for (gi, bias, dst, dk) in ((0, br, A, "A"), (1, bi, I_, "I")):
                    pst, psk = self.psnext(PS7)
                    T.op("pe", [(wgk, gi), kxcb], [psk], lambda e: e.matmul(
                        pst[:, 0:n], Wg[:, 2 * gi + d, :], xcb[:, g0 + o:g0 + o + n], start=True, stop=True))
                    T.op("act", [psk, "pvec"], [(dk, o)], lambda e: e.activation(
                        dst[:, o:o + n], pst[:, 0:n], AF.Sigmoid, bias=bias))

        def Bchain(c, d, mid=None):
            xc, kxc = xc_[c % 2], ("xc", c % 2)
            Md, Mk = M[d], ("M", d)
            g0 = d * L
            T.op("dve", Ik + [kxc], Ik, lambda e: e.tensor_tensor(I_[:, :], I_[:, :], xc[:, g0:g0 + NW], ALU.mult))
            T.op("act", Ak + ["cl"], [Mk], lambda e: e.activation(
                Md[:, :], A[:, :], AF.Exp, scale=self.cl[:, d, 1, c:c + 1]))
            T.op("act", Ak + ["cl"], Ak, lambda e: e.activation(
                A[:, :], A[:, :], AF.Exp, scale=self.cl[:, d, 0, c:c + 1]))
            T.op("dve", [Mk], [Mk], lambda e: e.tensor_scalar(Md[:, :], Md[:, :], 1.0, -1.0, ALU.min, ALU.mult))
            T.op("act", [Mk], [Mk], lambda e: e.activation(Md[:, :], Md[:, :], AF.Sqrt, scale=1.0, bias=1.0))
            if mid is not None:
                mid()
            T.op("dve", Ik + [Mk], Ik, lambda e: e.tensor_tensor(I_[:, :], I_[:, :], Md[:, :], ALU.mult))
            if d == 0:
                T.op("dve", Ak + Ik, [Mk], lambda e: e.tensor_tensor_scan(
                    Md[:, :], A[:, :], I_[:, :], 0.0, ALU.mult, ALU.add))
            else:
                rv = lambda t: sb_view(t[:, NW - 1:NW], [[-1, NW]])
                T.op("dve", Ak + Ik, [Mk], lambda e: e.tensor_tensor_scan(
                    rv(Md), rv(A), rv(I_), 0.0, ALU.mult, ALU.add))

        def stageE(c):
            if self.debug and c == 0:
                def dump(slot, ap, keys):
                    T.dma("sp", "dbg2_%d" % slot, keys, [("dbg2", slot)], lambda e: e.dma_start(
                        out=self.dbg2_d[slot * 128:(slot + 1) * 128, :], in_=ap))
                dump(0, xc_[0][:, L:L + S], [("xc", 0)])
                dump(1, gx[:, :], [("gx", b) for b in range(4)])
                dump(2, M[0][:, L:L + S], [("M", 0)])
                dump(3, M[1][:, 0:S], [("M", 1)])
                dump(4, A[:, 0:S], Ak)
                dump(5, I_[:, 0:S], Ik)
            T.op("dve", [("M", 0), ("M", 1)], [("M", 0)], lambda e: e.tensor_tensor(
                M[0][:, L:L + S], M[0][:, L:L + S], M[1][:, 0:S], ALU.add))
            T.op("dve", [("M", 0)] + [("gx", b) for b in range(4)], [("M", 0)], lambda e: e.tensor_tensor(
                M[0][:, L:L + S], M[0][:, L:L + S], gx[:, :], ALU.mult))
            T.dma("pool", "ubst", [("M", 0)], [("uT_d", c)], lambda e: e.dma_start(
                out=self.uT_d[c * 128:(c + 1) * 128, :], in_=M[0][:, L:L + S]))

        o0 = X1(0)
        X2(0, o0)
        CV(0)
        for c in range(16):
            Bgates(c, 0)
            nxt = X1(c + 1) if c + 1 < 16 else None
            Bchain(c, 0)
            if nxt is not None:
                X2(c + 1, nxt)
                CV(c + 1)
            GE(GM(c))
            if c % 2 == 1:
                pp = c // 2
                for q in range(4):
                    self.mod_step(modps, modk, 2 + (pp * 4 + q) // 8, (pp * 4 + q) % 8, 2)
            Bgates(c, 1)
            Bchain(c, 1)
            stageE(c)
        self.mod_finish(modps, modk, 2, 6)

    def p4_attn(self):
        T = self.T
        self.nr_alloc()
        hTblk = self.ar("hTblk", [128, KC, TB], BF16)
        uTblk = self.ar("uTblk", [128, KC, TB], BF16)
        attnT = self.ar("attnT", [128, 16, TB], BF16)
        mTblk = self.ar("mTblk", [128, KC, TB], BF16)
        QT = [self.ar("QT%d" % i, [128, 2, TB], BF16) for i in range(2)]
        NPT = 5
        PT = [self.ar("PT%d" % i, [128, TB], BF16) for i in range(NPT)]
        qr = [self.ar("qr%d" % i, [128, 256], BF16) for i in range(8)]
        at = [self.ar("at%d" % i, [128, 128], BF16) for i in range(8)]
        rden = self.ar("rden", [128, 8], F32)
        tA = [self.ar("tA%d" % i, [128, TB], F32) for i in range(2)]
        tB = [self.ar("tB%d" % i, [128, TB], F32) for i in range(2)]
        PS_S = [0, 1, 2]
        LA = 2
        isc = 1.0 / math.sqrt(128.0)
        pvap = lambda qt: self.ps[3 + qt // 2][:, (qt % 2) * 256:(qt % 2) * 256 + 129]
        pvk = lambda qt: ("ps", 3 + qt // 2)
        psq = lambda tt: self.ps[5][:, 0:256]
        psqk = lambda tt: ("ps", 5)
        pstb = [self.ps[6][:, 0:256].bitcast(BF16), self.ps[7][:, 0:256].bitcast(BF16)]
        qraw = [self.ar("qraw%d" % i, [128, 256], F32) for i in range(4)]
        st = {"npt": 0, "nat": 0, "ntr": 0}

        def trbuf():
            i = st["ntr"] % 2
            st["ntr"] += 1
            return pstb[i], ("ps", 6 + i)

        def qmm(tb, hp):
            wq, kwq = self.wnext(("q", tb, hp))
            for tt in range(4):
                def mm(e):
                    last = None
                    for k in range(KC):
                        last = e.matmul(psq(tt), hTblk[:, k, tt * 128:(tt + 1) * 128], wq[:, k, :],
                                        start=(k == 0), stop=(k == KC - 1))
                    return last
                T.op("pe", [kwq, "hTblk"], [psqk(tt)], mm)
                qi = (hp % 2) * 4 + tt
                T.op("act", [psqk(tt)], [("qraw", tt)], lambda e: e.activation(qraw[tt][:, :], psq(tt), AF.Identity))
                self.normrope(qraw[tt][:, :], ("qraw", tt), self.gq_b, "gq_b", tb * 4 + tt, qr[qi], ("qr", qi))

        def qtr(hp):
            qt_ = QT[hp % 2]
            for tt in range(4):
                qi = (hp % 2) * 4 + tt
                q_ = qr[qi]
                pb, pbk = trbuf()

                def tr(e):
                    e.transpose(pb[:, 0:128], q_[:, 0:128], self.identb[:, :])
                    return e.transpose(pb[:, 128:256], q_[:, 128:256], self.identb[:, :])
                T.op("pe", [("qr", qi), "identb"], [pbk], tr)
                T.op("dve", [pbk], [(("QT", hp % 2), tt)], lambda e: e.tensor_copy(
                    qt_[:, :, tt * 128:(tt + 1) * 128], pb[:, 0:256].rearrange("p (a b) -> p a b", a=2)))

        def tail_norm(h):
            items = []
            for qt in range(4):
                ai = st["nat"] % 8
                st["nat"] += 1
                a_, ka_ = at[ai], ("at", ai)
                rd, krd = rden[:, ai:ai + 1], ("rden", ai)
                po = pvap(qt)
                T.op("dve", [pvk(qt)], [krd], lambda e: e.reciprocal(rd, po[:, 128:129]))
                T.op("act", [pvk(qt), krd], [ka_], lambda e: e.activation(a_[:, :], po[:, 0:128], AF.Identity, scale=rd))
                items.append((h, qt, a_, ka_))
            return items

        def tail_tr(items):
            for (h, qt, a_, ka_) in items:
                pb, pbk = trbuf()
                T.op("pe", [ka_, "identb"], [pbk], lambda e: e.transpose(pb[:, 0:128], a_[:, :], self.identb[:, :]))
                T.op("dve", [pbk], [("attnT", h)], lambda e: e.tensor_copy(
                    attnT[:, h, qt * 128:(qt + 1) * 128], pb[:, 0:128]))

        for tb in range(NTB):
            T.dma("sp", "hTblk", [("hT_d", tb)], ["hTblk"], lambda e: e.dma_start(
                out=hTblk[:, :, :], in_=self.hT_d[:, tb * TB:(tb + 1) * TB].rearrange("(k p) n -> p k n", p=128)))
            T.dma("sp", "uTblk", [("uT_d", c) for c in range(16)], ["uTblk"], lambda e: e.dma_start(
                out=uTblk[:, :, :], in_=self.uT_d[:, tb * TB:(tb + 1) * TB].rearrange("(k p) n -> p k n", p=128)))
            pending = None
            qmm(tb, 0)
            for hp in range(8):
                g = hp // 2
                if hp < 7:
                    qmm(tb, hp + 1)
                qtr(hp)
                qt_ = QT[hp % 2]
                kqts = [(("QT", hp % 2), tt) for tt in range(4)]
                for hh in range(2):
                    h = 2 * hp + hh
                    sc = {}
                    for step in range(NKEY + LA):
                        if step < NKEY:
                            kc = step
                            pss, pssk = self.psnext(PS_S)
                            T.op("pe", kqts + ["KT"], [pssk], lambda e: e.matmul(
                                pss[:, :], self.KT[:, g, kc * 128:(kc + 1) * 128], qt_[:, hh, :], start=True, stop=True))
                            pi = st["npt"] % NPT
                            st["npt"] += 1
                            T.op("act", [pssk], [("PT", pi)], lambda e: e.activation(PT[pi][:, :], pss[:, :], AF.Exp, scale=isc))
                            sc[kc] = pi
                        if step == LA and pending is not None:
                            tail_tr(pending)
                            pending = None
                        j = step - LA
                        if j >= 0:
                            pj = sc.pop(j)

                            def pv(e):
                                last = None
                                for qt in range(4):
                                    last = e.matmul(pvap(qt), PT[pj][:, qt * 128:(qt + 1) * 128],
                                                    self.Vaug[:, j, g, 0:129],
                                                    start=(j == 0 and qt % 2 == 0), stop=(j == NKEY - 1),
                                                    skip_group_check=True)
                                return last
                            T.op("pe", [("PT", pj), "V"], [("ps", 3), ("ps", 4)], pv)
                    pending = tail_norm(h)
            tail_tr(pending)
            akeys = [("attnT", h) for h in range(16)]
            for ccp in range(8):
                def gemm2(tag, act, akeys_):
                    wp, wk = self.wnext((tag, tb, ccp))
                    res = []
                    for cl_ in range(2):
                        cs = slice(cl_ * 128, (cl_ + 1) * 128)
                        pst, psk = self.psnext()

                        def mm(e):
                            last = None
                            for k in range(KC):
                                last = e.matmul(pst[:, :], wp[:, k, cs], act[:, k, :], start=(k == 0), stop=(k == KC - 1))
                            return last
                        T.op("pe", [wk] + akeys_, [psk], mm)
                        res.append((pst, psk))
                    return res
                r1 = gemm2("gla", hTblk, ["hTblk"])
                for cl_ in range(2):
                    T.op("act", [r1[cl_][1]], [("tA", cl_)], lambda e: e.activation(tA[cl_][:, :], r1[cl_][0][:, :], AF.Sigmoid))
                r2 = gemm2("oa", attnT, akeys)
                for cl_ in range(2):
                    T.op("dve", [("tA", cl_), r2[cl_][1]], [("tA", cl_)], lambda e: e.tensor_tensor(
                        tA[cl_][:, :], tA[cl_][:, :], r2[cl_][0][:, :], ALU.mult))
                r3 = gemm2("glr", hTblk, ["hTblk"])
                for cl_ in range(2):
                    T.op("act", [r3[cl_][1]], [("tB", cl_)], lambda e: e.activation(tB[cl_][:, :], r3[cl_][0][:, :], AF.Sigmoid))
                r4 = gemm2("or", uTblk, ["uTblk"])
                for cl_ in range(2):
                    cc = 2 * ccp + cl_
                    T.op("dve", [("tB", cl_), r4[cl_][1]], [("tB", cl_)], lambda e: e.tensor_tensor(
                        tB[cl_][:, :], tB[cl_][:, :], r4[cl_][0][:, :], ALU.mult))
                    T.op("pool", [("tA", cl_), ("tB", cl_)], [("mTblk", cc)], lambda e: e.tensor_tensor(
                        mTblk[:, cc, :], tA[cl_][:, :], tB[cl_][:, :], ALU.add))
            T.dma("sp", "mTst", [("mTblk", cc) for cc in range(16)], [("mT_d", tb)], lambda e: e.dma_start(
                out=self.mT_d[:, tb * TB:(tb + 1) * TB].rearrange("(k p) n -> p k n", p=128), in_=mTblk[:, :, :]))

    def p6_mlp(self):
        T = self.T
        self.nm_alloc()
        mTblk = self.ar("mTblk2", [128, KC, TB], BF16)
        h2T = mTblk
        x1 = self.ar("x1buf", [128, 4, D], F32)
        actT = self.ar("actT", [128, 64, TB], BF16)
        gaa = self.ar("gaa", [128, D], F32)
        gaf = self.ar("gaf", [128, D], F32)
        gfb = self.ar("gfb", [128, D], F32)
        tmp = [self.ar("tmp%d" % i, [128, 512], F32) for i in range(3)]
        obuf = self.nm_xn
        T.dma("sp", "gaa", ["ga_d"], ["gaa"], lambda e: e.dma_start(
            out=gaa[:, :], in_=bass.AP(self.ga_d.tensor, 0, [[0, 128], [1, D]])))
        T.dma("sp", "gaf", ["ga_d"], ["gaf"], lambda e: e.dma_start(
            out=gaf[:, :], in_=bass.AP(self.ga_d.tensor, D, [[0, 128], [1, D]])))
        T.dma("sp", "gfb", [], ["gfb"], lambda e: e.dma_start(
            out=gfb[:, :], in_=bass.AP(self.gf_d.tensor, 0, [[0, 128], [1, D]])))
        nt = 0
        for tb in range(NTB):
            T.dma("sp", "mTld", [("mT_d", tb)], ["mTblk2"] + [("h2T", tt) for tt in range(4)], lambda e: e.dma_start(
                out=mTblk[:, :, :], in_=self.mT_d[:, tb * TB:(tb + 1) * TB].rearrange("(k p) n -> p k n", p=128)))
            x1k = [("x1", tt) for tt in range(4)]
            T.dma("sp", "x1ld", [], x1k, lambda e: e.dma_start(
                out=x1[:, :, :], in_=self.x_d[tb * TB:(tb + 1) * TB, :].rearrange("(t p) n -> p t n", p=128)))
            for np_ in range(8):
                wo, kwo = self.wnext(("wout", tb, np_))
                cols = slice(np_ * 256, (np_ + 1) * 256)
                for tt in range(4):
                    pst, psk = self.psnext()

                    def mm(e):
                        last = None
                        for k in range(KC):
                            last = e.matmul(pst[:, 0:256], mTblk[:, k, tt * 128:(tt + 1) * 128], wo[:, k, :],
                                            start=(k == 0), stop=(k == KC - 1))
                        return last
                    T.op("pe", [kwo, "mTblk2"], [psk], mm)
                    t_ = tmp[nt % 3]
                    kt_ = ("tmp", nt % 3)
                    nt += 1
                    T.op("dve", [psk, "gaa"], [kt_], lambda e: e.tensor_tensor(t_[:, 0:256], pst[:, 0:256], gaa[:, cols], ALU.mult))
                    T.op("pool", [kt_, ("x1", tt)], [("x1", tt)], lambda e: e.tensor_tensor(
                        x1[:, tt, cols], x1[:, tt, cols], t_[:, 0:256], ALU.add))
            if self.debug:
                T.dma("sp", "dbg2", x1k, [("dbg2", tb)], lambda e: e.dma_start(
                    out=self.dbg2_d[tb * TB:(tb + 1) * TB, :].rearrange("(t p) n -> p t n", p=128), in_=x1[:, :, :]))
            for tt in range(4):
                self.norm_mod_T(x1[:, tt, :], ("x1", tt), 4, lambda j: h2T[:, j, tt * 128:(tt + 1) * 128], [("h2T", tt), "mTblk2"])
            h2k = [("h2T", tt) for tt in range(4)]
            for fp_ in range(32):
                wu, kwu = self.wnext(("wup", tb, fp_))
                for cl_ in range(2):
                    fc = 2 * fp_ + cl_
                    cs = slice(cl_ * 128, (cl_ + 1) * 128)
                    pst, psk = self.psnext()

                    def mm(e):
                        last = None
                        for k in range(KC):
                            last = e.matmul(pst[:, :], wu[:, k, cs], h2T[:, k, :], start=(k == 0), stop=(k == KC - 1))
                        return last
                    T.op("pe", [kwu] + h2k, [psk], mm)
                    t_ = tmp[nt % 3]
                    kt_ = ("tmp", nt % 3)
                    nt += 1
                    T.op("act", [psk], [kt_], lambda e: e.activation(t_[:, :], pst[:, :], AF.Relu))
                    T.op("pool", [kt_], [("actT", fc)], lambda e: e.tensor_tensor(actT[:, fc, :], t_[:, :], t_[:, :], ALU.mult))
            for cb in range(4):
                banks = [0, 1, 2, 3] if cb % 2 == 0 else [4, 5, 6, 7]
                cols = slice(cb * 512, (cb + 1) * 512)
                for fp_ in range(8):
                    wd, kwd = self.wnext(("wdn", tb, cb, fp_))

                    def mm(e):
                        last = None
                        for tt in range(4):
                            for j in range(8):
                                last = e.matmul(self.ps[banks[tt]][:, :], actT[:, fp_ * 8 + j, tt * 128:(tt + 1) * 128],
                                                wd[:, j, :], start=(fp_ == 0 and j == 0), stop=(fp_ == 7 and j == 7))
                        return last
                    T.op("pe", [kwd] + [("actT", fp_ * 8 + j) for j in range(8)], [("ps", b) for b in banks], mm)
                for tt in range(4):
                    t_ = tmp[nt % 3]
                    kt_ = ("tmp", nt % 3)
                    nt += 1
                    T.op("dve", [("ps", banks[tt]), "gaf"], [kt_], lambda e: e.tensor_tensor(
                        t_[:, :], self.ps[banks[tt]][:, :], gaf[:, cols], ALU.mult))
                    T.op("pool", [kt_, ("x1", tt)], [("x1", tt)], lambda e: e.tensor_tensor(
                        x1[:, tt, cols], x1[:, tt, cols], t_[:, :], ALU.add))
            for tt in range(4):
                i = self.nm_i
                self.nm_i += 1
                ss = self.nm_st[:, (i % 8) * 2:(i % 8) * 2 + 1]
                rs = self.nm_st[:, (i % 8) * 2 + 1:(i % 8) * 2 + 2]
                kss, krs = ("nm_ss", i % 8), ("nm_rs", i % 8)
                T.op("act", [("x1", tt)], ["nm_junk", kss], lambda e: e.activation(
                    self.nm_junk[:, :], x1[:, tt, :], AF.Square, accum_out=ss))
                T.op("act", [kss, "epsb"], [kss], lambda e: e.activation(ss, ss, AF.Sqrt, scale=1.0 / D, bias=self.epsb[:, 0:1]))
                T.op("dve", [kss], [krs], lambda e: e.reciprocal(rs, ss))
                T.op("dve", [("x1", tt), krs, "gfb"], ["nm_xn"], lambda e: e.scalar_tensor_tensor(
                    obuf[:, :], x1[:, tt, :], rs, gfb[:, :], ALU.mult, ALU.mult))
                r0 = tb * TB + tt * 128
                T.dma("sp", "ost", ["nm_xn"], [("out", tb, tt)], lambda e: e.dma_start(
                    out=self.out_d[r0:r0 + 128, :], in_=obuf[:, :]))

    def finish(self):
        T = self.T
        if self.debug:
            T.barrier()
            self.ar_ptr = self.ar_end - 16384 - 64
            dbg = self.ar("dbgsb", [128, 4096], F32)
            T.op("dve", [], ["dbg"], lambda e: e.memset(dbg[:, :], 0.0))
            T.op("dve", ["cosT", "dbg"], ["dbg"], lambda e: e.tensor_copy(dbg[:, 0:1024], self.cosT[:, :, :].rearrange("p a b -> p (a b)")))
            T.op("dve", ["sinT", "dbg"], ["dbg"], lambda e: e.tensor_copy(dbg[:, 1024:2048], self.sinT[:, :, :].rearrange("p a b -> p (a b)")))
            T.op("dve", ["modL", "dbg"], ["dbg"], lambda e: e.tensor_copy(dbg[:, 2048:2144], self.modL[:, :]))
            T.op("dve", ["modC", "dbg"], ["dbg"], lambda e: e.tensor_copy(dbg[:, 2144:2240], self.modC[:, :]))
            T.op("dve", ["cl", "dbg"], ["dbg"], lambda e: e.tensor_copy(dbg[:, 2240:2304], self.cl[:, :, :, :].rearrange("p a b c -> p (a b c)")))
            if self.stage == 3:
                T.op("dve", ["dbg"], ["dbg"], lambda e: e.tensor_copy(dbg[:, 2304:2304 + 1152], self.KT[:, 0, 0:1152]))
                T.op("dve", ["dbg"], ["dbg"], lambda e: e.tensor_copy(dbg[:, 3456:3456 + 520], self.Vaug[:, 5, :, :].rearrange("p a b -> p (a b)")))
            T.dma("sp", "dbg", ["dbg"], ["dbg_d"], lambda e: e.dma_start(out=self.dbg_d, in_=dbg[:, :]))
        T.final_wait("sp")


def prep_inputs(inp, b):
    f = lambda a: np.ascontiguousarray(a, dtype=np.float32)
    fm = lambda v: f(np.asarray(v).reshape(-1, 16, 128).transpose(2, 0, 1).reshape(128, -1))
    pv = np.concatenate([
        fm(inp["c"][b][None]), fm(inp["c_ctx"][None]), fm(inp["g_mix"][0][None]), fm(inp["g_mlp"][0][None]),
        fm(inp["conv_w"][0]), fm(inp["conv_b"][0][None]), fm(inp["b_rg"][0]), fm(inp["b_ig"][0]),
        fm(inp["lru_lambda"][0]), fm(inp["b_mod"][0].reshape(6, D))], axis=1)
    assert pv.shape == (128, NPV), pv.shape
    return {
        "x": f(inp["x"][b]), "ctx": f(inp["ctx"][b]), "pvec": f(pv),
        "w_mod": f(inp["w_mod"][0]), "w_in": f(inp["w_in"][0]),
        "q_gain": f(inp["q_gain"][0][None]), "k_gain": f(inp["k_gain"][0][None]),
        "w_rg": f(inp["w_rg"][0].reshape(2 * 16 * 128, 128)), "w_ig": f(inp["w_ig"][0].reshape(2 * 16 * 128, 128)),
        "w_o_attn": f(inp["w_o_attn"][0]), "w_o_rnn": f(inp["w_o_rnn"][0]), "w_out": f(inp["w_out"][0]),
        "w_up": f(inp["w_up"][0]), "w_down": f(inp["w_down"][0]), "g_final": f(inp["g_final"][None]),
    }


def run(inputs, stage=99, debug=False, trace=False, ncores=8):
    bld = Builder(stage=stage, debug=debug)
    in_maps = [prep_inputs(inputs, b) for b in range(ncores)]
    res = run_bass_kernel_spmd(bld.nc, in_maps, core_ids=list(range(ncores)), trace=trace)
    return res


def kernel(**inputs):
    inputs = {k: np.asarray(v) for k, v in inputs.items()}
    res = run(inputs)
    return np.stack([res.results[b]["out"] for b in range(8)], axis=0).astype(np.float32)
```
